# Optimizing a Trainium2 kernel written in Bass

```python
import math
import jax, jax.numpy as jnp
from jax import lax
import numpy as np

D_MODEL = 2048
BATCH = 4
SEQ = 4096
DEPTH = 2

HEAD_DIM = 128
FOX_HEADS = D_MODEL // (2 * HEAD_DIM)
NSA_HEADS = D_MODEL // (2 * HEAD_DIM)
NSA_GROUP_SIZE = 4
NSA_KV_GROUPS = NSA_HEADS // NSA_GROUP_SIZE
FOX_W = FOX_HEADS * HEAD_DIM
NSA_W = NSA_HEADS * HEAD_DIM
NSA_KV_W = NSA_KV_GROUPS * HEAD_DIM
ROPE_DIM = HEAD_DIM // 4
ROPE_THETA = 500000.0
CMP_LEN = 32
CMP_STRIDE = 16
CMP_HIDDEN = 256
SLC_LEN = 64
SLC_TOP = 16
WINDOW = 512
FOX_Q_BLOCK = 128
NSA_Q_BLOCK = 64
FFN_HIDDEN = ((8 * D_MODEL + 3 * 256 - 1) // (3 * 256)) * 256
EPS = 1e-6
NEG_INF = -1e30
FORCE_SCORE = 1e6
IN_SIZES = [FOX_W, FOX_W, FOX_W, FOX_HEADS, NSA_W,
            NSA_KV_W, NSA_KV_W, NSA_KV_W, NSA_KV_W, NSA_KV_W, NSA_KV_W,
            NSA_HEADS * 3, D_MODEL, D_MODEL]
IN_W = sum(IN_SIZES)

kernel_name = "fox_nsa_parallel_hybrid"


def rms_norm(x, g):
    xf = x.astype(jnp.float32)
    y = xf * lax.rsqrt(jnp.mean(xf * xf, axis=-1, keepdims=True) + EPS) * g.astype(jnp.float32)
    return y.astype(x.dtype)


def partial_rope(x, pos):
    half = ROPE_DIM // 2
    inv_freq = jnp.power(ROPE_THETA, -jnp.arange(half, dtype=jnp.float32) * (2.0 / ROPE_DIM))
    ang = pos[:, None] * inv_freq[None, :]
    cos = jnp.cos(ang)[:, None, :]
    sin = jnp.sin(ang)[:, None, :]
    xf = x.astype(jnp.float32)
    x1 = xf[..., :half]
    x2 = xf[..., half:ROPE_DIM]
    out = jnp.concatenate([x1 * cos - x2 * sin, x2 * cos + x1 * sin, xf[..., ROPE_DIM:]], axis=-1)
    return out.astype(x.dtype)


def fox_attention(q, k, v, fg_logit, f_bias):
    B, T, H, Dh = q.shape
    QB = FOX_Q_BLOCK
    nblk = T // QB
    log_f = jax.nn.log_sigmoid(fg_logit.astype(jnp.float32) + f_bias.astype(jnp.float32))
    cum = jnp.cumsum(log_f, axis=1)
    cum_ht = cum.transpose(0, 2, 1)
    qb = q.reshape(B, nblk, QB, H, Dh).transpose(1, 0, 2, 3, 4)
    cb = cum.reshape(B, nblk, QB, H).transpose(1, 0, 3, 2)
    starts = jnp.arange(nblk) * QB
    kpos = jnp.arange(T)
    scale = HEAD_DIM ** -0.5

    def block(args):
        q_blk, c_blk, start = args
        s = jnp.einsum('bqhd,bkhd->bhqk', q_blk, k, preferred_element_type=jnp.float32) * scale
        s = s + (c_blk[..., :, None] - cum_ht[:, :, None, :])
        qpos = start + jnp.arange(QB)
        s = jnp.where(kpos[None, :] <= qpos[:, None], s, -jnp.inf)
        p = jax.nn.softmax(s, axis=-1)
        return jnp.einsum('bhqk,bkhd->bqhd', p.astype(v.dtype), v)

    o = lax.map(block, (qb, cb, starts))
    return o.transpose(1, 0, 2, 3, 4).reshape(B, T, H * Dh)


def compress_blocks(kv, pe, w1, w2):
    B, T, G, Dh = kv.shape
    nc = (T - CMP_LEN) // CMP_STRIDE + 1
    idx = np.arange(nc)[:, None] * CMP_STRIDE + np.arange(CMP_LEN)[None, :]
    blocks = kv[:, idx] + pe[None, None, :, None, :]
    blocks = blocks.transpose(0, 1, 3, 2, 4).reshape(B, nc, G, CMP_LEN * Dh)
    h = jax.nn.gelu(blocks @ w1)
    return h @ w2


def block_overlap(nc, nb):
    sc = np.arange(nc) * CMP_STRIDE
    ss = np.arange(nb) * SLC_LEN
    ov = np.minimum(sc[:, None] + CMP_LEN, ss[None, :] + SLC_LEN) - np.maximum(sc[:, None], ss[None, :])
    return (np.clip(ov, 0, None) / CMP_LEN).astype(np.float32)


def nsa_attention(q, kc, vc, ks, vs, kw, vw, gate_logit):
    B, T, H, Dh = q.shape
    G, HG = NSA_KV_GROUPS, NSA_GROUP_SIZE
    nc = kc.shape[1]
    nb = T // SLC_LEN
    n_sel = min(SLC_TOP, nb)
    QB = NSA_Q_BLOCK
    nq = T // QB
    scale = HEAD_DIM ** -0.5
    cmp_end = jnp.arange(nc) * CMP_STRIDE + CMP_LEN - 1
    overlap = jnp.asarray(block_overlap(nc, nb))
    ks_blk = ks.reshape(B, nb, SLC_LEN, G, Dh).transpose(0, 3, 1, 2, 4)
    vs_blk = vs.reshape(B, nb, SLC_LEN, G, Dh).transpose(0, 3, 1, 2, 4)
    kw_pad = jnp.pad(kw, ((0, 0), (WINDOW, 0), (0, 0), (0, 0)))
    vw_pad = jnp.pad(vw, ((0, 0), (WINDOW, 0), (0, 0), (0, 0)))
    gather = jax.vmap(jax.vmap(lambda blk, ix: blk[ix]))
    qc = q.reshape(B, nq, QB, G, HG, Dh).transpose(1, 0, 2, 3, 4, 5)
    gc = jax.nn.sigmoid(gate_logit.astype(jnp.float32)).reshape(B, nq, QB, G, HG, 3).transpose(1, 0, 2, 3, 4, 5)
    starts = jnp.arange(nq) * QB
    blk_ids = jnp.arange(nb)

    def block(args):
        q_blk, g_blk, start = args
        qpos = start + jnp.arange(QB)
        s_c = jnp.einsum('bqghd,bngd->bghqn', q_blk, kc, preferred_element_type=jnp.float32) * scale
        valid_c = cmp_end[None, :] <= qpos[:, None]
        s_c = jnp.where(valid_c, s_c, NEG_INF)
        p_c = jax.nn.softmax(s_c, axis=-1) * jnp.any(valid_c, axis=-1)[:, None].astype(jnp.float32)
        o_c = jnp.einsum('bghqn,bngd->bqghd', p_c.astype(vc.dtype), vc)
        imp = jnp.einsum('bghqn,nj->bgqj', p_c, overlap)
        cur = qpos // SLC_LEN
        forced = (blk_ids[None, :] == 0) | (blk_ids[None, :] == cur[:, None]) | (blk_ids[None, :] == cur[:, None] - 1)
        future = blk_ids[None, :] > cur[:, None]
        imp = jnp.where(forced, FORCE_SCORE, jnp.where(future, -1.0, imp))
        top_val, top_idx = lax.top_k(imp, n_sel)
        k_sel = gather(ks_blk, top_idx)
        v_sel = gather(vs_blk, top_idx)
        s_s = jnp.einsum('bqghd,bgqnld->bghqnl', q_blk, k_sel, preferred_element_type=jnp.float32) * scale
        kpos_s = top_idx[..., None] * SLC_LEN + jnp.arange(SLC_LEN)
        valid_s = (kpos_s <= qpos[None, None, :, None, None]) & (top_val >= 0.0)[..., None]
        s_s = jnp.where(valid_s[:, :, None], s_s, NEG_INF)
        p_s = jax.nn.softmax(s_s.reshape(B, G, HG, QB, n_sel * SLC_LEN), axis=-1)
        p_s = p_s.reshape(B, G, HG, QB, n_sel, SLC_LEN)
        o_s = jnp.einsum('bghqnl,bgqnld->bqghd', p_s.astype(v_sel.dtype), v_sel)
        k_win = lax.dynamic_slice_in_dim(kw_pad, start, WINDOW + QB, axis=1)
        v_win = lax.dynamic_slice_in_dim(vw_pad, start, WINDOW + QB, axis=1)
        kpos_w = start - WINDOW + jnp.arange(WINDOW + QB)
        dist = qpos[:, None] - kpos_w[None, :]
        valid_w = (dist >= 0) & (dist < WINDOW) & (kpos_w[None, :] >= 0)
        s_w = jnp.einsum('bqghd,bkgd->bghqk', q_blk, k_win, preferred_element_type=jnp.float32) * scale
        s_w = jnp.where(valid_w, s_w, NEG_INF)
        p_w = jax.nn.softmax(s_w, axis=-1)
        o_w = jnp.einsum('bghqk,bkgd->bqghd', p_w.astype(v_win.dtype), v_win)
        o = g_blk[..., 0:1] * o_c + g_blk[..., 1:2] * o_s + g_blk[..., 2:3] * o_w
        return o.astype(q.dtype)

    o = lax.map(block, (qc, gc, starts))
    return o.transpose(1, 0, 2, 3, 4, 5).reshape(B, T, H * Dh)


def hybrid_layer(x, n_mix_pre, n_mix_post, n_ffn_pre, n_ffn_post, w_in, f_bias,
                 ck_pe, ck_w1, ck_w2, cv_pe, cv_w1, cv_w2,
                 w_up_fox, w_up_nsa, w_out, w_gate, w_up, w_down):
    B, T, _ = x.shape
    a = rms_norm(x, n_mix_pre)
    proj = a @ w_in
    split_pts = np.cumsum(IN_SIZES)[:-1].tolist()
    (fq, fk, fv, ff, nq_, nkc, nvc, nks, nvs, nkw, nvw, ngate, g_fox, g_nsa) = jnp.split(proj, split_pts, axis=-1)
    pos = jnp.arange(T, dtype=jnp.float32)
    fq = fq.reshape(B, T, FOX_HEADS, HEAD_DIM)
    fk = fk.reshape(B, T, FOX_HEADS, HEAD_DIM)
    fv = fv.reshape(B, T, FOX_HEADS, HEAD_DIM)
    o_fox = fox_attention(fq, fk, fv, ff, f_bias)
    kv_shape = (B, T, NSA_KV_GROUPS, HEAD_DIM)
    nq_ = partial_rope(nq_.reshape(B, T, NSA_HEADS, HEAD_DIM), pos)
    nks = partial_rope(nks.reshape(kv_shape), pos)
    nkw = partial_rope(nkw.reshape(kv_shape), pos)
    kc = compress_blocks(nkc.reshape(kv_shape), ck_pe, ck_w1, ck_w2)
    vc = compress_blocks(nvc.reshape(kv_shape), cv_pe, cv_w1, cv_w2)
    nc = kc.shape[1]
    kc = partial_rope(kc, (jnp.arange(nc) * CMP_STRIDE + CMP_LEN - 1).astype(jnp.float32))
    o_nsa = nsa_attention(nq_, kc, vc, nks, nvs.reshape(kv_shape), nkw, nvw.reshape(kv_shape), ngate)
    mix = jax.nn.sigmoid(g_fox) * (o_fox @ w_up_fox) + jax.nn.sigmoid(g_nsa) * (o_nsa @ w_up_nsa)
    x = x + rms_norm(mix @ w_out, n_mix_post)
    a = rms_norm(x, n_ffn_pre)
    h = (jax.nn.silu(a @ w_gate) * (a @ w_up)) @ w_down
    return x + rms_norm(h, n_ffn_post)


def setup_inputs(seed: int = 0) -> dict:
    key = jax.random.key(seed)
    ks = jax.random.split(key, 24)
    f32 = jnp.float32

    def w(k, shape, fan_in):
        return jax.random.normal(k, shape, f32) * (fan_in ** -0.5)

    def gain(k):
        return 1.0 + 0.02 * jax.random.normal(k, (DEPTH, D_MODEL), f32)

    L = DEPTH
    return {
        "x": jax.random.normal(ks[0], (BATCH, SEQ, D_MODEL), f32),
        "norm_mix_pre": gain(ks[1]),
        "norm_mix_post": gain(ks[2]),
        "norm_ffn_pre": gain(ks[3]),
        "norm_ffn_post": gain(ks[4]),
        "w_in": w(ks[5], (L, D_MODEL, IN_W), D_MODEL),
        "fox_forget_bias": jax.random.uniform(ks[6], (L, FOX_HEADS), f32, 1.0, 4.0),
        "cmp_k_pe": 0.1 * jax.random.normal(ks[7], (L, CMP_LEN, HEAD_DIM), f32),
        "cmp_k_w1": w(ks[8], (L, CMP_LEN * HEAD_DIM, CMP_HIDDEN), CMP_LEN * HEAD_DIM),
        "cmp_k_w2": w(ks[9], (L, CMP_HIDDEN, HEAD_DIM), CMP_HIDDEN),
        "cmp_v_pe": 0.1 * jax.random.normal(ks[10], (L, CMP_LEN, HEAD_DIM), f32),
        "cmp_v_w1": w(ks[11], (L, CMP_LEN * HEAD_DIM, CMP_HIDDEN), CMP_LEN * HEAD_DIM),
        "cmp_v_w2": w(ks[12], (L, CMP_HIDDEN, HEAD_DIM), CMP_HIDDEN),
        "w_up_fox": w(ks[13], (L, FOX_W, D_MODEL), FOX_W),
        "w_up_nsa": w(ks[14], (L, NSA_W, D_MODEL), NSA_W),
        "w_out": w(ks[15], (L, D_MODEL, D_MODEL), D_MODEL),
        "w_ffn_gate": w(ks[16], (L, D_MODEL, FFN_HIDDEN), D_MODEL),
        "w_ffn_up": w(ks[17], (L, D_MODEL, FFN_HIDDEN), D_MODEL),
        "w_ffn_down": w(ks[18], (L, FFN_HIDDEN, D_MODEL), FFN_HIDDEN),
    }


def reference(x, norm_mix_pre, norm_mix_post, norm_ffn_pre, norm_ffn_post, w_in, fox_forget_bias,
              cmp_k_pe, cmp_k_w1, cmp_k_w2, cmp_v_pe, cmp_v_w1, cmp_v_w2,
              w_up_fox, w_up_nsa, w_out, w_ffn_gate, w_ffn_up, w_ffn_down):
    for l in range(DEPTH):
        x = hybrid_layer(x, norm_mix_pre[l], norm_mix_post[l], norm_ffn_pre[l], norm_ffn_post[l],
                         w_in[l], fox_forget_bias[l],
                         cmp_k_pe[l], cmp_k_w1[l], cmp_k_w2[l], cmp_v_pe[l], cmp_v_w1[l], cmp_v_w2[l],
                         w_up_fox[l], w_up_nsa[l], w_out[l], w_ffn_gate[l], w_ffn_up[l], w_ffn_down[l])
    return x
```

```python
import numpy as np
from contextlib import ExitStack
import concourse.bass as bass
import concourse.mybir as mybir
from concourse.bass_utils import run_bass_kernel_spmd

F32 = mybir.dt.float32
BF16 = mybir.dt.bfloat16
AF = mybir.ActivationFunctionType
ALU = mybir.AluOpType

T = 4096
D = 2048
NT = T // 128
L = 2
FF = 5632
INW = 9760
SCALE = 128 ** -0.5
NEG = -30000.0
EPS = 1e-6
O_FQ, O_FK, O_FV, O_FF, O_NQ, O_NKC, O_NVC, O_NKS, O_NVS, O_NKW, O_NVW, O_NG, O_GF, O_GN = (
    0, 1024, 2048, 3072, 3080, 4104, 4360, 4616, 4872, 5128, 5384, 5640, 5664, 7712)


class Slot:
    __slots__ = ("sem", "cnt")

    def __init__(self, sem):
        self.sem = sem
        self.cnt = 0


class Buf:
    __slots__ = ("w", "r", "slot", "name", "persist", "excl")

    def __init__(self, name="", persist=False, excl=False):
        self.excl = excl
        self.w = None
        self.r = []
        self.slot = None
        self.name = name
        self.persist = persist


class Sched:
    def __init__(self, nc, es, nslots=72):
        self.nc = nc
        self.es = es
        self.eng = {"pe": nc.tensor, "act": nc.scalar, "dve": nc.vector, "pool": nc.gpsimd, "sp": nc.sync}
        self.sem = {k: es.enter_context(nc.semaphore("s_" + k)) for k in self.eng}
        self.cnt = {k: 0 for k in self.eng}
        self.waited = {k: {} for k in self.eng}
        self.free = [Slot(es.enter_context(nc.semaphore("d%d" % i))) for i in range(nslots)]
        self.inuse = []
        self.fence = []
        self.fence_pending = {k: False for k in self.eng}
        self.ninst = 0
        self.nwait = 0

    def _wait(self, E, tok):
        if tok is None:
            return
        sem, v = tok
        if E == "pe" and sem is self.sem["pe"]:
            return
        key = id(sem)
        if self.waited[E].get(key, 0) >= v:
            return
        self.waited[E][key] = v
        self.eng[E].wait_ge(sem, v)
        self.nwait += 1

    def _deps(self, E, reads, writes):
        if self.fence_pending[E]:
            self.fence_pending[E] = False
            for t in self.fence:
                self._wait(E, t)
        for b in reads:
            self._wait(E, b.w)
        for b in writes:
            self._wait(E, b.w)
            for t in b.r:
                self._wait(E, t)

    def _commit(self, tok, reads, writes):
        for b in reads:
            b.r.append(tok)
            if len(b.r) > 32:
                d = {}
                for s, v in b.r:
                    if id(s) not in d or d[id(s)][1] < v:
                        d[id(s)] = (s, v)
                b.r = list(d.values())
        for b in writes:
            b.w = tok
            b.r = []

    def op(self, E, reads, writes, fn, inc=True):
        ex = [b for b in reads if b.excl]
        if ex:
            reads = [b for b in reads if not b.excl]
            writes = list(writes) + ex
        self._deps(E, reads, writes)
        ins = fn(self.eng[E])
        if inc:
            self.cnt[E] += 1
            ins.then_inc(self.sem[E], 1)
            tok = (self.sem[E], self.cnt[E])
        else:
            tok = (self.sem[E], self.cnt[E] + 1)
        self._commit(tok, reads, writes)
        self.ninst += 1
        return tok

    def dma(self, Q, out, in_, reads, writes, slotbuf, **kw):
        if slotbuf.slot is None:
            slotbuf.slot = self.free.pop()
            if not slotbuf.persist:
                self.inuse.append(slotbuf.slot)
        self._deps(Q, reads, writes)
        ins = self.eng[Q].dma_start(out=out, in_=in_, **kw)
        sl = slotbuf.slot
        sl.cnt += 16
        ins.then_inc(sl.sem, 16)
        tok = (sl.sem, sl.cnt)
        self._commit(tok, reads, writes)
        self.ninst += 1
        return tok

    def barrier(self):
        f = [(self.sem[k], self.cnt[k]) for k in self.eng if self.cnt[k] > 0]
        f += [(s.sem, s.cnt) for s in self.inuse if s.cnt > 0]
        self.fence = f
        self.free.extend(self.inuse)
        self.inuse = []
        for k in self.eng:
            self.fence_pending[k] = True

    def finish(self, extra=()):
        self.barrier()
        self._deps("sp", [], [])
        for b in extra:
            self._wait("sp", b.w)


class Arena:
    def __init__(self, ap, nelem):
        self.ap = ap
        self.n = nelem
        self.base = 0
        self.off = 0

    def reset(self):
        self.off = self.base

    def freeze(self):
        self.base = self.off

    def alloc(self, n, dt=BF16, parts=128, name=""):
        k = 2 if dt == F32 else 1
        o = (self.off + 1) // 2 * 2
        assert o + n * k <= self.n, "arena overflow %s %d" % (name, o + n * k)
        self.off = o + n * k
        v = self.ap[0:parts, o:o + n * k]
        if dt == F32:
            v = v.bitcast(F32)
        return v, Buf(name)


def host_consts():
    import ml_dtypes
    bf = ml_dtypes.bfloat16
    c = {}
    c["c_identb"] = np.eye(128, dtype=np.float32).astype(bf)
    c["c_identf"] = np.eye(128, dtype=np.float32)
    c["c_onesb"] = np.ones((128, 128), np.float32).astype(bf)
    k = np.arange(128)[:, None]
    q = np.arange(128)[None, :]
    c["c_caus"] = np.where(k <= q, 0.0, NEG).astype(np.float32).astype(bf)
    c["c_anti"] = np.where(k > q, 0.0, NEG).astype(np.float32).astype(bf)
    R = np.zeros((32, 32), np.float32)
    for m in range(16):
        R[m + 16, m] = -1.0
        R[m, m + 16] = 1.0
    c["c_rrot"] = R.astype(bf)
    j = np.arange(64)[:, None]
    kk = np.arange(T)[None, :]
    c["c_E"] = (kk // 64 == j).astype(np.float32).astype(bf)
    half = 16
    inv_freq = np.power(np.float32(500000.0), -np.arange(half, dtype=np.float32) * np.float32(2.0 / 32)).astype(np.float32)

    def cs(pos):
        ang = (pos.astype(np.float32)[:, None] * inv_freq[None, :]).astype(np.float32)
        co = np.cos(ang).astype(np.float32).T
        si = np.sin(ang).astype(np.float32).T
        return np.concatenate([co, co], 0), np.concatenate([si, si], 0)
    c["c_cos"], c["c_sin"] = cs(np.arange(T))
    pc = np.arange(256) * 16 + 31
    c["c_cosc"], c["c_sinc"] = cs(pc)
    r = np.arange(128)[:, None]
    m = np.floor((r - 31) / 16.0)
    cc = np.arange(512)[None, :]
    c["c_maskc"] = ((cc - 256) <= m).astype(np.float32)
    J = np.arange(128)[None, :]
    jc = 64 + (r >= 64).astype(np.int64)
    fut = J > jc
    forced = (J == jc) | (J == jc - 1)
    c["c_keep"] = (~(fut | forced)).astype(np.float32)
    c["c_add"] = np.where(forced, 1e6, np.where(fut, -1.0, 0.0)).astype(np.float32)
    s8 = np.zeros((8, 8, 128), np.float32)
    for i in range(8):
        s8[i, i, :] = 1.0
    c["c_sel8"] = s8.reshape(8, 8 * 128)
    s24 = np.zeros((24, 24, 128), np.float32)
    for i in range(24):
        s24[i, i, :] = 1.0
    c["c_sel24"] = s24.reshape(24, 24 * 128)
    return c


WNAMES = [("w_in", D, INW), ("w_up_fox", 1024, D), ("w_up_nsa", 1024, D), ("w_out", D, D),
          ("w_ffn_gate", D, FF), ("w_ffn_up", D, FF), ("w_ffn_down", FF, D),
          ("cmp_k_w1", 4096, 256), ("cmp_k_w2", 256, 128), ("cmp_v_w1", 4096, 256), ("cmp_v_w2", 256, 128)]
SMALL = [("norm_mix_pre", [L, D]), ("norm_mix_post", [L, D]), ("norm_ffn_pre", [L, D]), ("norm_ffn_post", [L, D]),
         ("fox_forget_bias", [L, 8]), ("cmp_k_pe", [L, 32, 128]), ("cmp_v_pe", [L, 32, 128])]


class KB:
    def __init__(self, nlayers=L, stop=None, taps=(), LW=L):
        self.LW = LW
        self.nlayers = nlayers
        self.stop = stop
        self.taps = taps
        nc = self.nc = bass.Bass("TRN2", target_bir_lowering=False)
        self.es = ExitStack()
        dt = nc.dram_tensor
        self.x = dt("x", [T, D], F32, kind="ExternalInput").ap()
        self.win = {}
        for n, k, m in WNAMES:
            self.win[n] = dt(n, [LW, k, m], F32, kind="ExternalInput").ap()
        self.small = {}
        for n, shp in SMALL:
            self.small[n] = dt(n, [LW] + list(shp[1:]), F32, kind="ExternalInput").ap()
        self.cin = {}
        for n, a in host_consts().items():
            self.cin[n] = dt(n, list(a.shape), F32 if a.dtype == np.float32 else BF16, kind="ExternalInput").ap()
        self.out = dt("out", [T, D], F32, kind="ExternalOutput").ap()
        self.scr = {}

    def scratch(self, name, shape, dtype):
        kind = "ExternalOutput" if name in self.taps else "Internal"
        t = self.nc.dram_tensor(name, shape, dtype, kind=kind).ap()
        self.scr[name] = t
        return t

    def build(self):
        nc = self.nc
        with self.es as es:
            S = self.S = Sched(nc, es)
            arena_t = es.enter_context(nc.sbuf_tensor("arena", [128, 90112], BF16))
            A = self.A = Arena(arena_t, 90112)
            self.ps = [es.enter_context(nc.psum_tensor("ps%d" % i, [128, 512], F32)) for i in range(8)]
            self.bps = [Buf("ps%d" % i, excl=True) for i in range(8)]
            sc = self.scratch
            self.wb = {}
            for n, k, m in WNAMES:
                self.wb[n] = sc("wb_" + n, [self.LW, k, m], BF16)
            self.xT = [sc("xT%d" % i, [D, T], F32) for i in range(3)]
            self.yT = sc("yT", [D, T], F32)
            self.fqT = sc("fqT", [1024, T], BF16)
            self.fkT = sc("fkT", [1024, T], BF16)
            self.fv = sc("fv", [T, 1024], BF16)
            self.ffT = sc("ffT", [8, T], F32)
            self.nqT = sc("nqT", [1024, T], BF16)
            self.nkcT = sc("nkcT", [512, T], BF16)
            self.nksT = sc("nksT", [256, T], BF16)
            self.nvs = sc("nvs", [T, 256], BF16)
            self.nkwT = sc("nkwT", [256, T], BF16)
            self.nvw = sc("nvw", [T, 256], BF16)
            self.ngT = sc("ngT", [24, T], F32)
            self.gfT = sc("gfT", [D, T], BF16)
            self.gnT = sc("gnT", [D, T], BF16)
            self.kcT = sc("kcT", [2, 128, 256], BF16)
            self.vc = sc("vc", [2, 256, 128], BF16)
            self.ofoxT = sc("ofoxT", [1024, T], BF16)
            self.onsaT = sc("onsaT", [1024, T], BF16)
            self.hT = sc("hT", [FF, T], BF16)
            self.consts()
            if self.stop != (0, "T0a"):
                self.precast()
            if self.stop not in ((0, "T0a"), (0, "T0b")):
                self.transpose_in()
            S.barrier()
            cur = 0
            done = False
            for l in range(self.nlayers if self.stop not in ((0, "T0a"), (0, "T0b"), (0, "T0c")) else 0):
                for phase in ("A", "FOXPREP", "CMP", "FOX", "NSA", "C", "D"):
                    A.reset()
                    if self.stop is not None and self.stop[1] in ("A1", "A2", "A3", "A4"):
                        self.phase_A(l, self.xT[cur])
                        S.barrier()
                        done = True
                        break
                    if phase == "A":
                        self.phase_A(l, self.xT[cur])
                    elif phase == "FOXPREP":
                        self.phase_foxprep(l)
                    elif phase == "CMP":
                        self.phase_cmp(l)
                    elif phase == "FOX":
                        self.phase_fox(l)
                    elif phase == "NSA":
                        self.phase_nsa(l)
                    elif phase == "C":
                        nx = (cur + 1) % 3
                        self.phase_C(l, self.xT[cur], self.xT[nx])
                        cur = nx
                    elif phase == "D":
                        nx = (cur + 1) % 3
                        self.phase_D(l, self.xT[cur], self.xT[nx])
                        cur = nx
                    S.barrier()
                    if self.stop == (l, phase):
                        done = True
                        break
                if done:
                    break
            A.reset()
            if self.stop not in ((0, "T0a"), (0, "T0b")):
                self.transpose_out(self.xT[cur])
            S.finish(list(self.wtok.values()) if hasattr(self, 'wtok') else ())
        return nc

    def consts(self):
        S, A = self.S, self.A
        self.C = {}
        for n, ap in self.cin.items():
            shp = list(ap.shape)
            dtp = ap.dtype
            if n in ("c_cos", "c_sin", "c_maskc", "c_keep", "c_add", "c_cosc", "c_sinc", "c_E", "c_sel8", "c_sel24"):
                continue
            v, b = A.alloc(shp[1], dtp, parts=shp[0], name=n)
            S.dma("sp", v, ap[:, :], [], [b], b)
            self.C[n] = (v, b)
        g, gb = A.alloc(L * 4 * 16, F32, name="gains")
        for i, n in enumerate(["norm_mix_pre", "norm_mix_post", "norm_ffn_pre", "norm_ffn_post"]):
            for l in range(self.LW):
                o = (l * 4 + i) * 16
                S.dma("sp", g[:, o:o + 16], self.small[n][l, :].rearrange("(c p) -> p c", p=128), [], [gb], gb,
                      allow_slow_non_contiguous=True)
        self.gains = (g, gb)
        nb, nbb = A.alloc(self.LW, F32, parts=8, name="negb")
        S.dma("sp", nb, self.small["fox_forget_bias"].rearrange("l h -> h l"), [], [nbb], nbb, allow_slow_non_contiguous=True)
        S.op("dve", [nbb], [nbb], lambda e: e.tensor_scalar(out=nb, in0=nb, scalar1=-1.0, scalar2=None, op0=ALU.mult))
        self.negb = (nb, nbb)
        A.freeze()

    def precast(self):
        S = self.S
        self.wtok = {}
        for n, k, m in WNAMES:
            for l in range(self.nlayers):
                b = Buf("wc_" + n, persist=True)
                nparts = 4 if k * m > 4000000 else 1
                rs = k // nparts
                for i in range(nparts):
                    S.dma("pool", self.wb[n][l, i * rs:(i + 1) * rs, :], self.win[n][l, i * rs:(i + 1) * rs, :], [], [b], b)
                self.wtok[(n, l)] = b

    def gain(self, l, i, c):
        g = self.gains[0]
        o = (l * 4 + i) * 16 + c
        return g[:, o:o + 1]

    def transpose_in(self):
        S, A = self.S, self.A
        A.reset()
        identf, bid = self.C["c_identf"]
        xin = [A.alloc(4 * D, F32, name="xin%d" % i) for i in range(2)]
        stg = [A.alloc(16 * 512, F32, name="tstg%d" % i) for i in range(2)]
        xT0 = self.xT[0]
        for blk in range(T // 512):
            xi, xib = xin[blk % 2]
            st, stb = stg[blk % 2]
            xiv = xi.rearrange("p (a d) -> p a d", a=4)
            S.dma("sp", xiv, self.x[blk * 512:(blk + 1) * 512, :].rearrange("(a p) d -> p a d", p=128), [], [xib], xib)
            stv = st.rearrange("p (c t) -> p c t", c=16)
            for c in range(16):
                pb = c % 4
                for a in range(4):
                    S.op("pe", [xib, bid], [self.bps[pb]],
                         lambda e: e.transpose(out=self.ps[pb][:, a * 128:(a + 1) * 128], in_=xiv[:, a, c * 128:(c + 1) * 128], identity=identf),
                         inc=(a == 3))
                eng = "act" if c % 2 == 0 else "dve"
                if eng == "act":
                    S.op("act", [self.bps[pb]], [stb], lambda e: e.activation(out=stv[:, c, :], in_=self.ps[pb][:, :], func=AF.Copy))
                else:
                    S.op("dve", [self.bps[pb]], [stb], lambda e: e.tensor_copy(out=stv[:, c, :], in_=self.ps[pb][:, :]))
            S.dma("sp", xT0[:, blk * 512:(blk + 1) * 512].rearrange("(c p) t -> p c t", p=128), stv, [stb], [], stb)

    def transpose_out(self, xT):
        S, A = self.S, self.A
        identf, bid = self.C["c_identf"]
        xin = [A.alloc(16 * 512, F32, name="oin%d" % i) for i in range(2)]
        stg = [A.alloc(4 * D, F32, name="ostg%d" % i) for i in range(2)]
        for blk in range(T // 512):
            xi, xib = xin[blk % 2]
            st, stb = stg[blk % 2]
            xiv = xi.rearrange("p (c t) -> p c t", c=16)
            S.dma("sp", xiv, xT[:, blk * 512:(blk + 1) * 512].rearrange("(c p) t -> p c t", p=128), [], [xib], xib)
            stv = st.rearrange("p (a d) -> p a d", a=4)
            i = 0
            for a in range(4):
                for cg in range(4):
                    pb = i % 4
                    i += 1
                    for cc in range(4):
                        c = cg * 4 + cc
                        S.op("pe", [xib, bid], [self.bps[pb]],
                             lambda e: e.transpose(out=self.ps[pb][:, cc * 128:(cc + 1) * 128], in_=xiv[:, c, a * 128:(a + 1) * 128], identity=identf),
                             inc=(cc == 3))
                    if i % 2 == 0:
                        S.op("act", [self.bps[pb]], [stb], lambda e: e.activation(out=stv[:, a, cg * 512:(cg + 1) * 512], in_=self.ps[pb][:, :], func=AF.Copy))
                    else:
                        S.op("dve", [self.bps[pb]], [stb], lambda e: e.tensor_copy(out=stv[:, a, cg * 512:(cg + 1) * 512], in_=self.ps[pb][:, :]))
            S.dma("sp", self.out[blk * 512:(blk + 1) * 512, :].rearrange("(a p) d -> p a d", p=128), stv, [stb], [], stb)

    def rstd_bc(self, sq_chunks, sqb, nb, rs, rsb, pbank):
        S = self.S
        ones, ob = self.C["c_onesb"]
        ps, bp = self.ps[pbank], self.bps[pbank]
        for c in range(16):
            S.op("pe", [sqb, ob], [bp], lambda e: e.matmul(ps[:, 0:nb], lhsT=ones, rhs=sq_chunks(c), start=(c == 0), stop=(c == 15)), inc=(c == 15))
        S.op("act", [bp], [rsb], lambda e: e.activation(out=rs, in_=ps[:, 0:nb], func=AF.Ln, scale=1.0 / D, bias=self.epsc))
        S.op("act", [rsb], [rsb], lambda e: e.activation(out=rs, in_=rs, func=AF.Exp, scale=-0.5))

    def norm_pre(self, l, gi, xT, tok0, TB, res, resb):
        S, A = self.S, self.A
        NB = 128
        xb = [A.alloc(16 * NB, F32, name="npx%d" % i) for i in range(2)]
        sq, sqb = A.alloc(16 * NB, BF16, name="npsq")
        rs, rsb = A.alloc(NB, F32, name="nprs")
        sqv = sq.rearrange("p (c t) -> p c t", c=16)
        for bi in range(TB // NB):
            x_, xbb = xb[bi % 2]
            xv = x_.rearrange("p (c t) -> p c t", c=16)
            t0 = tok0 + bi * NB
            S.dma("sp", xv, xT[:, t0:t0 + NB].rearrange("(c p) t -> p c t", p=128), [], [xbb], xbb)
            S.op("act", [xbb], [sqb], lambda e: e.activation(out=sq, in_=x_, func=AF.Square))
            self.rstd_bc(lambda c: sqv[:, c, :], sqb, NB, rs, rsb, 7)
            for c in range(16):
                eng = "dve"
                S.op(eng, [xbb, rsb, self.gains[1]], [resb],
                     lambda e: e.scalar_tensor_tensor(out=res[:, c, bi * NB:(bi + 1) * NB], in0=xv[:, c, :], scalar=self.gain(l, gi, c), in1=rs,
                                                      op0=ALU.mult, op1=ALU.mult))

    def norm_post(self, l, gi, yT, xT, xTn, tok0, TB):
        S, A = self.S, self.A
        NB = 256
        xb = [A.alloc(16 * NB, F32, name="nqx%d" % i) for i in range(2)]
        yb = [A.alloc(16 * NB, F32, name="nqy%d" % i) for i in range(2)]
        sq, sqb = A.alloc(16 * NB, BF16, name="nqsq")
        rs, rsb = A.alloc(NB, F32, name="nqrs")
        sqv = sq.rearrange("p (c t) -> p c t", c=16)
        for bi in range(TB // NB):
            x_, xbb = xb[bi % 2]
            y_, ybb = yb[bi % 2]
            xv = x_.rearrange("p (c t) -> p c t", c=16)
            yv = y_.rearrange("p (c t) -> p c t", c=16)
            t0 = tok0 + bi * NB
            S.dma("sp", xv, xT[:, t0:t0 + NB].rearrange("(c p) t -> p c t", p=128), [], [xbb], xbb)
            S.dma("sp", yv, yT[:, t0:t0 + NB].rearrange("(c p) t -> p c t", p=128), [], [ybb], ybb)
            S.op("act", [ybb], [sqb], lambda e: e.activation(out=sq, in_=y_, func=AF.Square))
            self.rstd_bc(lambda c: sqv[:, c, :], sqb, NB, rs, rsb, 7)
            for c in range(16):
                S.op("dve", [ybb, rsb, self.gains[1]], [ybb],
                     lambda e: e.scalar_tensor_tensor(out=yv[:, c, :], in0=yv[:, c, :], scalar=self.gain(l, gi, c), in1=rs,
                                                      op0=ALU.mult, op1=ALU.mult))
            S.op("dve", [ybb, xbb], [ybb], lambda e: e.tensor_tensor(out=y_, in0=y_, in1=x_, op=ALU.add))
            S.dma("sp", xTn[:, t0:t0 + NB].rearrange("(c p) t -> p c t", p=128), yv, [ybb], [], ybb)

    def load_w(self, name, l, c0, ncols, KC, wt, wtb):
        S = self.S
        src = self.wb[name][l, :, c0:c0 + ncols].rearrange("(kc p) n -> p kc n", p=128)
        S.dma("sp", wt, src, [self.wtok[(name, l)]], [wtb], wtb)

    def gemm_F(self, segs, KC, act, actb, TB, wslots, epi, banks=(0, 1, 2, 3, 4, 5, 6), after=None):
        S = self.S
        nts = TB // 512
        bi = 0
        wi = 0
        for (wn, l, c0, ncols, tag) in segs:
            CBmax = max(128, (8192 // KC) // 128 * 128)
            CBmax = min(CBmax, 512)
            cb0 = 0
            while cb0 < ncols:
                CB = min(CBmax, ncols - cb0)
                wt_full, wtb = wslots[wi % len(wslots)]
                wi += 1
                wt = wt_full[:, 0:KC * CB].rearrange("p (k n) -> p k n", k=KC)
                self.load_w(wn, l, c0 + cb0, CB, KC, wt, wtb)
                for m0 in range(0, CB, 128):
                    M = min(128, CB - m0)
                    chunk = (cb0 + m0) // 128
                    bks = [banks[(bi + i) % len(banks)] for i in range(nts)]
                    bi += nts
                    for kc in range(KC):
                        for ts in range(nts):
                            pb = bks[ts]
                            S.op("pe", [wtb, actb], [self.bps[pb]],
                                 lambda e: e.matmul(self.ps[pb][0:M, :], lhsT=wt[:, kc, m0:m0 + M], rhs=act[:, kc, ts * 512:(ts + 1) * 512],
                                                    start=(kc == 0), stop=(kc == KC - 1)),
                                 inc=(kc == KC - 1))
                    for ts in range(nts):
                        epi(tag, chunk, M, ts, self.ps[bks[ts]], self.bps[bks[ts]])
                cb0 += CB

    def gemm_T(self, wn, l, c0, ncols, KC, act, actb, TB, wslots, epi, banks=(0, 1, 2, 3)):
        S = self.S
        bi = 0
        wi = 0
        cb0 = 0
        while cb0 < ncols:
            CB = min(512, ncols - cb0)
            wt_full, wtb = wslots[wi % len(wslots)]
            wi += 1
            wt = wt_full[:, 0:KC * CB].rearrange("p (k n) -> p k n", k=KC)
            self.load_w(wn, l, c0 + cb0, CB, KC, wt, wtb)
            for tt in range(TB // 128):
                pb = banks[bi % len(banks)]
                bi += 1
                for kc in range(KC):
                    S.op("pe", [wtb, actb], [self.bps[pb]],
                         lambda e: e.matmul(self.ps[pb][:, 0:CB], lhsT=act[:, kc, tt * 128:(tt + 1) * 128], rhs=wt[:, kc, :],
                                            start=(kc == 0), stop=(kc == KC - 1)),
                         inc=(kc == KC - 1))
                epi(tt, cb0, CB, self.ps[pb], self.bps[pb])
            cb0 += CB

    def phase_A(self, l, xT):
        S, A = self.S, self.A
        TB = 2048
        res_, resb = A.alloc(16 * TB, BF16, name="resA")
        res = res_.rearrange("p (c t) -> p c t", c=16)
        wslots = [A.alloc(8192, BF16, name="wsl%d" % i) for i in range(2)]
        stg = [A.alloc(TB, BF16, name="stgA%d" % i) for i in range(3)]
        stf, stfb = A.alloc(TB, F32, parts=32, name="stgAf")
        cosv, cosb = A.alloc(TB, F32, parts=32, name="cosA")
        sinv, sinb = A.alloc(TB, F32, parts=32, name="sinA")
        tA = [A.alloc(512, F32, parts=32, name="tA%d" % i) for i in range(2)]
        tB = [A.alloc(512, F32, parts=32, name="tB%d" % i) for i in range(2)]
        vst = [A.alloc(512, BF16, name="vst%d" % i) for i in range(3)]
        rrot, rrb = self.C["c_rrot"]
        mark = A.off
        for half in range(T // TB):
            tok0 = half * TB
            A.off = mark
            S.dma("sp", cosv, self.cin["c_cos"][:, tok0:tok0 + TB], [], [cosb], cosb)
            S.dma("sp", sinv, self.cin["c_sin"][:, tok0:tok0 + TB], [], [sinb], sinb)
            self.norm_pre(l, 0, xT, tok0, TB, res, resb)
            sub = self.stop[1] if self.stop is not None else ""
            if sub == "A1":
                if "resA" not in self.scr:
                    self.scratch("resA", [16 * 128, T], BF16)
                S.dma("sp", self.scr["resA"][:, tok0:tok0 + TB].rearrange("(c p) t -> p c t", p=128), res, [resb], [], resb)
                continue
            st_i = [0]
            deferred = []

            def flush():
                while deferred:
                    deferred.pop(0)()

            def epi(tag, chunk, M, ts, ps, bp):
                dest, rope, func, fp32 = tag
                if ts == 0:
                    st_i[0] += 1
                if fp32:
                    sv, sb_ = stf, stfb
                else:
                    sv, sb_ = stg[st_i[0] % 3]
                cols = slice(ts * 512, (ts + 1) * 512)
                if func is not None:
                    S.op("act", [bp], [sb_], lambda e: e.activation(out=sv[0:M, cols], in_=ps[0:M, :], func=func))
                elif rope:
                    S.op("act", [bp], [sb_], lambda e: e.activation(out=sv[0:M, cols], in_=ps[0:M, :], func=AF.Copy))
                    ta, tab = tA[ts % 2]
                    S.op("dve", [bp, cosb], [tab], lambda e: e.tensor_tensor(out=ta, in0=ps[0:32, :], in1=cosv[:, cols], op=ALU.mult))

                    def part2(sv=sv, sb_=sb_, cols=cols, ta=ta, tab=tab, ts=ts):
                        tb_, tbb = tB[ts % 2]
                        S.op("pe", [sb_, rrb], [self.bps[7]], lambda e: e.matmul(self.ps[7][0:32, :], lhsT=rrot, rhs=sv[0:32, cols], start=True, stop=True))
                        S.op("dve", [self.bps[7], sinb], [tbb], lambda e: e.tensor_tensor(out=tb_, in0=self.ps[7][0:32, :], in1=sinv[:, cols], op=ALU.mult))
                        S.op("dve", [tab, tbb], [sb_], lambda e: e.tensor_tensor(out=sv[0:32, cols], in0=ta, in1=tb_, op=ALU.add))
                    part2()
                else:
                    if (chunk + ts) % 2 == 0:
                        S.op("act", [bp], [sb_], lambda e: e.activation(out=sv[0:M, cols], in_=ps[0:M, :], func=AF.Copy))
                    else:
                        S.op("dve", [bp], [sb_], lambda e: e.tensor_copy(out=sv[0:M, cols], in_=ps[0:M, :]))
                if ts == TB // 512 - 1:
                    r0 = chunk * 128
                    S.dma("sp", dest[r0:r0 + M, tok0:tok0 + TB], sv[0:M, :], [sb_], [], sb_)

            segsF = [
                ("w_in", l, O_FQ, 1024, (self.fqT, False, None, False)),
                ("w_in", l, O_FK, 1024, (self.fkT, False, None, False)),
                ("w_in", l, O_FF, 8, (self.ffT, False, None, True)),
                ("w_in", l, O_NQ, 1024, (self.nqT, True, None, False)),
                ("w_in", l, O_NKC, 512, (self.nkcT, False, None, False)),
                ("w_in", l, O_NKS, 256, (self.nksT, True, None, False)),
                ("w_in", l, O_NKW, 256, (self.nkwT, True, None, False)),
                ("w_in", l, O_NG, 24, (self.ngT, False, AF.Sigmoid, True)),
                ("w_in", l, O_GF, 2048, (self.gfT, False, AF.Sigmoid, False)),
                ("w_in", l, O_GN, 2048, (self.gnT, False, AF.Sigmoid, False)),
            ]
            if sub == "A2":
                segsF = segsF[0:1]
            if sub == "A3":
                segsF = segsF[0:3]
            self.gemm_F(segsF, 16, res, resb, TB, wslots, epi)
            flush()
            if sub in ("A2", "A3", "A4"):
                continue
            vi = [0]
            for (c0, ncols, dest) in [(O_FV, 1024, self.fv), (O_NVS, 256, self.nvs), (O_NVW, 256, self.nvw)]:
                def epiT(tt, cb0, CB, ps, bp, dest=dest):
                    sv, sb_ = vst[vi[0] % 3]
                    vi[0] += 1
                    if vi[0] % 2 == 0:
                        S.op("act", [bp], [sb_], lambda e: e.activation(out=sv[:, 0:CB], in_=ps[:, 0:CB], func=AF.Copy))
                    else:
                        S.op("dve", [bp], [sb_], lambda e: e.tensor_copy(out=sv[:, 0:CB], in_=ps[:, 0:CB]))
                    r0 = tok0 + tt * 128
                    S.dma("sp", dest[r0:r0 + 128, cb0:cb0 + CB], sv[:, 0:CB], [sb_], [], sb_)
                self.gemm_T("w_in", l, c0, ncols, 16, res, resb, TB, wslots, epiT)

    def phase_foxprep(self, l):
        S, A = self.S, self.A
        f, fb = A.alloc(T, F32, parts=8, name="ffrow")
        g, gb = A.alloc(T, F32, parts=8, name="ffrow2")
        nb, nbb = self.negb
        dffT = Buf("dffT")
        S.dma("sp", f, self.ffT[:, :], [], [fb], fb)
        S.op("act", [fb, nbb], [fb], lambda e: e.activation(out=f, in_=f, func=AF.Exp, scale=-1.0, bias=nb[:, l:l + 1]))
        S.op("act", [fb], [fb], lambda e: e.activation(out=f, in_=f, func=AF.Ln, scale=1.0, bias=self.onec[0:8, :]))
        S.op("dve", [fb], [gb], lambda e: e.tensor_tensor_scan(out=g, data0=f, data1=f, initial=0.0, op0=ALU.add, op1=ALU.bypass))
        S.dma("sp", self.ffT[:, :], g, [gb], [], gb)

    def phase_cmp(self, l):
        S, A = self.S, self.A
        src, srcb = A.alloc(T, BF16, name="cmpsrc")
        w1_, w1b = A.alloc(32 * 256, BF16, name="cw1")
        w1 = w1_.rearrange("p (k n) -> p k n", k=32)
        w2_, w2b = A.alloc(2 * 128, BF16, name="cw2")
        w2 = w2_.rearrange("p (k n) -> p k n", k=2)
        pe32, pe32b = A.alloc(128, F32, parts=32, name="pe32")
        peT, peTb = A.alloc(32, BF16, name="peT")
        cst, cstb = A.alloc(2, F32, name="ccst")
        hT_, hTb = A.alloc(512, BF16, name="chT")
        hT = hT_.rearrange("p (k n) -> p k n", k=2)
        okb_, okbb = A.alloc(256, BF16, name="cokb")
        ta, tab = A.alloc(256, F32, parts=32, name="cta")
        tb_, tbb = A.alloc(256, F32, parts=32, name="ctb")
        cosc, coscb = A.alloc(256, F32, parts=32, name="cosc")
        sinc, sincb = A.alloc(256, F32, parts=32, name="sinc")
        ovb_, ovbb = A.alloc(256, BF16, name="covb")
        ovb = ovb_.rearrange("p (k n) -> p k n", k=2)
        identf, idfb = self.C["c_identf"]
        rrot, rrb = self.C["c_rrot"]
        S.dma("sp", cosc, self.cin["c_cosc"][:, :], [], [coscb], coscb)
        S.dma("sp", sinc, self.cin["c_sinc"][:, :], [], [sincb], sincb)
        for kind in range(2):
            pre = "cmp_k" if kind == 0 else "cmp_v"
            self.load_w(pre + "_w1", l, 0, 256, 32, w1, w1b)
            self.load_w(pre + "_w2", l, 0, 128, 2, w2, w2b)
            S.dma("sp", pe32, self.small[pre + "_pe"][l, :, :], [], [pe32b], pe32b)
            S.op("pe", [pe32b, idfb], [self.bps[0]], lambda e: e.transpose(out=self.ps[0][:, 0:32], in_=pe32, identity=identf[0:32, 0:32]))
            S.op("dve", [self.bps[0]], [peTb], lambda e: e.tensor_copy(out=peT, in_=self.ps[0][:, 0:32]))
            for hc in range(2):
                for ll in range(32):
                    S.op("pe", [w1b, peTb], [self.bps[1]],
                         lambda e: e.matmul(self.ps[1][:, hc:hc + 1], lhsT=w1[:, ll, hc * 128:(hc + 1) * 128], rhs=peT[:, ll:ll + 1],
                                            start=(ll == 0), stop=(ll == 31)), inc=(ll == 31))
            S.op("dve", [self.bps[1]], [cstb], lambda e: e.tensor_copy(out=cst, in_=self.ps[1][:, 0:2]))
            for g in range(2):
                r0 = kind * 256 + g * 128
                S.dma("sp", src, self.nkcT[r0:r0 + 128, :], [], [srcb], srcb)
                for hc in range(2):
                    pb = 2 + hc
                    for ll in range(32):
                        S.op("pe", [w1b, srcb], [self.bps[pb]],
                             lambda e: e.matmul(self.ps[pb][:, 0:255], lhsT=w1[:, ll, hc * 128:(hc + 1) * 128], rhs=src[:, ll:ll + 16 * 254 + 1:16],
                                                start=(ll == 0), stop=(ll == 31)), inc=(ll == 31))
                    S.op("act", [self.bps[pb], cstb], [hTb],
                         lambda e: e.activation(out=hT[:, hc, 0:255], in_=self.ps[pb][:, 0:255], func=AF.Gelu_apprx_tanh, bias=cst[:, hc:hc + 1]))
                if kind == 0:
                    for hc in range(2):
                        S.op("pe", [w2b, hTb], [self.bps[4]],
                             lambda e: e.matmul(self.ps[4][:, 0:255], lhsT=w2[:, hc, :], rhs=hT[:, hc, 0:255], start=(hc == 0), stop=(hc == 1)), inc=(hc == 1))
                    S.op("dve", [], [okbb], lambda e: e.memset(okb_, 0.0))
                    S.op("act", [self.bps[4]], [okbb], lambda e: e.activation(out=okb_[:, 0:255], in_=self.ps[4][:, 0:255], func=AF.Copy))
                    S.op("dve", [self.bps[4], coscb], [tab], lambda e: e.tensor_tensor(out=ta[:, 0:255], in0=self.ps[4][0:32, 0:255], in1=cosc[:, 0:255], op=ALU.mult))
                    S.op("pe", [okbb, rrb], [self.bps[5]], lambda e: e.matmul(self.ps[5][0:32, 0:255], lhsT=rrot, rhs=okb_[0:32, 0:255], start=True, stop=True))
                    S.op("dve", [self.bps[5], sincb], [tbb], lambda e: e.tensor_tensor(out=tb_[:, 0:255], in0=self.ps[5][0:32, 0:255], in1=sinc[:, 0:255], op=ALU.mult))
                    S.op("dve", [tab, tbb], [okbb], lambda e: e.tensor_tensor(out=okb_[0:32, 0:255], in0=ta[:, 0:255], in1=tb_[:, 0:255], op=ALU.add))
                    S.dma("sp", self.kcT[g, :, :], okb_, [okbb], [], okbb)
                else:
                    S.op("dve", [], [ovbb], lambda e: e.memset(ovb_, 0.0))
                    for nc_ in range(2):
                        n = 128 if nc_ == 0 else 127
                        for hc in range(2):
                            S.op("pe", [w2b, hTb], [self.bps[6]],
                                 lambda e: e.matmul(self.ps[6][0:n, nc_ * 128:(nc_ + 1) * 128], lhsT=hT[:, hc, nc_ * 128:nc_ * 128 + n], rhs=w2[:, hc, :],
                                                    start=(hc == 0), stop=(hc == 1)), inc=(hc == 1))
                        S.op("act", [self.bps[6]], [ovbb], lambda e: e.activation(out=ovb[0:n, nc_, :], in_=self.ps[6][0:n, nc_ * 128:(nc_ + 1) * 128], func=AF.Copy))
                    S.dma("sp", self.vc[g, :, :].rearrange("(k p) d -> p k d", p=128), ovb, [ovbb], [], ovbb)

    def attn_core(self, Q, tiles, qT, qTb, kT, kTb, vv, vb, pO, pR, sbanks, ptiles, extra=None, fox=None, nstart=None):
        S = self.S
        ones, ob = self.C["c_onesb"]
        identb, idb = self.C["c_identb"]
        n = len(tiles)
        q0 = Q * 512
        st = nstart if nstart is not None else [0]

        def qk(i):
            kt, qlo, qhi, masks = tiles[i]
            sb_i = sbanks[(st[0] + i) % len(sbanks)]
            ps, bp = self.ps[sb_i], self.bps[sb_i]
            last_plain = (extra is None and not masks)
            S.op("pe", [kTb, qTb], [bp], lambda e: e.matmul(ps[:, qlo:qhi], lhsT=kT[:, kt * 128:(kt + 1) * 128], rhs=qT[:, q0 + qlo:q0 + qhi],
                                                            start=True, stop=last_plain), inc=last_plain)
            for mi, (slo, map_, mb) in enumerate(masks):
                lastm = (extra is None and mi == len(masks) - 1)
                S.op("pe", [mb, idb], [bp], lambda e: e.matmul(ps[:, slo:slo + 128], lhsT=identb, rhs=map_, start=False, stop=lastm), inc=lastm)
            if extra is not None:
                elhs, eb, erhs, erb = extra
                S.op("pe", [eb, erb], [bp], lambda e: e.matmul(ps[:, qlo:qhi], lhsT=elhs(kt), rhs=erhs[:, q0 + qlo:q0 + qhi], start=False, stop=True))

        def pv(i):
            kt, qlo, qhi, masks = tiles[i]
            sb_i = sbanks[(st[0] + i) % len(sbanks)]
            ps, bp = self.ps[sb_i], self.bps[sb_i]
            pt, ptb = ptiles[(st[0] + i) % len(ptiles)]
            if fox is not None:
                cqb, cqbb, negck, nckb, tmps = fox
                tm, tmb = tmps[(st[0] + i) % len(tmps)]
                S.op("dve", [bp, cqbb], [tmb], lambda e: e.scalar_tensor_tensor(out=tm[:, qlo:qhi], in0=ps[:, qlo:qhi], scalar=SCALE, in1=cqb[:, qlo:qhi],
                                                                               op0=ALU.mult, op1=ALU.add))
                S.op("act", [tmb, nckb], [ptb], lambda e: e.activation(out=pt[:, qlo:qhi], in_=tm[:, qlo:qhi], func=AF.Exp, bias=negck(kt), scale=1.0))
            else:
                S.op("act", [bp], [ptb], lambda e: e.activation(out=pt[:, qlo:qhi], in_=ps[:, qlo:qhi], func=AF.Exp, scale=SCALE))
            S.op("pe", [vb, ptb], [self.bps[pO]], lambda e: e.matmul(self.ps[pO][:, qlo:qhi], lhsT=vv[:, kt, :], rhs=pt[:, qlo:qhi], start=(i == 0), stop=(i == n - 1)),
                 inc=False)
            S.op("pe", [ob, ptb], [self.bps[pR]], lambda e: e.matmul(self.ps[pR][:, qlo:qhi], lhsT=ones, rhs=pt[:, qlo:qhi], start=(i == 0), stop=(i == n - 1)),
                 inc=True)

        for i in range(n + 1):
            if i < n:
                qk(i)
            if i >= 1:
                pv(i - 1)
        st[0] += n

    def phase_fox(self, l):
        S, A = self.S, self.A
        Lrow, Lrb = A.alloc(T, F32, parts=8, name="Lrow")
        crow, crb = A.alloc(T, F32, parts=8, name="crow")
        Lcol, Lcb = A.alloc(NT * 8, F32, name="Lcol")
        qs = [A.alloc(T, BF16, name="fq%d" % i) for i in range(2)]
        ks = [A.alloc(T, BF16, name="fk%d" % i) for i in range(2)]
        vs = [A.alloc(T, BF16, name="fv%d" % i) for i in range(2)]
        cqs = [A.alloc(512, F32, name="cqb%d" % i) for i in range(2)]
        tmps = [A.alloc(512, F32, name="ftm%d" % i) for i in range(2)]
        pts = [A.alloc(512, BF16, name="fpt%d" % i) for i in range(3)]
        rinv, rib = A.alloc(512, F32, name="frinv")
        ost = [A.alloc(512, BF16, name="fost%d" % i) for i in range(2)]
        sel8, s8b = A.alloc(1024, F32, parts=8, name="sel8")
        S.dma("sp", sel8, self.cin["c_sel8"][:, :], [], [s8b], s8b)
        identf, idfb = self.C["c_identf"]
        caus, cab = self.C["c_caus"]
        S.dma("sp", Lrow, self.ffT[:, :], [], [Lrb], Lrb)
        S.op("dve", [Lrb], [crb], lambda e: e.tensor_scalar(out=crow, in0=Lrow, scalar1=-1.0, scalar2=None, op0=ALU.mult))
        for kt in range(NT):
            S.op("pe", [Lrb, idfb], [self.bps[7]], lambda e: e.transpose(out=self.ps[7][:, kt * 8:(kt + 1) * 8], in_=Lrow[:, kt * 128:(kt + 1) * 128], identity=identf[0:8, 0:8]),
                 inc=(kt == NT - 1))
        S.op("dve", [self.bps[7]], [Lcb], lambda e: e.tensor_copy(out=Lcol, in_=self.ps[7][:, 0:NT * 8]))
        nst = [0]
        it = 0
        for h in range(8):
            qT, qTb = qs[h % 2]
            kT, kTb = ks[h % 2]
            v_, vb = vs[h % 2]
            vv = v_.rearrange("p (k d) -> p k d", k=NT)
            S.dma("sp", qT, self.fqT[h * 128:(h + 1) * 128, :], [], [qTb], qTb)
            S.dma("sp", kT, self.fkT[h * 128:(h + 1) * 128, :], [], [kTb], kTb)
            S.dma("sp", vv, self.fv[:, h * 128:(h + 1) * 128].rearrange("(k p) d -> p k d", p=128), [], [vb], vb)
            for Q in range(T // 512):
                cqb, cqbb = cqs[it % 2]
                pO, pR = (2, 3) if it % 2 == 0 else (4, 5)
                it += 1
                S.op("pe", [s8b, crb], [self.bps[6]], lambda e: e.matmul(self.ps[6][:, :], lhsT=sel8[:, h * 128:(h + 1) * 128], rhs=crow[:, Q * 512:(Q + 1) * 512], start=True, stop=True))
                S.op("act", [self.bps[6]], [cqbb], lambda e: e.activation(out=cqb, in_=self.ps[6][:, :], func=AF.Copy))
                tiles = []
                for kt in range(4 * Q + 4):
                    i = kt - 4 * Q
                    if i < 0:
                        tiles.append((kt, 0, 512, []))
                    else:
                        tiles.append((kt, 128 * i, 512, [(128 * i, caus, cab)]))
                self.attn_core(Q, tiles, qT, qTb, kT, kTb, vv, vb, pO, pR, (0, 1), pts,
                               fox=(cqb, cqbb, lambda kt: Lcol[:, kt * 8 + h:kt * 8 + h + 1], Lcb, tmps), nstart=nst)
                o_, ob_ = ost[it % 2]
                S.op("dve", [self.bps[pR]], [rib], lambda e: e.reciprocal(out=rinv, in_=self.ps[pR][:, :]))
                S.op("dve", [self.bps[pO], rib], [ob_], lambda e: e.tensor_tensor(out=o_, in0=self.ps[pO][:, :], in1=rinv, op=ALU.mult))
                S.dma("sp", self.ofoxT[h * 128:(h + 1) * 128, Q * 512:(Q + 1) * 512], o_, [ob_], [], ob_)

    def phase_nsa(self, l):
        S, A = self.S, self.A
        ksT, ksb = A.alloc(T, BF16, name="nks")
        kwT, kwb = A.alloc(T, BF16, name="nkw")
        vs_, vsb = A.alloc(T, BF16, name="nvs")
        vw_, vwb = A.alloc(T, BF16, name="nvw")
        vsv = vs_.rearrange("p (k d) -> p k d", k=NT)
        vwv = vw_.rearrange("p (k d) -> p k d", k=NT)
        kc, kcb = A.alloc(256, BF16, name="nkc")
        vc_, vcb = A.alloc(256, BF16, name="nvc")
        vcv = vc_.rearrange("p (k d) -> p k d", k=2)
        qh = [A.alloc(T, BF16, name="nq%d" % i) for i in range(4)]
        oc_, ocb = A.alloc(4 * T, BF16, name="noc")
        ocv = oc_.rearrange("p (h t) -> p h t", h=4)
        nm, nmb = A.alloc(T, BF16, parts=64, name="nnegm")
        gT, gTb = A.alloc(T, F32, parts=24, name="ngT")
        mkc, mkcb = A.alloc(512, F32, name="nmaskc")
        keep, keepb = A.alloc(128, F32, name="nkeep")
        addt, addb = A.alloc(128, F32, name="nadd")
        e_t = [A.alloc(256, F32, name="ne%d" % i) for i in range(2)]
        p_t = [A.alloc(256, F32, name="np%d" % i) for i in range(2)]
        pb_t = [A.alloc(256, BF16, name="npb%d" % i) for i in range(2)]
        pT_t = [A.alloc(256, BF16, name="npT%d" % i) for i in range(2)]
        rs_t = [A.alloc(2, F32, name="nrs%d" % i) for i in range(2)]
        pg, pgb = A.alloc(256, F32, name="npg")
        imp, impb = A.alloc(64, F32, name="nimp")
        v0, v0b = A.alloc(64, F32, name="nv0")
        v1, v1b = A.alloc(64, F32, name="nv1")
        v2, v2b = A.alloc(64, F32, name="nv2")
        mx, mxb = A.alloc(16, F32, name="nmx")
        sl, slb = A.alloc(64, BF16, name="nsl")
        pts = [A.alloc(512, BF16, name="npt%d" % i) for i in range(3)]
        osn = [A.alloc(512, F32, name="nos%d" % i) for i in range(2)]
        rinv, rib = A.alloc(512, F32, name="nrinv")
        acc, accb = A.alloc(512, F32, name="nacc")
        ost = [A.alloc(512, BF16, name="nost%d" % i) for i in range(2)]
        Ec, Eb = A.alloc(T, BF16, parts=64, name="cE")
        sel24, s24b = A.alloc(3072, F32, parts=24, name="sel24")
        S.dma("sp", Ec, self.cin["c_E"][:, :], [], [Eb], Eb)
        S.dma("sp", sel24, self.cin["c_sel24"][:, :], [], [s24b], s24b)
        identb, idb = self.C["c_identb"]
        caus, cab = self.C["c_caus"]
        anti, anb = self.C["c_anti"]
        S.dma("sp", gT, self.ngT[:, :], [], [gTb], gTb)
        S.dma("sp", mkc, self.cin["c_maskc"][:, :], [], [mkcb], mkcb)
        S.dma("sp", keep, self.cin["c_keep"][:, :], [], [keepb], keepb)
        S.dma("sp", addt, self.cin["c_add"][:, :], [], [addb], addb)
        nst = [0]
        it = 0
        for g in range(2):
            S.dma("sp", ksT, self.nksT[g * 128:(g + 1) * 128, :], [], [ksb], ksb)
            S.dma("sp", kwT, self.nkwT[g * 128:(g + 1) * 128, :], [], [kwb], kwb)
            S.dma("sp", vsv, self.nvs[:, g * 128:(g + 1) * 128].rearrange("(k p) d -> p k d", p=128), [], [vsb], vsb)
            S.dma("sp", vwv, self.nvw[:, g * 128:(g + 1) * 128].rearrange("(k p) d -> p k d", p=128), [], [vwb], vwb)
            S.dma("sp", kc, self.kcT[g, :, :], [], [kcb], kcb)
            S.dma("sp", vcv, self.vc[g, :, :].rearrange("(k p) d -> p k d", p=128), [], [vcb], vcb)
            for hh in range(4):
                h = g * 4 + hh
                S.dma("sp", qh[hh][0], self.nqT[h * 128:(h + 1) * 128, :], [], [qh[hh][1]], qh[hh][1])
            k1 = 0
            for qt in range(NT):
                msk = mkc[:, 256 - 8 * qt:512 - 8 * qt]
                for hh in range(4):
                    qT, qTb = qh[hh]
                    e_, eb_ = e_t[k1 % 2]
                    p_, pb_ = p_t[k1 % 2]
                    pbf, pbfb = pb_t[k1 % 2]
                    pT, pTb = pT_t[k1 % 2]
                    rs, rsb = rs_t[k1 % 2]
                    sbk = k1 % 2
                    k1 += 1
                    S.op("pe", [qTb, kcb], [self.bps[sbk]], lambda e: e.matmul(self.ps[sbk][:, 0:256], lhsT=qT[:, qt * 128:(qt + 1) * 128], rhs=kc, start=True, stop=True))
                    S.op("act", [self.bps[sbk]], [eb_], lambda e: e.activation(out=e_, in_=self.ps[sbk][:, 0:256], func=AF.Exp, scale=SCALE))
                    S.op("dve", [], [rsb], lambda e: e.memset(rs, 0.0))
                    S.op("dve", [eb_, mkcb], [eb_, rsb], lambda e: e.scalar_tensor_tensor(out=e_, in0=e_, scalar=1.0, in1=msk, op0=ALU.mult, op1=ALU.mult, accum_out=rs[:, 0:1]))
                    S.op("dve", [rsb], [rsb], lambda e: e.tensor_scalar(out=rs[:, 1:2], in0=rs[:, 0:1], scalar1=1e-30, scalar2=None, op0=ALU.max))
                    S.op("dve", [rsb], [rsb], lambda e: e.reciprocal(out=rs[:, 1:2], in_=rs[:, 1:2]))
                    S.op("dve", [eb_, rsb], [pb_], lambda e: e.tensor_scalar(out=p_, in0=e_, scalar1=rs[:, 1:2], scalar2=None, op0=ALU.mult))
                    S.op("dve", [pb_], [pbfb], lambda e: e.tensor_copy(out=pbf, in_=p_))
                    if hh == 0:
                        S.op("dve", [pb_], [pgb], lambda e: e.tensor_copy(out=pg, in_=p_))
                    else:
                        S.op("dve", [pb_, pgb], [pgb], lambda e: e.tensor_tensor(out=pg, in0=pg, in1=p_, op=ALU.add))
                    psb = self.ps[6][:].bitcast(BF16)
                    for nc_ in range(2):
                        S.op("pe", [pbfb, idb], [self.bps[6]], lambda e: e.transpose(out=psb[:, nc_ * 128:(nc_ + 1) * 128], in_=pbf[:, nc_ * 128:(nc_ + 1) * 128], identity=identb),
                             inc=(nc_ == 1))
                    S.op("act", [self.bps[6]], [pTb], lambda e: e.activation(out=pT, in_=psb[:, 0:256], func=AF.Copy))
                    for nc_ in range(2):
                        S.op("pe", [vcb, pTb], [self.bps[7]], lambda e: e.matmul(self.ps[7][:, 0:128], lhsT=vcv[:, nc_, :], rhs=pT[:, nc_ * 128:(nc_ + 1) * 128],
                                                                                 start=(nc_ == 0), stop=(nc_ == 1)), inc=(nc_ == 1))
                    S.op("dve", [self.bps[7]], [ocb], lambda e: e.tensor_copy(out=ocv[:, hh, qt * 128:(qt + 1) * 128], in_=self.ps[7][:, 0:128]))
                pg4 = pg.rearrange("p (j m) -> p j m", m=4)
                S.op("dve", [pgb], [impb], lambda e: e.tensor_tensor(out=imp, in0=pg4[:, :, 0], in1=pg4[:, :, 1], op=ALU.add))
                S.op("dve", [pgb, impb], [impb], lambda e: e.tensor_tensor(out=imp, in0=imp, in1=pg4[:, :, 2], op=ALU.add))
                S.op("dve", [pgb, impb], [impb], lambda e: e.scalar_tensor_tensor(out=imp, in0=pg4[:, :, 3], scalar=0.5, in1=imp, op0=ALU.mult, op1=ALU.add))
                S.op("dve", [pgb, impb], [impb], lambda e: e.scalar_tensor_tensor(out=imp[:, 1:64], in0=pg4[:, 0:63, 3], scalar=0.5, in1=imp[:, 1:64], op0=ALU.mult, op1=ALU.add))
                kp = keep[:, 64 - 2 * qt:128 - 2 * qt]
                ad = addt[:, 64 - 2 * qt:128 - 2 * qt]
                S.op("dve", [impb, keepb], [v0b], lambda e: e.tensor_tensor(out=v0, in0=imp, in1=kp, op=ALU.mult))
                S.op("dve", [v0b, addb], [v0b], lambda e: e.tensor_tensor(out=v0, in0=v0, in1=ad, op=ALU.add))
                S.op("dve", [v0b], [v0b], lambda e: e.memset(v0[:, 0:1], 1e6))
                S.op("dve", [v0b], [mxb], lambda e: e.max(out=mx[:, 0:8], in_=v0))
                S.op("dve", [v0b, mxb], [v1b], lambda e: e.match_replace(out=v1, in_to_replace=mx[:, 0:8], in_values=v0, imm_value=-2.0))
                S.op("dve", [v1b], [mxb], lambda e: e.max(out=mx[:, 8:16], in_=v1))
                S.op("dve", [v1b, mxb], [v2b], lambda e: e.match_replace(out=v2, in_to_replace=mx[:, 8:16], in_values=v1, imm_value=-2.0))
                S.op("dve", [v0b, v2b], [v2b], lambda e: e.tensor_tensor(out=v2, in0=v2, in1=v0, op=ALU.not_equal))
                S.op("dve", [v0b], [v1b], lambda e: e.tensor_scalar(out=v1, in0=v0, scalar1=0.0, scalar2=None, op0=ALU.is_ge))
                S.op("dve", [v1b, v2b], [v2b], lambda e: e.tensor_tensor(out=v2, in0=v2, in1=v1, op=ALU.mult))
                S.op("dve", [v2b], [slb], lambda e: e.tensor_scalar(out=sl, in0=v2, scalar1=-1.0, scalar2=-NEG, op0=ALU.add, op1=ALU.mult))
                psb5 = self.ps[5][:].bitcast(BF16)
                S.op("pe", [slb, idb], [self.bps[5]], lambda e: e.transpose(out=psb5[0:64, 0:128], in_=sl, identity=identb))
                S.op("act", [self.bps[5]], [nmb], lambda e: e.activation(out=nm[:, qt * 128:(qt + 1) * 128], in_=psb5[0:64, 0:128], func=AF.Copy))
            for hh in range(4):
                h = g * 4 + hh
                qT, qTb = qh[hh]
                for Q in range(T // 512):
                    tiles = []
                    for kt in range(4 * Q + 4):
                        i = kt - 4 * Q
                        if i < 0:
                            tiles.append((kt, 0, 512, []))
                        else:
                            tiles.append((kt, 128 * i, 512, [(128 * i, caus, cab)]))
                    self.attn_core(Q, tiles, qT, qTb, ksT, ksb, vsv, vsb, 2, 3, (0, 1), pts,
                                   extra=(lambda kt: Ec[:, kt * 128:(kt + 1) * 128], Eb, nm, nmb), nstart=nst)
                    os_, osb = osn[0]
                    S.op("dve", [self.bps[3]], [rib], lambda e: e.reciprocal(out=rinv, in_=self.ps[3][:, :]))
                    S.op("dve", [self.bps[2], rib], [osb], lambda e: e.tensor_tensor(out=os_, in0=self.ps[2][:, :], in1=rinv, op=ALU.mult))
                    tiles = []
                    for kt in range(max(0, 4 * Q - 4), 4 * Q + 4):
                        d = kt - 4 * Q
                        lo = max(0, d)
                        hi = min(3, d + 4)
                        masks = []
                        if 0 <= d <= 3:
                            masks.append((128 * d, caus, cab))
                        if 0 <= d + 4 <= 3:
                            masks.append((128 * (d + 4), anti, anb))
                        tiles.append((kt, 128 * lo, 128 * (hi + 1), masks))
                    tiles.sort(key=lambda t_: 0 if t_[0] == 4 * Q else 1)
                    self.attn_core(Q, tiles, qT, qTb, kwT, kwb, vwv, vwb, 4, 5, (0, 1), pts, nstart=nst)
                    ow_, owb = osn[1]
                    S.op("dve", [self.bps[5]], [rib], lambda e: e.reciprocal(out=rinv, in_=self.ps[5][:, :]))
                    S.op("dve", [self.bps[4], rib], [owb], lambda e: e.tensor_tensor(out=ow_, in0=self.ps[4][:, :], in1=rinv, op=ALU.mult))
                    o_, ob_ = ost[it % 2]
                    it += 1
                    qs_ = slice(Q * 512, (Q + 1) * 512)
                    for br, (src_, srcb_) in enumerate([(ocv[:, hh, qs_], ocb), (os_, osb), (ow_, owb)]):
                        r = h * 3 + br
                        S.op("pe", [s24b, gTb], [self.bps[6]], lambda e: e.matmul(self.ps[6][:, :], lhsT=sel24[:, r * 128:(r + 1) * 128], rhs=gT[:, qs_], start=True, stop=True))
                        if br == 0:
                            S.op("dve", [self.bps[6], srcb_], [accb], lambda e: e.tensor_tensor(out=acc, in0=src_, in1=self.ps[6][:, :], op=ALU.mult))
                        else:
                            S.op("dve", [self.bps[6], srcb_], [srcb_], lambda e: e.tensor_tensor(out=src_, in0=src_, in1=self.ps[6][:, :], op=ALU.mult))
                            if br == 1:
                                S.op("dve", [srcb_, accb], [accb], lambda e: e.tensor_tensor(out=acc, in0=acc, in1=src_, op=ALU.add))
                            else:
                                S.op("dve", [srcb_, accb], [ob_], lambda e: e.tensor_tensor(out=o_, in0=acc, in1=src_, op=ALU.add))
                    S.dma("sp", self.onsaT[h * 128:(h + 1) * 128, qs_], o_, [ob_], [], ob_)

    def phase_C(self, l, xT, xTn):
        S, A = self.S, self.A
        TB = 1024
        of_, ofb = A.alloc(8 * TB, BF16, name="Cof")
        on_, onb = A.alloc(8 * TB, BF16, name="Con")
        mx_, mxb = A.alloc(16 * TB, BF16, name="Cmix")
        ofv = of_.rearrange("p (c t) -> p c t", c=8)
        onv = on_.rearrange("p (c t) -> p c t", c=8)
        mixv = mx_.rearrange("p (c t) -> p c t", c=16)
        wslots = [A.alloc(8192, BF16, name="Cws%d" % i) for i in range(3)]
        gts = [A.alloc(TB, BF16, name="Cg%d" % i) for i in range(4)]
        t1s = [A.alloc(512, F32, name="Ct%d" % i) for i in range(2)]
        yst = [A.alloc(TB, F32, name="Cy%d" % i) for i in range(3)]
        mark = A.off
        for blk in range(T // TB):
            A.off = mark
            tok0 = blk * TB
            S.dma("sp", ofv, self.ofoxT[:, tok0:tok0 + TB].rearrange("(c p) t -> p c t", p=128), [], [ofb], ofb)
            S.dma("sp", onv, self.onsaT[:, tok0:tok0 + TB].rearrange("(c p) t -> p c t", p=128), [], [onb], onb)
            nts = TB // 512
            gi = 0
            for cb in range(4):
                wf_, wfb = wslots[(2 * cb) % 3]
                wn_, wnb = wslots[(2 * cb + 1) % 3]
                wf = wf_[:, 0:8 * 512].rearrange("p (k n) -> p k n", k=8)
                wn = wn_[:, 0:8 * 512].rearrange("p (k n) -> p k n", k=8)
                self.load_w("w_up_fox", l, cb * 512, 512, 8, wf, wfb)
                self.load_w("w_up_nsa", l, cb * 512, 512, 8, wn, wnb)
                for mc in range(4):
                    chunk = cb * 4 + mc
                    gf, gfb = gts[gi % 4]
                    gn, gnb = gts[(gi + 1) % 4]
                    gi += 2
                    S.dma("sp", gf, self.gfT[chunk * 128:(chunk + 1) * 128, tok0:tok0 + TB], [], [gfb], gfb)
                    S.dma("sp", gn, self.gnT[chunk * 128:(chunk + 1) * 128, tok0:tok0 + TB], [], [gnb], gnb)
                    for ts in range(nts):
                        pf = (chunk * nts + ts) % 3 * 2
                        pn = pf + 1
                        cols = slice(ts * 512, (ts + 1) * 512)
                        for kc in range(8):
                            S.op("pe", [wfb, ofb], [self.bps[pf]], lambda e: e.matmul(self.ps[pf][:, :], lhsT=wf[:, kc, mc * 128:(mc + 1) * 128], rhs=ofv[:, kc, cols],
                                                                                      start=(kc == 0), stop=(kc == 7)), inc=(kc == 7))
                        for kc in range(8):
                            S.op("pe", [wnb, onb], [self.bps[pn]], lambda e: e.matmul(self.ps[pn][:, :], lhsT=wn[:, kc, mc * 128:(mc + 1) * 128], rhs=onv[:, kc, cols],
                                                                                      start=(kc == 0), stop=(kc == 7)), inc=(kc == 7))
                        t1, t1b = t1s[0]
                        t2, t2b = t1s[1]
                        S.op("dve", [self.bps[pf], gfb], [t1b], lambda e: e.tensor_tensor(out=t1, in0=self.ps[pf][:, :], in1=gf[:, cols], op=ALU.mult))
                        S.op("dve", [self.bps[pn], gnb], [t2b], lambda e: e.tensor_tensor(out=t2, in0=self.ps[pn][:, :], in1=gn[:, cols], op=ALU.mult))
                        S.op("dve", [t1b, t2b], [mxb], lambda e: e.tensor_tensor(out=mixv[:, chunk, cols], in0=t1, in1=t2, op=ALU.add))
            if "mixT" in self.taps:
                if "mixT" not in self.scr:
                    self.scratch("mixT", [D, T], BF16)
                S.dma("sp", self.scr["mixT"][:, tok0:tok0 + TB].rearrange("(c p) t -> p c t", p=128), mixv, [mxb], [], mxb)
            yi = [0]

            def epi(tag, chunk, M, ts, ps, bp):
                if ts == 0:
                    yi[0] += 1
                sv, sb_ = yst[yi[0] % 3]
                cols = slice(ts * 512, (ts + 1) * 512)
                if (chunk + ts) % 2 == 0:
                    S.op("act", [bp], [sb_], lambda e: e.activation(out=sv[:, cols], in_=ps[:, :], func=AF.Copy))
                else:
                    S.op("dve", [bp], [sb_], lambda e: e.tensor_copy(out=sv[:, cols], in_=ps[:, :]))
                if ts == nts - 1:
                    S.dma("sp", self.yT[chunk * 128:(chunk + 1) * 128, tok0:tok0 + TB], sv, [sb_], [], sb_)
            self.gemm_F([("w_out", l, 0, D, None)], 16, mixv, mxb, TB, wslots[0:2], epi, banks=(0, 1, 2, 3, 4, 5))
        S.barrier()
        A.off = mark = self.A.base
        self.A.reset()
        self.norm_post(l, 1, self.yT, xT, xTn, 0, T)

    def phase_D(self, l, xT, xTn):
        S, A = self.S, self.A
        TB = 2048
        res_, resb = A.alloc(16 * TB, BF16, name="resD")
        res = res_.rearrange("p (c t) -> p c t", c=16)
        wslots = [A.alloc(8192, BF16, name="Dws%d" % i) for i in range(2)]
        sil = [A.alloc(512, F32, name="Dsil%d" % i) for i in range(2)]
        hst = [A.alloc(TB, BF16, name="Dh%d" % i) for i in range(3)]
        mark = A.off
        nts = TB // 512
        for half in range(T // TB):
            A.off = mark
            tok0 = half * TB
            self.norm_pre(l, 2, xT, tok0, TB, res, resb)
            hi = 0
            for cb in range(FF // 256):
                wt_, wtb = wslots[cb % 2]
                wg = wt_[:, 0:4096].rearrange("p (k n) -> p k n", k=16)
                wu = wt_[:, 4096:8192].rearrange("p (k n) -> p k n", k=16)
                self.load_w("w_ffn_gate", l, cb * 256, 256, 16, wg, wtb)
                self.load_w("w_ffn_up", l, cb * 256, 256, 16, wu, wtb)
                for mc in range(2):
                    chunk = cb * 2 + mc
                    hv, hb = hst[hi % 3]
                    hi += 1
                    for tp in range(nts // 2):
                        bk = [(0, 1, 2, 3), (4, 5, 6, 7)][(chunk * 2 + tp) % 2]
                        for kc in range(16):
                            for j in range(2):
                                ts = tp * 2 + j
                                cols = slice(ts * 512, (ts + 1) * 512)
                                pg_, pu_ = bk[2 * j], bk[2 * j + 1]
                                S.op("pe", [wtb, resb], [self.bps[pg_]], lambda e: e.matmul(self.ps[pg_][:, :], lhsT=wg[:, kc, mc * 128:(mc + 1) * 128], rhs=res[:, kc, cols],
                                                                                           start=(kc == 0), stop=(kc == 15)), inc=(kc == 15))
                                S.op("pe", [wtb, resb], [self.bps[pu_]], lambda e: e.matmul(self.ps[pu_][:, :], lhsT=wu[:, kc, mc * 128:(mc + 1) * 128], rhs=res[:, kc, cols],
                                                                                           start=(kc == 0), stop=(kc == 15)), inc=(kc == 15))
                        for j in range(2):
                            ts = tp * 2 + j
                            cols = slice(ts * 512, (ts + 1) * 512)
                            pg_, pu_ = bk[2 * j], bk[2 * j + 1]
                            sv, sb_ = sil[j]
                            S.op("act", [self.bps[pg_]], [sb_], lambda e: e.activation(out=sv, in_=self.ps[pg_][:, :], func=AF.Silu))
                            S.op("dve", [self.bps[pu_], sb_], [hb], lambda e: e.tensor_tensor(out=hv[:, cols], in0=sv, in1=self.ps[pu_][:, :], op=ALU.mult))
                    S.dma("sp", self.hT[chunk * 128:(chunk + 1) * 128, tok0:tok0 + TB], hv, [hb], [], hb)
        S.barrier()
        A.reset()
        KC = FF // 128
        hbs = [A.alloc(KC * 512, BF16, name="Dhb%d" % i) for i in range(2)]
        wsl = [A.alloc(KC * 128, BF16, name="Dwd%d" % i) for i in range(2)]
        yst = [A.alloc(512, F32, name="Dy%d" % i) for i in range(3)]
        yi = 0
        for blk in range(T // 512):
            tok0 = blk * 512
            hb_, hbb = hbs[blk % 2]
            hbv = hb_.rearrange("p (c t) -> p c t", c=KC)
            for q4 in range(4):
                S.dma("sp", hbv[:, q4 * 11:(q4 + 1) * 11, :], self.hT[q4 * 11 * 128:(q4 + 1) * 11 * 128, tok0:tok0 + 512].rearrange("(c p) t -> p c t", p=128), [], [hbb], hbb)
            for cb in range(16):
                wt_, wtb = wsl[cb % 2]
                wt = wt_.rearrange("p (k n) -> p k n", k=KC)
                self.load_w("w_ffn_down", l, cb * 128, 128, KC, wt, wtb)
                pb = (blk * 16 + cb) % 6
                for kc in range(KC):
                    S.op("pe", [wtb, hbb], [self.bps[pb]], lambda e: e.matmul(self.ps[pb][:, :], lhsT=wt[:, kc, :], rhs=hbv[:, kc, :], start=(kc == 0), stop=(kc == KC - 1)),
                         inc=(kc == KC - 1))
                sv, sb_ = yst[yi % 3]
                yi += 1
                if cb % 2 == 0:
                    S.op("act", [self.bps[pb]], [sb_], lambda e: e.activation(out=sv, in_=self.ps[pb][:, :], func=AF.Copy))
                else:
                    S.op("dve", [self.bps[pb]], [sb_], lambda e: e.tensor_copy(out=sv, in_=self.ps[pb][:, :]))
                S.dma("sp", self.yT[cb * 128:(cb + 1) * 128, tok0:tok0 + 512], sv, [sb_], [], sb_)
        S.barrier()
        A.reset()
        self.norm_post(l, 3, self.yT, xT, xTn, 0, T)


def make_consts_tiles(kb):
    S, A = kb.S, kb.A
    e, eb = A.alloc(1, F32, name="epsc")
    o, ob = A.alloc(1, F32, name="onec")
    return e, o


def build_program(nlayers=L, stop=None, taps=(), LW=L):
    kb = KB(nlayers, stop, taps, LW)
    orig_consts = kb.consts

    def consts2():
        S, A = kb.S, kb.A
        e, eb = A.alloc(1, F32, name="epsc")
        o, ob = A.alloc(1, F32, name="onec")
        S.op("dve", [], [eb], lambda en: en.memset(e, EPS))
        S.op("dve", [], [ob], lambda en: en.memset(o, 1.0))
        kb.epsc = e
        kb.onec = o
        orig_consts()
    kb.consts = consts2
    nc = kb.build()
    return kb, nc


_CACHE = {}


def kernel(**inputs):
    x = np.ascontiguousarray(inputs["x"], dtype=np.float32)
    if "prog" not in _CACHE:
        _CACHE["prog"] = build_program()
    kb, nc = _CACHE["prog"]
    cst = host_consts()
    base = {}
    for n, _, _ in WNAMES:
        base[n] = np.ascontiguousarray(inputs[n], dtype=np.float32)
    name_map = {"norm_mix_pre": "norm_mix_pre", "norm_mix_post": "norm_mix_post", "norm_ffn_pre": "norm_ffn_pre",
                "norm_ffn_post": "norm_ffn_post", "fox_forget_bias": "fox_forget_bias", "cmp_k_pe": "cmp_k_pe", "cmp_v_pe": "cmp_v_pe"}
    for n in name_map:
        base[n] = np.ascontiguousarray(inputs[n], dtype=np.float32)
    base.update(cst)
    NCORES = 4
    in_maps = []
    for c in range(NCORES):
        m = dict(base)
        m["x"] = x[c % 4]
        in_maps.append(m)
    res = run_bass_kernel_spmd(nc, in_maps, core_ids=list(range(NCORES)))
    out = np.stack([np.asarray(res.results[c]["out"], dtype=np.float32) for c in range(4)], axis=0)
    return out
```

```python
import numpy as np
from contextlib import ExitStack
import concourse.bass as bass
import concourse.mybir as mybir
from concourse.bass_utils import run_bass_kernel_spmd

F32 = mybir.dt.float32
BF16 = mybir.dt.bfloat16
AF = mybir.ActivationFunctionType
ALU = mybir.AluOpType

T = 4096
D = 2048
NT = T // 128
L = 2
FF = 5632
INW = 9760
SCALE = 128 ** -0.5
NEG = -30000.0
EPS = 1e-6
O_FQ, O_FK, O_FV, O_FF, O_NQ, O_NKC, O_NVC, O_NKS, O_NVS, O_NKW, O_NVW, O_NG, O_GF, O_GN = (
    0, 1024, 2048, 3072, 3080, 4104, 4360, 4616, 4872, 5128, 5384, 5640, 5664, 7712)


class Slot:
    __slots__ = ("sem", "cnt")

    def __init__(self, sem):
        self.sem = sem
        self.cnt = 0


class Buf:
    __slots__ = ("w", "r", "slot", "name", "persist", "excl")

    def __init__(self, name="", persist=False, excl=False):
        self.excl = excl
        self.w = None
        self.r = []
        self.slot = None
        self.name = name
        self.persist = persist


class Sched:
    def __init__(self, nc, es, nslots=72):
        self.nc = nc
        self.es = es
        self.eng = {"pe": nc.tensor, "act": nc.scalar, "dve": nc.vector, "pool": nc.gpsimd, "sp": nc.sync}
        self.sem = {k: es.enter_context(nc.semaphore("s_" + k)) for k in self.eng}
        self.cnt = {k: 0 for k in self.eng}
        self.waited = {k: {} for k in self.eng}
        self.free = [Slot(es.enter_context(nc.semaphore("d%d" % i))) for i in range(nslots)]
        self.inuse = []
        self.fence = []
        self.fence_pending = {k: False for k in self.eng}
        self.ninst = 0
        self.nwait = 0

    def _wait(self, E, tok):
        if tok is None:
            return
        sem, v = tok
        if E == "pe" and sem is self.sem["pe"]:
            return
        key = id(sem)
        if self.waited[E].get(key, 0) >= v:
            return
        self.waited[E][key] = v
        self.eng[E].wait_ge(sem, v)
        self.nwait += 1

    def _deps(self, E, reads, writes):
        if self.fence_pending[E]:
            self.fence_pending[E] = False
            for t in self.fence:
                self._wait(E, t)
        for b in reads:
            self._wait(E, b.w)
        for b in writes:
            self._wait(E, b.w)
            for t in b.r:
                self._wait(E, t)

    def _commit(self, tok, reads, writes):
        for b in reads:
            b.r.append(tok)
            if len(b.r) > 32:
                d = {}
                for s, v in b.r:
                    if id(s) not in d or d[id(s)][1] < v:
                        d[id(s)] = (s, v)
                b.r = list(d.values())
        for b in writes:
            b.w = tok
            b.r = []

    def op(self, E, reads, writes, fn, inc=True):
        ex = [b for b in reads if b.excl]
        if ex:
            reads = [b for b in reads if not b.excl]
            writes = list(writes) + ex
        self._deps(E, reads, writes)
        ins = fn(self.eng[E])
        if inc:
            self.cnt[E] += 1
            ins.then_inc(self.sem[E], 1)
            tok = (self.sem[E], self.cnt[E])
        else:
            tok = (self.sem[E], self.cnt[E] + 1)
        self._commit(tok, reads, writes)
        self.ninst += 1
        return tok

    def dma(self, Q, out, in_, reads, writes, slotbuf, **kw):
        if slotbuf.slot is None:
            slotbuf.slot = self.free.pop()
            if not slotbuf.persist:
                self.inuse.append(slotbuf.slot)
        self._deps(Q, reads, writes)
        ins = self.eng[Q].dma_start(out=out, in_=in_, **kw)
        sl = slotbuf.slot
        sl.cnt += 16
        ins.then_inc(sl.sem, 16)
        tok = (sl.sem, sl.cnt)
        self._commit(tok, reads, writes)
        self.ninst += 1
        return tok

    def barrier(self):
        f = [(self.sem[k], self.cnt[k]) for k in self.eng if self.cnt[k] > 0]
        f += [(s.sem, s.cnt) for s in self.inuse if s.cnt > 0]
        self.fence = f
        self.free.extend(self.inuse)
        self.inuse = []
        for k in self.eng:
            self.fence_pending[k] = True

    def finish(self, extra=()):
        self.barrier()
        self._deps("sp", [], [])
        for b in extra:
            self._wait("sp", b.w)


class Arena:
    def __init__(self, ap, nelem):
        self.ap = ap
        self.n = nelem
        self.base = 0
        self.off = 0

    def reset(self):
        self.off = self.base

    def freeze(self):
        self.base = self.off

    def alloc(self, n, dt=BF16, parts=128, name=""):
        k = 2 if dt == F32 else 1
        o = (self.off + 1) // 2 * 2
        assert o + n * k <= self.n, "arena overflow %s %d" % (name, o + n * k)
        self.off = o + n * k
        v = self.ap[0:parts, o:o + n * k]
        if dt == F32:
            v = v.bitcast(F32)
        return v, Buf(name)


def host_consts():
    import ml_dtypes
    bf = ml_dtypes.bfloat16
    c = {}
    c["c_identb"] = np.eye(128, dtype=np.float32).astype(bf)
    c["c_identf"] = np.eye(128, dtype=np.float32)
    c["c_onesb"] = np.ones((128, 128), np.float32).astype(bf)
    k = np.arange(128)[:, None]
    q = np.arange(128)[None, :]
    c["c_caus"] = np.where(k <= q, 0.0, NEG).astype(np.float32).astype(bf)
    c["c_anti"] = np.where(k > q, 0.0, NEG).astype(np.float32).astype(bf)
    R = np.zeros((32, 32), np.float32)
    for m in range(16):
        R[m + 16, m] = -1.0
        R[m, m + 16] = 1.0
    c["c_rrot"] = R.astype(bf)
    j = np.arange(64)[:, None]
    kk = np.arange(T)[None, :]
    c["c_E"] = (kk // 64 == j).astype(np.float32).astype(bf)
    half = 16
    inv_freq = np.power(np.float32(500000.0), -np.arange(half, dtype=np.float32) * np.float32(2.0 / 32)).astype(np.float32)

    def cs(pos):
        ang = (pos.astype(np.float32)[:, None] * inv_freq[None, :]).astype(np.float32)
        co = np.cos(ang).astype(np.float32).T
        si = np.sin(ang).astype(np.float32).T
        return np.concatenate([co, co], 0), np.concatenate([si, si], 0)
    c["c_cos"], c["c_sin"] = cs(np.arange(T))
    pc = np.arange(256) * 16 + 31
    c["c_cosc"], c["c_sinc"] = cs(pc)
    r = np.arange(128)[:, None]
    m = np.floor((r - 31) / 16.0)
    cc = np.arange(512)[None, :]
    c["c_maskc"] = ((cc - 256) <= m).astype(np.float32)
    J = np.arange(128)[None, :]
    jc = 64 + (r >= 64).astype(np.int64)
    fut = J > jc
    forced = (J == jc) | (J == jc - 1)
    c["c_keep"] = (~(fut | forced)).astype(np.float32)
    c["c_add"] = np.where(forced, 1e6, np.where(fut, -1.0, 0.0)).astype(np.float32)
    s8 = np.zeros((8, 8, 128), np.float32)
    for i in range(8):
        s8[i, i, :] = 1.0
    c["c_sel8"] = s8.reshape(8, 8 * 128)
    s24 = np.zeros((24, 24, 128), np.float32)
    for i in range(24):
        s24[i, i, :] = 1.0
    c["c_sel24"] = s24.reshape(24, 24 * 128)
    return c


WNAMES = [("w_in", D, INW), ("w_up_fox", 1024, D), ("w_up_nsa", 1024, D), ("w_out", D, D),
          ("w_ffn_gate", D, FF), ("w_ffn_up", D, FF), ("w_ffn_down", FF, D),
          ("cmp_k_w1", 4096, 256), ("cmp_k_w2", 256, 128), ("cmp_v_w1", 4096, 256), ("cmp_v_w2", 256, 128)]
SMALL = [("norm_mix_pre", [L, D]), ("norm_mix_post", [L, D]), ("norm_ffn_pre", [L, D]), ("norm_ffn_post", [L, D]),
         ("fox_forget_bias", [L, 8]), ("cmp_k_pe", [L, 32, 128]), ("cmp_v_pe", [L, 32, 128])]


class KB:
    def __init__(self, nlayers=L, stop=None, taps=(), LW=L):
        self.LW = LW
        self.nlayers = nlayers
        self.stop = stop
        self.taps = taps
        nc = self.nc = bass.Bass("TRN2", target_bir_lowering=False)
        self.es = ExitStack()
        dt = nc.dram_tensor
        self.x = dt("x", [T, D], F32, kind="ExternalInput").ap()
        self.win = {}
        for n, k, m in WNAMES:
            self.win[n] = dt(n, [LW, k, m], F32, kind="ExternalInput").ap()
        self.small = {}
        for n, shp in SMALL:
            self.small[n] = dt(n, [LW] + list(shp[1:]), F32, kind="ExternalInput").ap()
        self.cin = {}
        for n, a in host_consts().items():
            self.cin[n] = dt(n, list(a.shape), F32 if a.dtype == np.float32 else BF16, kind="ExternalInput").ap()
        self.out = dt("out", [T, D], F32, kind="ExternalOutput").ap()
        self.scr = {}

    def scratch(self, name, shape, dtype):
        kind = "ExternalOutput" if name in self.taps else "Internal"
        t = self.nc.dram_tensor(name, shape, dtype, kind=kind).ap()
        self.scr[name] = t
        return t

    def build(self):
        nc = self.nc
        with self.es as es:
            S = self.S = Sched(nc, es)
            arena_t = es.enter_context(nc.sbuf_tensor("arena", [128, 90112], BF16))
            A = self.A = Arena(arena_t, 90112)
            self.ps = [es.enter_context(nc.psum_tensor("ps%d" % i, [128, 512], F32)) for i in range(8)]
            self.bps = [Buf("ps%d" % i, excl=True) for i in range(8)]
            sc = self.scratch
            self.wb = {}
            for n, k, m in WNAMES:
                self.wb[n] = sc("wb_" + n, [self.LW, k, m], BF16)
            self.xT = [sc("xT%d" % i, [D, T], F32) for i in range(3)]
            self.yT = sc("yT", [D, T], F32)
            self.fqT = sc("fqT", [1024, T], BF16)
            self.fkT = sc("fkT", [1024, T], BF16)
            self.fv = sc("fv", [T, 1024], BF16)
            self.ffT = sc("ffT", [8, T], F32)
            self.nqT = sc("nqT", [1024, T], BF16)
            self.nkcT = sc("nkcT", [512, T], BF16)
            self.nksT = sc("nksT", [256, T], BF16)
            self.nvs = sc("nvs", [T, 256], BF16)
            self.nkwT = sc("nkwT", [256, T], BF16)
            self.nvw = sc("nvw", [T, 256], BF16)
            self.ngT = sc("ngT", [24, T], F32)
            self.gfT = sc("gfT", [D, T], BF16)
            self.gnT = sc("gnT", [D, T], BF16)
            self.kcT = sc("kcT", [2, 128, 256], BF16)
            self.vc = sc("vc", [2, 256, 128], BF16)
            self.ofoxT = sc("ofoxT", [1024, T], BF16)
            self.onsaT = sc("onsaT", [1024, T], BF16)
            self.hT = sc("hT", [FF, T], BF16)
            self.consts()
            if self.stop != (0, "T0a"):
                self.precast()
            if self.stop not in ((0, "T0a"), (0, "T0b")):
                self.transpose_in()
            S.barrier()
            cur = 0
            done = False
            for l in range(self.nlayers if self.stop not in ((0, "T0a"), (0, "T0b"), (0, "T0c")) else 0):
                for phase in ("A", "FOXPREP", "CMP", "FOX", "NSA", "C", "D"):
                    A.reset()
                    if self.stop is not None and self.stop[1] in ("A1", "A2", "A3", "A4"):
                        self.phase_A(l, self.xT[cur])
                        S.barrier()
                        done = True
                        break
                    if phase == "A":
                        self.phase_A(l, self.xT[cur])
                    elif phase == "FOXPREP":
                        self.phase_foxprep(l)
                    elif phase == "CMP":
                        self.phase_cmp(l)
                    elif phase == "FOX":
                        self.phase_fox(l)
                    elif phase == "NSA":
                        self.phase_nsa(l)
                    elif phase == "C":
                        nx = (cur + 1) % 3
                        self.phase_C(l, self.xT[cur], self.xT[nx])
                        cur = nx
                    elif phase == "D":
                        nx = (cur + 1) % 3
                        self.phase_D(l, self.xT[cur], self.xT[nx])
                        cur = nx
                    S.barrier()
                    if self.stop == (l, phase):
                        done = True
                        break
                if done:
                    break
            A.reset()
            if self.stop not in ((0, "T0a"), (0, "T0b")):
                self.transpose_out(self.xT[cur])
            S.finish(list(self.wtok.values()) if hasattr(self, 'wtok') else ())
        return nc

    def consts(self):
        S, A = self.S, self.A
        self.C = {}
        for n, ap in self.cin.items():
            shp = list(ap.shape)
            dtp = ap.dtype
            if n in ("c_cos", "c_sin", "c_maskc", "c_keep", "c_add", "c_cosc", "c_sinc", "c_E", "c_sel8", "c_sel24"):
                continue
            v, b = A.alloc(shp[1], dtp, parts=shp[0], name=n)
            S.dma("sp", v, ap[:, :], [], [b], b)
            self.C[n] = (v, b)
        g, gb = A.alloc(L * 4 * 16, F32, name="gains")
        for i, n in enumerate(["norm_mix_pre", "norm_mix_post", "norm_ffn_pre", "norm_ffn_post"]):
            for l in range(self.LW):
                o = (l * 4 + i) * 16
                S.dma("sp", g[:, o:o + 16], self.small[n][l, :].rearrange("(c p) -> p c", p=128), [], [gb], gb,
                      allow_slow_non_contiguous=True)
        self.gains = (g, gb)
        nb, nbb = A.alloc(self.LW, F32, parts=8, name="negb")
        S.dma("sp", nb, self.small["fox_forget_bias"].rearrange("l h -> h l"), [], [nbb], nbb, allow_slow_non_contiguous=True)
        S.op("dve", [nbb], [nbb], lambda e: e.tensor_scalar(out=nb, in0=nb, scalar1=-1.0, scalar2=None, op0=ALU.mult))
        self.negb = (nb, nbb)
        A.freeze()

    def precast(self):
        S = self.S
        self.wtok = {}
        for n, k, m in WNAMES:
            for l in range(self.nlayers):
                b = Buf("wc_" + n, persist=True)
                nparts = 4 if k * m > 4000000 else 1
                rs = k // nparts
                for i in range(nparts):
                    S.dma("pool", self.wb[n][l, i * rs:(i + 1) * rs, :], self.win[n][l, i * rs:(i + 1) * rs, :], [], [b], b)
                self.wtok[(n, l)] = b

    def gain(self, l, i, c):
        g = self.gains[0]
        o = (l * 4 + i) * 16 + c
        return g[:, o:o + 1]

    def transpose_in(self):
        S, A = self.S, self.A
        A.reset()
        identf, bid = self.C["c_identf"]
        xin = [A.alloc(4 * D, F32, name="xin%d" % i) for i in range(2)]
        stg = [A.alloc(16 * 512, F32, name="tstg%d" % i) for i in range(2)]
        xT0 = self.xT[0]
        for blk in range(T // 512):
            xi, xib = xin[blk % 2]
            st, stb = stg[blk % 2]
            xiv = xi.rearrange("p (a d) -> p a d", a=4)
            S.dma("sp", xiv, self.x[blk * 512:(blk + 1) * 512, :].rearrange("(a p) d -> p a d", p=128), [], [xib], xib)
            stv = st.rearrange("p (c t) -> p c t", c=16)
            for c in range(16):
                pb = c % 4
                for a in range(4):
                    S.op("pe", [xib, bid], [self.bps[pb]],
                         lambda e: e.transpose(out=self.ps[pb][:, a * 128:(a + 1) * 128], in_=xiv[:, a, c * 128:(c + 1) * 128], identity=identf),
                         inc=(a == 3))
                eng = "act" if c % 2 == 0 else "dve"
                if eng == "act":
                    S.op("act", [self.bps[pb]], [stb], lambda e: e.activation(out=stv[:, c, :], in_=self.ps[pb][:, :], func=AF.Copy))
                else:
                    S.op("dve", [self.bps[pb]], [stb], lambda e: e.tensor_copy(out=stv[:, c, :], in_=self.ps[pb][:, :]))
            S.dma("sp", xT0[:, blk * 512:(blk + 1) * 512].rearrange("(c p) t -> p c t", p=128), stv, [stb], [], stb)

    def transpose_out(self, xT):
        S, A = self.S, self.A
        identf, bid = self.C["c_identf"]
        xin = [A.alloc(16 * 512, F32, name="oin%d" % i) for i in range(2)]
        stg = [A.alloc(4 * D, F32, name="ostg%d" % i) for i in range(2)]
        for blk in range(T // 512):
            xi, xib = xin[blk % 2]
            st, stb = stg[blk % 2]
            xiv = xi.rearrange("p (c t) -> p c t", c=16)
            S.dma("sp", xiv, xT[:, blk * 512:(blk + 1) * 512].rearrange("(c p) t -> p c t", p=128), [], [xib], xib)
            stv = st.rearrange("p (a d) -> p a d", a=4)
            i = 0
            for a in range(4):
                for cg in range(4):
                    pb = i % 4
                    i += 1
                    for cc in range(4):
                        c = cg * 4 + cc
                        S.op("pe", [xib, bid], [self.bps[pb]],
                             lambda e: e.transpose(out=self.ps[pb][:, cc * 128:(cc + 1) * 128], in_=xiv[:, c, a * 128:(a + 1) * 128], identity=identf),
                             inc=(cc == 3))
                    if i % 2 == 0:
                        S.op("act", [self.bps[pb]], [stb], lambda e: e.activation(out=stv[:, a, cg * 512:(cg + 1) * 512], in_=self.ps[pb][:, :], func=AF.Copy))
                    else:
                        S.op("dve", [self.bps[pb]], [stb], lambda e: e.tensor_copy(out=stv[:, a, cg * 512:(cg + 1) * 512], in_=self.ps[pb][:, :]))
            S.dma("sp", self.out[blk * 512:(blk + 1) * 512, :].rearrange("(a p) d -> p a d", p=128), stv, [stb], [], stb)

    def rstd_bc(self, sq_chunks, sqb, nb, rs, rsb, pbank):
        S = self.S
        ones, ob = self.C["c_onesb"]
        ps, bp = self.ps[pbank], self.bps[pbank]
        for c in range(16):
            S.op("pe", [sqb, ob], [bp], lambda e: e.matmul(ps[:, 0:nb], lhsT=ones, rhs=sq_chunks(c), start=(c == 0), stop=(c == 15)), inc=(c == 15))
        S.op("act", [bp], [rsb], lambda e: e.activation(out=rs, in_=ps[:, 0:nb], func=AF.Ln, scale=1.0 / D, bias=self.epsc))
        S.op("act", [rsb], [rsb], lambda e: e.activation(out=rs, in_=rs, func=AF.Exp, scale=-0.5))

    def norm_pre(self, l, gi, xT, tok0, TB, res, resb):
        S, A = self.S, self.A
        NB = 128
        xb = [A.alloc(16 * NB, F32, name="npx%d" % i) for i in range(2)]
        sq, sqb = A.alloc(16 * NB, BF16, name="npsq")
        rs, rsb = A.alloc(NB, F32, name="nprs")
        sqv = sq.rearrange("p (c t) -> p c t", c=16)
        for bi in range(TB // NB):
            x_, xbb = xb[bi % 2]
            xv = x_.rearrange("p (c t) -> p c t", c=16)
            t0 = tok0 + bi * NB
            S.dma("sp", xv, xT[:, t0:t0 + NB].rearrange("(c p) t -> p c t", p=128), [], [xbb], xbb)
            S.op("act", [xbb], [sqb], lambda e: e.activation(out=sq, in_=x_, func=AF.Square))
            self.rstd_bc(lambda c: sqv[:, c, :], sqb, NB, rs, rsb, 7)
            for c in range(16):
                eng = "dve"
                S.op(eng, [xbb, rsb, self.gains[1]], [resb],
                     lambda e: e.scalar_tensor_tensor(out=res[:, c, bi * NB:(bi + 1) * NB], in0=xv[:, c, :], scalar=self.gain(l, gi, c), in1=rs,
                                                      op0=ALU.mult, op1=ALU.mult))

    def norm_post(self, l, gi, yT, xT, xTn, tok0, TB):
        S, A = self.S, self.A
        NB = 256
        xb = [A.alloc(16 * NB, F32, name="nqx%d" % i) for i in range(2)]
        yb = [A.alloc(16 * NB, F32, name="nqy%d" % i) for i in range(2)]
        sq, sqb = A.alloc(16 * NB, BF16, name="nqsq")
        rs, rsb = A.alloc(NB, F32, name="nqrs")
        sqv = sq.rearrange("p (c t) -> p c t", c=16)
        for bi in range(TB // NB):
            x_, xbb = xb[bi % 2]
            y_, ybb = yb[bi % 2]
            xv = x_.rearrange("p (c t) -> p c t", c=16)
            yv = y_.rearrange("p (c t) -> p c t", c=16)
            t0 = tok0 + bi * NB
            S.dma("sp", xv, xT[:, t0:t0 + NB].rearrange("(c p) t -> p c t", p=128), [], [xbb], xbb)
            S.dma("sp", yv, yT[:, t0:t0 + NB].rearrange("(c p) t -> p c t", p=128), [], [ybb], ybb)
            S.op("act", [ybb], [sqb], lambda e: e.activation(out=sq, in_=y_, func=AF.Square))
            self.rstd_bc(lambda c: sqv[:, c, :], sqb, NB, rs, rsb, 7)
            for c in range(16):
                S.op("dve", [ybb, rsb, self.gains[1]], [ybb],
                     lambda e: e.scalar_tensor_tensor(out=yv[:, c, :], in0=yv[:, c, :], scalar=self.gain(l, gi, c), in1=rs,
                                                      op0=ALU.mult, op1=ALU.mult))
            S.op("dve", [ybb, xbb], [ybb], lambda e: e.tensor_tensor(out=y_, in0=y_, in1=x_, op=ALU.add))
            S.dma("sp", xTn[:, t0:t0 + NB].rearrange("(c p) t -> p c t", p=128), yv, [ybb], [], ybb)

    def load_w(self, name, l, c0, ncols, KC, wt, wtb):
        S = self.S
        src = self.wb[name][l, :, c0:c0 + ncols].rearrange("(kc p) n -> p kc n", p=128)
        S.dma("sp", wt, src, [self.wtok[(name, l)]], [wtb], wtb)

    def gemm_F(self, segs, KC, act, actb, TB, wslots, epi, banks=(0, 1, 2, 3, 4, 5, 6), after=None):
        S = self.S
        nts = TB // 512
        bi = 0
        wi = 0
        for (wn, l, c0, ncols, tag) in segs:
            CBmax = max(128, (8192 // KC) // 128 * 128)
            CBmax = min(CBmax, 512)
            cb0 = 0
            while cb0 < ncols:
                CB = min(CBmax, ncols - cb0)
                wt_full, wtb = wslots[wi % len(wslots)]
                wi += 1
                wt = wt_full[:, 0:KC * CB].rearrange("p (k n) -> p k n", k=KC)
                self.load_w(wn, l, c0 + cb0, CB, KC, wt, wtb)
                for m0 in range(0, CB, 128):
                    M = min(128, CB - m0)
                    chunk = (cb0 + m0) // 128
                    bks = [banks[(bi + i) % len(banks)] for i in range(nts)]
                    bi += nts
                    for kc in range(KC):
                        for ts in range(nts):
                            pb = bks[ts]
                            S.op("pe", [wtb, actb], [self.bps[pb]],
                                 lambda e: e.matmul(self.ps[pb][0:M, :], lhsT=wt[:, kc, m0:m0 + M], rhs=act[:, kc, ts * 512:(ts + 1) * 512],
                                                    start=(kc == 0), stop=(kc == KC - 1)),
                                 inc=(kc == KC - 1 and ts == nts - 1))
                    for ts in range(nts):
                        epi(tag, chunk, M, ts, self.ps[bks[ts]], self.bps[bks[ts]])
                cb0 += CB

    def gemm_T(self, wn, l, c0, ncols, KC, act, actb, TB, wslots, epi, banks=(0, 1, 2, 3)):
        S = self.S
        bi = 0
        wi = 0
        cb0 = 0
        while cb0 < ncols:
            CB = min(512, ncols - cb0)
            wt_full, wtb = wslots[wi % len(wslots)]
            wi += 1
            wt = wt_full[:, 0:KC * CB].rearrange("p (k n) -> p k n", k=KC)
            self.load_w(wn, l, c0 + cb0, CB, KC, wt, wtb)
            for tt in range(TB // 128):
                pb = banks[bi % len(banks)]
                bi += 1
                for kc in range(KC):
                    S.op("pe", [wtb, actb], [self.bps[pb]],
                         lambda e: e.matmul(self.ps[pb][:, 0:CB], lhsT=act[:, kc, tt * 128:(tt + 1) * 128], rhs=wt[:, kc, :],
                                            start=(kc == 0), stop=(kc == KC - 1)),
                         inc=(kc == KC - 1))
                epi(tt, cb0, CB, self.ps[pb], self.bps[pb])
            cb0 += CB

    def phase_A(self, l, xT):
        S, A = self.S, self.A
        TB = 2048
        res_, resb = A.alloc(16 * TB, BF16, name="resA")
        res = res_.rearrange("p (c t) -> p c t", c=16)
        wslots = [A.alloc(8192, BF16, name="wsl%d" % i) for i in range(2)]
        stg = [A.alloc(TB, BF16, name="stgA%d" % i) for i in range(3)]
        stf, stfb = A.alloc(TB, F32, parts=32, name="stgAf")
        cosv, cosb = A.alloc(TB, F32, parts=32, name="cosA")
        sinv, sinb = A.alloc(TB, F32, parts=32, name="sinA")
        tA = [A.alloc(512, F32, parts=32, name="tA%d" % i) for i in range(2)]
        tB = [A.alloc(512, F32, parts=32, name="tB%d" % i) for i in range(2)]
        vst = [A.alloc(512, BF16, name="vst%d" % i) for i in range(3)]
        rrot, rrb = self.C["c_rrot"]
        mark = A.off
        for half in range(T // TB):
            tok0 = half * TB
            A.off = mark
            S.dma("sp", cosv, self.cin["c_cos"][:, tok0:tok0 + TB], [], [cosb], cosb)
            S.dma("sp", sinv, self.cin["c_sin"][:, tok0:tok0 + TB], [], [sinb], sinb)
            self.norm_pre(l, 0, xT, tok0, TB, res, resb)
            sub = self.stop[1] if self.stop is not None else ""
            if sub == "A1":
                if "resA" not in self.scr:
                    self.scratch("resA", [16 * 128, T], BF16)
                S.dma("sp", self.scr["resA"][:, tok0:tok0 + TB].rearrange("(c p) t -> p c t", p=128), res, [resb], [], resb)
                continue
            st_i = [0]
            deferred = []

            def flush():
                while deferred:
                    deferred.pop(0)()

            def epi(tag, chunk, M, ts, ps, bp):
                dest, rope, func, fp32 = tag
                if ts == 0:
                    st_i[0] += 1
                if fp32:
                    sv, sb_ = stf, stfb
                else:
                    sv, sb_ = stg[st_i[0] % 3]
                cols = slice(ts * 512, (ts + 1) * 512)
                if func is not None:
                    S.op("act", [bp], [sb_], lambda e: e.activation(out=sv[0:M, cols], in_=ps[0:M, :], func=func))
                elif rope:
                    S.op("act", [bp], [sb_], lambda e: e.activation(out=sv[0:M, cols], in_=ps[0:M, :], func=AF.Copy))
                    ta, tab = tA[ts % 2]
                    S.op("dve", [bp, cosb], [tab], lambda e: e.tensor_tensor(out=ta, in0=ps[0:32, :], in1=cosv[:, cols], op=ALU.mult))

                    def part2(sv=sv, sb_=sb_, cols=cols, ta=ta, tab=tab, ts=ts):
                        tb_, tbb = tB[ts % 2]
                        S.op("pe", [sb_, rrb], [self.bps[7]], lambda e: e.matmul(self.ps[7][0:32, :], lhsT=rrot, rhs=sv[0:32, cols], start=True, stop=True))
                        S.op("dve", [self.bps[7], sinb], [tbb], lambda e: e.tensor_tensor(out=tb_, in0=self.ps[7][0:32, :], in1=sinv[:, cols], op=ALU.mult))
                        S.op("dve", [tab, tbb], [sb_], lambda e: e.tensor_tensor(out=sv[0:32, cols], in0=ta, in1=tb_, op=ALU.add))
                    part2()
                else:
                    if (chunk + ts) % 2 == 0:
                        S.op("act", [bp], [sb_], lambda e: e.activation(out=sv[0:M, cols], in_=ps[0:M, :], func=AF.Copy))
                    else:
                        S.op("dve", [bp], [sb_], lambda e: e.tensor_copy(out=sv[0:M, cols], in_=ps[0:M, :]))
                if ts == TB // 512 - 1:
                    r0 = chunk * 128
                    S.dma("sp", dest[r0:r0 + M, tok0:tok0 + TB], sv[0:M, :], [sb_], [], sb_)

            segsF = [
                ("w_in", l, O_FQ, 1024, (self.fqT, False, None, False)),
                ("w_in", l, O_FK, 1024, (self.fkT, False, None, False)),
                ("w_in", l, O_FF, 8, (self.ffT, False, None, True)),
                ("w_in", l, O_NQ, 1024, (self.nqT, True, None, False)),
                ("w_in", l, O_NKC, 512, (self.nkcT, False, None, False)),
                ("w_in", l, O_NKS, 256, (self.nksT, True, None, False)),
                ("w_in", l, O_NKW, 256, (self.nkwT, True, None, False)),
                ("w_in", l, O_NG, 24, (self.ngT, False, AF.Sigmoid, True)),
                ("w_in", l, O_GF, 2048, (self.gfT, False, AF.Sigmoid, False)),
                ("w_in", l, O_GN, 2048, (self.gnT, False, AF.Sigmoid, False)),
            ]
            if sub == "A2":
                segsF = segsF[0:1]
            if sub == "A3":
                segsF = segsF[0:3]
            self.gemm_F(segsF, 16, res, resb, TB, wslots, epi)
            flush()
            if sub in ("A2", "A3", "A4"):
                continue
            vi = [0]
            for (c0, ncols, dest) in [(O_FV, 1024, self.fv), (O_NVS, 256, self.nvs), (O_NVW, 256, self.nvw)]:
                def epiT(tt, cb0, CB, ps, bp, dest=dest):
                    sv, sb_ = vst[vi[0] % 3]
                    vi[0] += 1
                    if vi[0] % 2 == 0:
                        S.op("act", [bp], [sb_], lambda e: e.activation(out=sv[:, 0:CB], in_=ps[:, 0:CB], func=AF.Copy))
                    else:
                        S.op("dve", [bp], [sb_], lambda e: e.tensor_copy(out=sv[:, 0:CB], in_=ps[:, 0:CB]))
                    r0 = tok0 + tt * 128
                    S.dma("sp", dest[r0:r0 + 128, cb0:cb0 + CB], sv[:, 0:CB], [sb_], [], sb_)
                self.gemm_T("w_in", l, c0, ncols, 16, res, resb, TB, wslots, epiT)

    def phase_foxprep(self, l):
        S, A = self.S, self.A
        f, fb = A.alloc(T, F32, parts=8, name="ffrow")
        g, gb = A.alloc(T, F32, parts=8, name="ffrow2")
        nb, nbb = self.negb
        dffT = Buf("dffT")
        S.dma("sp", f, self.ffT[:, :], [], [fb], fb)
        S.op("act", [fb, nbb], [fb], lambda e: e.activation(out=f, in_=f, func=AF.Exp, scale=-1.0, bias=nb[:, l:l + 1]))
        S.op("act", [fb], [fb], lambda e: e.activation(out=f, in_=f, func=AF.Ln, scale=1.0, bias=self.onec[0:8, :]))
        S.op("dve", [fb], [gb], lambda e: e.tensor_tensor_scan(out=g, data0=f, data1=f, initial=0.0, op0=ALU.add, op1=ALU.bypass))
        S.dma("sp", self.ffT[:, :], g, [gb], [], gb)

    def phase_cmp(self, l):
        S, A = self.S, self.A
        src, srcb = A.alloc(T, BF16, name="cmpsrc")
        w1_, w1b = A.alloc(32 * 256, BF16, name="cw1")
        w1 = w1_.rearrange("p (k n) -> p k n", k=32)
        w2_, w2b = A.alloc(2 * 128, BF16, name="cw2")
        w2 = w2_.rearrange("p (k n) -> p k n", k=2)
        pe32, pe32b = A.alloc(128, F32, parts=32, name="pe32")
        peT, peTb = A.alloc(32, BF16, name="peT")
        cst, cstb = A.alloc(2, F32, name="ccst")
        hT_, hTb = A.alloc(512, BF16, name="chT")
        hT = hT_.rearrange("p (k n) -> p k n", k=2)
        okb_, okbb = A.alloc(256, BF16, name="cokb")
        ta, tab = A.alloc(256, F32, parts=32, name="cta")
        tb_, tbb = A.alloc(256, F32, parts=32, name="ctb")
        cosc, coscb = A.alloc(256, F32, parts=32, name="cosc")
        sinc, sincb = A.alloc(256, F32, parts=32, name="sinc")
        ovb_, ovbb = A.alloc(256, BF16, name="covb")
        ovb = ovb_.rearrange("p (k n) -> p k n", k=2)
        identf, idfb = self.C["c_identf"]
        rrot, rrb = self.C["c_rrot"]
        S.dma("sp", cosc, self.cin["c_cosc"][:, :], [], [coscb], coscb)
        S.dma("sp", sinc, self.cin["c_sinc"][:, :], [], [sincb], sincb)
        for kind in range(2):
            pre = "cmp_k" if kind == 0 else "cmp_v"
            self.load_w(pre + "_w1", l, 0, 256, 32, w1, w1b)
            self.load_w(pre + "_w2", l, 0, 128, 2, w2, w2b)
            S.dma("sp", pe32, self.small[pre + "_pe"][l, :, :], [], [pe32b], pe32b)
            S.op("pe", [pe32b, idfb], [self.bps[0]], lambda e: e.transpose(out=self.ps[0][:, 0:32], in_=pe32, identity=identf[0:32, 0:32]))
            S.op("dve", [self.bps[0]], [peTb], lambda e: e.tensor_copy(out=peT, in_=self.ps[0][:, 0:32]))
            for hc in range(2):
                for ll in range(32):
                    S.op("pe", [w1b, peTb], [self.bps[1]],
                         lambda e: e.matmul(self.ps[1][:, hc:hc + 1], lhsT=w1[:, ll, hc * 128:(hc + 1) * 128], rhs=peT[:, ll:ll + 1],
                                            start=(ll == 0), stop=(ll == 31)), inc=(ll == 31))
            S.op("dve", [self.bps[1]], [cstb], lambda e: e.tensor_copy(out=cst, in_=self.ps[1][:, 0:2]))
            for g in range(2):
                r0 = kind * 256 + g * 128
                S.dma("sp", src, self.nkcT[r0:r0 + 128, :], [], [srcb], srcb)
                for hc in range(2):
                    pb = 2 + hc
                    for ll in range(32):
                        S.op("pe", [w1b, srcb], [self.bps[pb]],
                             lambda e: e.matmul(self.ps[pb][:, 0:255], lhsT=w1[:, ll, hc * 128:(hc + 1) * 128], rhs=src[:, ll:ll + 16 * 254 + 1:16],
                                                start=(ll == 0), stop=(ll == 31)), inc=(ll == 31))
                    S.op("act", [self.bps[pb], cstb], [hTb],
                         lambda e: e.activation(out=hT[:, hc, 0:255], in_=self.ps[pb][:, 0:255], func=AF.Gelu_apprx_tanh, bias=cst[:, hc:hc + 1]))
                if kind == 0:
                    for hc in range(2):
                        S.op("pe", [w2b, hTb], [self.bps[4]],
                             lambda e: e.matmul(self.ps[4][:, 0:255], lhsT=w2[:, hc, :], rhs=hT[:, hc, 0:255], start=(hc == 0), stop=(hc == 1)), inc=(hc == 1))
                    S.op("dve", [], [okbb], lambda e: e.memset(okb_, 0.0))
                    S.op("act", [self.bps[4]], [okbb], lambda e: e.activation(out=okb_[:, 0:255], in_=self.ps[4][:, 0:255], func=AF.Copy))
                    S.op("dve", [self.bps[4], coscb], [tab], lambda e: e.tensor_tensor(out=ta[:, 0:255], in0=self.ps[4][0:32, 0:255], in1=cosc[:, 0:255], op=ALU.mult))
                    S.op("pe", [okbb, rrb], [self.bps[5]], lambda e: e.matmul(self.ps[5][0:32, 0:255], lhsT=rrot, rhs=okb_[0:32, 0:255], start=True, stop=True))
                    S.op("dve", [self.bps[5], sincb], [tbb], lambda e: e.tensor_tensor(out=tb_[:, 0:255], in0=self.ps[5][0:32, 0:255], in1=sinc[:, 0:255], op=ALU.mult))
                    S.op("dve", [tab, tbb], [okbb], lambda e: e.tensor_tensor(out=okb_[0:32, 0:255], in0=ta[:, 0:255], in1=tb_[:, 0:255], op=ALU.add))
                    S.dma("sp", self.kcT[g, :, :], okb_, [okbb], [], okbb)
                else:
                    S.op("dve", [], [ovbb], lambda e: e.memset(ovb_, 0.0))
                    for nc_ in range(2):
                        n = 128 if nc_ == 0 else 127
                        for hc in range(2):
                            S.op("pe", [w2b, hTb], [self.bps[6]],
                                 lambda e: e.matmul(self.ps[6][0:n, nc_ * 128:(nc_ + 1) * 128], lhsT=hT[:, hc, nc_ * 128:nc_ * 128 + n], rhs=w2[:, hc, :],
                                                    start=(hc == 0), stop=(hc == 1)), inc=(hc == 1))
                        S.op("act", [self.bps[6]], [ovbb], lambda e: e.activation(out=ovb[0:n, nc_, :], in_=self.ps[6][0:n, nc_ * 128:(nc_ + 1) * 128], func=AF.Copy))
                    S.dma("sp", self.vc[g, :, :].rearrange("(k p) d -> p k d", p=128), ovb, [ovbb], [], ovbb)

    def attn_core(self, Q, tiles, qT, qTb, kT, kTb, vv, vb, pO, pR, sbanks, ptiles, extra=None, fox=None, nstart=None):
        S = self.S
        ones, ob = self.C["c_onesb"]
        identb, idb = self.C["c_identb"]
        n = len(tiles)
        q0 = Q * 512
        st = nstart if nstart is not None else [0]

        def qk(i):
            kt, qlo, qhi, masks = tiles[i]
            sb_i = sbanks[(st[0] + i) % len(sbanks)]
            ps, bp = self.ps[sb_i], self.bps[sb_i]
            last_plain = (extra is None and not masks)
            S.op("pe", [kTb, qTb], [bp], lambda e: e.matmul(ps[:, qlo:qhi], lhsT=kT[:, kt * 128:(kt + 1) * 128], rhs=qT[:, q0 + qlo:q0 + qhi],
                                                            start=True, stop=last_plain), inc=last_plain)
            for mi, (slo, map_, mb) in enumerate(masks):
                lastm = (extra is None and mi == len(masks) - 1)
                S.op("pe", [mb, idb], [bp], lambda e: e.matmul(ps[:, slo:slo + 128], lhsT=identb, rhs=map_, start=False, stop=lastm), inc=lastm)
            if extra is not None:
                elhs, eb, erhs, erb = extra
                S.op("pe", [eb, erb], [bp], lambda e: e.matmul(ps[:, qlo:qhi], lhsT=elhs(kt), rhs=erhs[:, q0 + qlo:q0 + qhi], start=False, stop=True))

        def pv(i):
            kt, qlo, qhi, masks = tiles[i]
            sb_i = sbanks[(st[0] + i) % len(sbanks)]
            ps, bp = self.ps[sb_i], self.bps[sb_i]
            pt, ptb = ptiles[(st[0] + i) % len(ptiles)]
            if fox is not None:
                cqb, cqbb, negck, nckb, tmps = fox
                tm, tmb = tmps[(st[0] + i) % len(tmps)]
                S.op("dve", [bp, cqbb], [tmb], lambda e: e.scalar_tensor_tensor(out=tm[:, qlo:qhi], in0=ps[:, qlo:qhi], scalar=SCALE, in1=cqb[:, qlo:qhi],
                                                                               op0=ALU.mult, op1=ALU.add))
                S.op("act", [tmb, nckb], [ptb], lambda e: e.activation(out=pt[:, qlo:qhi], in_=tm[:, qlo:qhi], func=AF.Exp, bias=negck(kt), scale=1.0))
            else:
                S.op("act", [bp], [ptb], lambda e: e.activation(out=pt[:, qlo:qhi], in_=ps[:, qlo:qhi], func=AF.Exp, scale=SCALE))
            S.op("pe", [vb, ptb], [self.bps[pO]], lambda e: e.matmul(self.ps[pO][:, qlo:qhi], lhsT=vv[:, kt, :], rhs=pt[:, qlo:qhi], start=(i == 0), stop=(i == n - 1)),
                 inc=False)
            S.op("pe", [ob, ptb], [self.bps[pR]], lambda e: e.matmul(self.ps[pR][:, qlo:qhi], lhsT=ones, rhs=pt[:, qlo:qhi], start=(i == 0), stop=(i == n - 1)),
                 inc=True)

        for i in range(n + 1):
            if i < n:
                qk(i)
            if i >= 1:
                pv(i - 1)
        st[0] += n

    def phase_fox(self, l):
        S, A = self.S, self.A
        Lrow, Lrb = A.alloc(T, F32, parts=8, name="Lrow")
        crow, crb = A.alloc(T, F32, parts=8, name="crow")
        Lcol, Lcb = A.alloc(NT * 8, F32, name="Lcol")
        qs = [A.alloc(T, BF16, name="fq%d" % i) for i in range(2)]
        ks = [A.alloc(T, BF16, name="fk%d" % i) for i in range(2)]
        vs = [A.alloc(T, BF16, name="fv%d" % i) for i in range(2)]
        cqs = [A.alloc(512, F32, name="cqb%d" % i) for i in range(2)]
        tmps = [A.alloc(512, F32, name="ftm%d" % i) for i in range(2)]
        pts = [A.alloc(512, BF16, name="fpt%d" % i) for i in range(3)]
        rinv, rib = A.alloc(512, F32, name="frinv")
        ost = [A.alloc(512, BF16, name="fost%d" % i) for i in range(2)]
        sel8, s8b = A.alloc(1024, F32, parts=8, name="sel8")
        S.dma("sp", sel8, self.cin["c_sel8"][:, :], [], [s8b], s8b)
        identf, idfb = self.C["c_identf"]
        caus, cab = self.C["c_caus"]
        S.dma("sp", Lrow, self.ffT[:, :], [], [Lrb], Lrb)
        S.op("dve", [Lrb], [crb], lambda e: e.tensor_scalar(out=crow, in0=Lrow, scalar1=-1.0, scalar2=None, op0=ALU.mult))
        for kt in range(NT):
            S.op("pe", [Lrb, idfb], [self.bps[7]], lambda e: e.transpose(out=self.ps[7][:, kt * 8:(kt + 1) * 8], in_=Lrow[:, kt * 128:(kt + 1) * 128], identity=identf[0:8, 0:8]),
                 inc=(kt == NT - 1))
        S.op("dve", [self.bps[7]], [Lcb], lambda e: e.tensor_copy(out=Lcol, in_=self.ps[7][:, 0:NT * 8]))
        nst = [0]
        it = 0
        for h in range(8):
            qT, qTb = qs[h % 2]
            kT, kTb = ks[h % 2]
            v_, vb = vs[h % 2]
            vv = v_.rearrange("p (k d) -> p k d", k=NT)
            S.dma("sp", qT, self.fqT[h * 128:(h + 1) * 128, :], [], [qTb], qTb)
            S.dma("sp", kT, self.fkT[h * 128:(h + 1) * 128, :], [], [kTb], kTb)
            S.dma("sp", vv, self.fv[:, h * 128:(h + 1) * 128].rearrange("(k p) d -> p k d", p=128), [], [vb], vb)
            for Q in range(T // 512):
                cqb, cqbb = cqs[it % 2]
                pO, pR = (2, 3) if it % 2 == 0 else (4, 5)
                it += 1
                S.op("pe", [s8b, crb], [self.bps[6]], lambda e: e.matmul(self.ps[6][:, :], lhsT=sel8[:, h * 128:(h + 1) * 128], rhs=crow[:, Q * 512:(Q + 1) * 512], start=True, stop=True))
                S.op("act", [self.bps[6]], [cqbb], lambda e: e.activation(out=cqb, in_=self.ps[6][:, :], func=AF.Copy))
                tiles = []
                for kt in range(4 * Q + 4):
                    i = kt - 4 * Q
                    if i < 0:
                        tiles.append((kt, 0, 512, []))
                    else:
                        tiles.append((kt, 128 * i, 512, [(128 * i, caus, cab)]))
                self.attn_core(Q, tiles, qT, qTb, kT, kTb, vv, vb, pO, pR, (0, 1), pts,
                               fox=(cqb, cqbb, lambda kt: Lcol[:, kt * 8 + h:kt * 8 + h + 1], Lcb, tmps), nstart=nst)
                o_, ob_ = ost[it % 2]
                S.op("dve", [self.bps[pR]], [rib], lambda e: e.reciprocal(out=rinv, in_=self.ps[pR][:, :]))
                S.op("dve", [self.bps[pO], rib], [ob_], lambda e: e.tensor_tensor(out=o_, in0=self.ps[pO][:, :], in1=rinv, op=ALU.mult))
                S.dma("sp", self.ofoxT[h * 128:(h + 1) * 128, Q * 512:(Q + 1) * 512], o_, [ob_], [], ob_)

    def phase_nsa(self, l):
        S, A = self.S, self.A
        ksT, ksb = A.alloc(T, BF16, name="nks")
        kwT, kwb = A.alloc(T, BF16, name="nkw")
        vs_, vsb = A.alloc(T, BF16, name="nvs")
        vw_, vwb = A.alloc(T, BF16, name="nvw")
        vsv = vs_.rearrange("p (k d) -> p k d", k=NT)
        vwv = vw_.rearrange("p (k d) -> p k d", k=NT)
        kc, kcb = A.alloc(256, BF16, name="nkc")
        vc_, vcb = A.alloc(256, BF16, name="nvc")
        vcv = vc_.rearrange("p (k d) -> p k d", k=2)
        qh = [A.alloc(T, BF16, name="nq%d" % i) for i in range(4)]
        oc_, ocb = A.alloc(4 * T, BF16, name="noc")
        ocv = oc_.rearrange("p (h t) -> p h t", h=4)
        nm, nmb = A.alloc(T, BF16, parts=64, name="nnegm")
        gT, gTb = A.alloc(T, F32, parts=24, name="ngT")
        mkc, mkcb = A.alloc(512, F32, name="nmaskc")
        keep, keepb = A.alloc(128, F32, name="nkeep")
        addt, addb = A.alloc(128, F32, name="nadd")
        e_t = [A.alloc(256, F32, name="ne%d" % i) for i in range(2)]
        p_t = [A.alloc(256, F32, name="np%d" % i) for i in range(2)]
        pb_t = [A.alloc(256, BF16, name="npb%d" % i) for i in range(2)]
        pT_t = [A.alloc(256, BF16, name="npT%d" % i) for i in range(2)]
        rs_t = [A.alloc(2, F32, name="nrs%d" % i) for i in range(2)]
        pg, pgb = A.alloc(256, F32, name="npg")
        imp, impb = A.alloc(64, F32, name="nimp")
        v0, v0b = A.alloc(64, F32, name="nv0")
        v1, v1b = A.alloc(64, F32, name="nv1")
        v2, v2b = A.alloc(64, F32, name="nv2")
        mx, mxb = A.alloc(16, F32, name="nmx")
        sl, slb = A.alloc(64, BF16, name="nsl")
        pts = [A.alloc(512, BF16, name="npt%d" % i) for i in range(3)]
        osn = [A.alloc(512, F32, name="nos%d" % i) for i in range(2)]
        rinv, rib = A.alloc(512, F32, name="nrinv")
        acc, accb = A.alloc(512, F32, name="nacc")
        ost = [A.alloc(512, BF16, name="nost%d" % i) for i in range(2)]
        Ec, Eb = A.alloc(T, BF16, parts=64, name="cE")
        sel24, s24b = A.alloc(3072, F32, parts=24, name="sel24")
        S.dma("sp", Ec, self.cin["c_E"][:, :], [], [Eb], Eb)
        S.dma("sp", sel24, self.cin["c_sel24"][:, :], [], [s24b], s24b)
        identb, idb = self.C["c_identb"]
        caus, cab = self.C["c_caus"]
        anti, anb = self.C["c_anti"]
        S.dma("sp", gT, self.ngT[:, :], [], [gTb], gTb)
        S.dma("sp", mkc, self.cin["c_maskc"][:, :], [], [mkcb], mkcb)
        S.dma("sp", keep, self.cin["c_keep"][:, :], [], [keepb], keepb)
        S.dma("sp", addt, self.cin["c_add"][:, :], [], [addb], addb)
        nst = [0]
        it = 0
        for g in range(2):
            S.dma("sp", ksT, self.nksT[g * 128:(g + 1) * 128, :], [], [ksb], ksb)
            S.dma("sp", kwT, self.nkwT[g * 128:(g + 1) * 128, :], [], [kwb], kwb)
            S.dma("sp", vsv, self.nvs[:, g * 128:(g + 1) * 128].rearrange("(k p) d -> p k d", p=128), [], [vsb], vsb)
            S.dma("sp", vwv, self.nvw[:, g * 128:(g + 1) * 128].rearrange("(k p) d -> p k d", p=128), [], [vwb], vwb)
            S.dma("sp", kc, self.kcT[g, :, :], [], [kcb], kcb)
            S.dma("sp", vcv, self.vc[g, :, :].rearrange("(k p) d -> p k d", p=128), [], [vcb], vcb)
            for hh in range(4):
                h = g * 4 + hh
                S.dma("sp", qh[hh][0], self.nqT[h * 128:(h + 1) * 128, :], [], [qh[hh][1]], qh[hh][1])
            k1 = 0
            for qt in range(NT):
                msk = mkc[:, 256 - 8 * qt:512 - 8 * qt]
                for hh in range(4):
                    qT, qTb = qh[hh]
                    e_, eb_ = e_t[k1 % 2]
                    p_, pb_ = p_t[k1 % 2]
                    pbf, pbfb = pb_t[k1 % 2]
                    pT, pTb = pT_t[k1 % 2]
                    rs, rsb = rs_t[k1 % 2]
                    sbk = k1 % 2
                    k1 += 1
                    S.op("pe", [qTb, kcb], [self.bps[sbk]], lambda e: e.matmul(self.ps[sbk][:, 0:256], lhsT=qT[:, qt * 128:(qt + 1) * 128], rhs=kc, start=True, stop=True))
                    S.op("act", [self.bps[sbk]], [eb_], lambda e: e.activation(out=e_, in_=self.ps[sbk][:, 0:256], func=AF.Exp, scale=SCALE))
                    S.op("dve", [], [rsb], lambda e: e.memset(rs, 0.0))
                    S.op("dve", [eb_, mkcb], [eb_, rsb], lambda e: e.scalar_tensor_tensor(out=e_, in0=e_, scalar=1.0, in1=msk, op0=ALU.mult, op1=ALU.mult, accum_out=rs[:, 0:1]))
                    S.op("dve", [rsb], [rsb], lambda e: e.tensor_scalar(out=rs[:, 1:2], in0=rs[:, 0:1], scalar1=1e-30, scalar2=None, op0=ALU.max))
                    S.op("dve", [rsb], [rsb], lambda e: e.reciprocal(out=rs[:, 1:2], in_=rs[:, 1:2]))
                    S.op("dve", [eb_, rsb], [pb_], lambda e: e.tensor_scalar(out=p_, in0=e_, scalar1=rs[:, 1:2], scalar2=None, op0=ALU.mult))
                    S.op("dve", [pb_], [pbfb], lambda e: e.tensor_copy(out=pbf, in_=p_))
                    if hh == 0:
                        S.op("dve", [pb_], [pgb], lambda e: e.tensor_copy(out=pg, in_=p_))
                    else:
                        S.op("dve", [pb_, pgb], [pgb], lambda e: e.tensor_tensor(out=pg, in0=pg, in1=p_, op=ALU.add))
                    psb = self.ps[6][:].bitcast(BF16)
                    for nc_ in range(2):
                        S.op("pe", [pbfb, idb], [self.bps[6]], lambda e: e.transpose(out=psb[:, nc_ * 128:(nc_ + 1) * 128], in_=pbf[:, nc_ * 128:(nc_ + 1) * 128], identity=identb),
                             inc=(nc_ == 1))
                    S.op("act", [self.bps[6]], [pTb], lambda e: e.activation(out=pT, in_=psb[:, 0:256], func=AF.Copy))
                    for nc_ in range(2):
                        S.op("pe", [vcb, pTb], [self.bps[7]], lambda e: e.matmul(self.ps[7][:, 0:128], lhsT=vcv[:, nc_, :], rhs=pT[:, nc_ * 128:(nc_ + 1) * 128],
                                                                                 start=(nc_ == 0), stop=(nc_ == 1)), inc=(nc_ == 1))
                    S.op("dve", [self.bps[7]], [ocb], lambda e: e.tensor_copy(out=ocv[:, hh, qt * 128:(qt + 1) * 128], in_=self.ps[7][:, 0:128]))
                pg4 = pg.rearrange("p (j m) -> p j m", m=4)
                S.op("dve", [pgb], [impb], lambda e: e.tensor_tensor(out=imp, in0=pg4[:, :, 0], in1=pg4[:, :, 1], op=ALU.add))
                S.op("dve", [pgb, impb], [impb], lambda e: e.tensor_tensor(out=imp, in0=imp, in1=pg4[:, :, 2], op=ALU.add))
                S.op("dve", [pgb, impb], [impb], lambda e: e.scalar_tensor_tensor(out=imp, in0=pg4[:, :, 3], scalar=0.5, in1=imp, op0=ALU.mult, op1=ALU.add))
                S.op("dve", [pgb, impb], [impb], lambda e: e.scalar_tensor_tensor(out=imp[:, 1:64], in0=pg4[:, 0:63, 3], scalar=0.5, in1=imp[:, 1:64], op0=ALU.mult, op1=ALU.add))
                kp = keep[:, 64 - 2 * qt:128 - 2 * qt]
                ad = addt[:, 64 - 2 * qt:128 - 2 * qt]
                S.op("dve", [impb, keepb], [v0b], lambda e: e.tensor_tensor(out=v0, in0=imp, in1=kp, op=ALU.mult))
                S.op("dve", [v0b, addb], [v0b], lambda e: e.tensor_tensor(out=v0, in0=v0, in1=ad, op=ALU.add))
                S.op("dve", [v0b], [v0b], lambda e: e.memset(v0[:, 0:1], 1e6))
                S.op("dve", [v0b], [mxb], lambda e: e.max(out=mx[:, 0:8], in_=v0))
                S.op("dve", [v0b, mxb], [v1b], lambda e: e.match_replace(out=v1, in_to_replace=mx[:, 0:8], in_values=v0, imm_value=-2.0))
                S.op("dve", [v1b], [mxb], lambda e: e.max(out=mx[:, 8:16], in_=v1))
                S.op("dve", [v1b, mxb], [v2b], lambda e: e.match_replace(out=v2, in_to_replace=mx[:, 8:16], in_values=v1, imm_value=-2.0))
                S.op("dve", [v0b, v2b], [v2b], lambda e: e.tensor_tensor(out=v2, in0=v2, in1=v0, op=ALU.not_equal))
                S.op("dve", [v0b], [v1b], lambda e: e.tensor_scalar(out=v1, in0=v0, scalar1=0.0, scalar2=None, op0=ALU.is_ge))
                S.op("dve", [v1b, v2b], [v2b], lambda e: e.tensor_tensor(out=v2, in0=v2, in1=v1, op=ALU.mult))
                S.op("dve", [v2b], [slb], lambda e: e.tensor_scalar(out=sl, in0=v2, scalar1=-1.0, scalar2=-NEG, op0=ALU.add, op1=ALU.mult))
                psb5 = self.ps[5][:].bitcast(BF16)
                S.op("pe", [slb, idb], [self.bps[5]], lambda e: e.transpose(out=psb5[0:64, 0:128], in_=sl, identity=identb))
                S.op("act", [self.bps[5]], [nmb], lambda e: e.activation(out=nm[:, qt * 128:(qt + 1) * 128], in_=psb5[0:64, 0:128], func=AF.Copy))
            for hh in range(4):
                h = g * 4 + hh
                qT, qTb = qh[hh]
                for Q in range(T // 512):
                    tiles = []
                    for kt in range(4 * Q + 4):
                        i = kt - 4 * Q
                        if i < 0:
                            tiles.append((kt, 0, 512, []))
                        else:
                            tiles.append((kt, 128 * i, 512, [(128 * i, caus, cab)]))
                    self.attn_core(Q, tiles, qT, qTb, ksT, ksb, vsv, vsb, 2, 3, (0, 1), pts,
                                   extra=(lambda kt: Ec[:, kt * 128:(kt + 1) * 128], Eb, nm, nmb), nstart=nst)
                    os_, osb = osn[0]
                    S.op("dve", [self.bps[3]], [rib], lambda e: e.reciprocal(out=rinv, in_=self.ps[3][:, :]))
                    S.op("dve", [self.bps[2], rib], [osb], lambda e: e.tensor_tensor(out=os_, in0=self.ps[2][:, :], in1=rinv, op=ALU.mult))
                    tiles = []
                    for kt in range(max(0, 4 * Q - 4), 4 * Q + 4):
                        d = kt - 4 * Q
                        lo = max(0, d)
                        hi = min(3, d + 4)
                        masks = []
                        if 0 <= d <= 3:
                            masks.append((128 * d, caus, cab))
                        if 0 <= d + 4 <= 3:
                            masks.append((128 * (d + 4), anti, anb))
                        tiles.append((kt, 128 * lo, 128 * (hi + 1), masks))
                    tiles.sort(key=lambda t_: 0 if t_[0] == 4 * Q else 1)
                    self.attn_core(Q, tiles, qT, qTb, kwT, kwb, vwv, vwb, 4, 5, (0, 1), pts, nstart=nst)
                    ow_, owb = osn[1]
                    S.op("dve", [self.bps[5]], [rib], lambda e: e.reciprocal(out=rinv, in_=self.ps[5][:, :]))
                    S.op("dve", [self.bps[4], rib], [owb], lambda e: e.tensor_tensor(out=ow_, in0=self.ps[4][:, :], in1=rinv, op=ALU.mult))
                    o_, ob_ = ost[it % 2]
                    it += 1
                    qs_ = slice(Q * 512, (Q + 1) * 512)
                    for br, (src_, srcb_) in enumerate([(ocv[:, hh, qs_], ocb), (os_, osb), (ow_, owb)]):
                        r = h * 3 + br
                        S.op("pe", [s24b, gTb], [self.bps[6]], lambda e: e.matmul(self.ps[6][:, :], lhsT=sel24[:, r * 128:(r + 1) * 128], rhs=gT[:, qs_], start=True, stop=True))
                        if br == 0:
                            S.op("dve", [self.bps[6], srcb_], [accb], lambda e: e.tensor_tensor(out=acc, in0=src_, in1=self.ps[6][:, :], op=ALU.mult))
                        else:
                            S.op("dve", [self.bps[6], srcb_], [srcb_], lambda e: e.tensor_tensor(out=src_, in0=src_, in1=self.ps[6][:, :], op=ALU.mult))
                            if br == 1:
                                S.op("dve", [srcb_, accb], [accb], lambda e: e.tensor_tensor(out=acc, in0=acc, in1=src_, op=ALU.add))
                            else:
                                S.op("dve", [srcb_, accb], [ob_], lambda e: e.tensor_tensor(out=o_, in0=acc, in1=src_, op=ALU.add))
                    S.dma("sp", self.onsaT[h * 128:(h + 1) * 128, qs_], o_, [ob_], [], ob_)

    def phase_C(self, l, xT, xTn):
        S, A = self.S, self.A
        TB = 1024
        of_, ofb = A.alloc(8 * TB, BF16, name="Cof")
        on_, onb = A.alloc(8 * TB, BF16, name="Con")
        mx_, mxb = A.alloc(16 * TB, BF16, name="Cmix")
        ofv = of_.rearrange("p (c t) -> p c t", c=8)
        onv = on_.rearrange("p (c t) -> p c t", c=8)
        mixv = mx_.rearrange("p (c t) -> p c t", c=16)
        wslots = [A.alloc(8192, BF16, name="Cws%d" % i) for i in range(3)]
        gts = [A.alloc(TB, BF16, name="Cg%d" % i) for i in range(4)]
        t1s = [A.alloc(512, F32, name="Ct%d" % i) for i in range(2)]
        yst = [A.alloc(TB, F32, name="Cy%d" % i) for i in range(3)]
        mark = A.off
        for blk in range(T // TB):
            A.off = mark
            tok0 = blk * TB
            S.dma("sp", ofv, self.ofoxT[:, tok0:tok0 + TB].rearrange("(c p) t -> p c t", p=128), [], [ofb], ofb)
            S.dma("sp", onv, self.onsaT[:, tok0:tok0 + TB].rearrange("(c p) t -> p c t", p=128), [], [onb], onb)
            nts = TB // 512
            gi = 0
            for cb in range(4):
                wf_, wfb = wslots[(2 * cb) % 3]
                wn_, wnb = wslots[(2 * cb + 1) % 3]
                wf = wf_[:, 0:8 * 512].rearrange("p (k n) -> p k n", k=8)
                wn = wn_[:, 0:8 * 512].rearrange("p (k n) -> p k n", k=8)
                self.load_w("w_up_fox", l, cb * 512, 512, 8, wf, wfb)
                self.load_w("w_up_nsa", l, cb * 512, 512, 8, wn, wnb)
                for mc in range(4):
                    chunk = cb * 4 + mc
                    gf, gfb = gts[gi % 4]
                    gn, gnb = gts[(gi + 1) % 4]
                    gi += 2
                    S.dma("sp", gf, self.gfT[chunk * 128:(chunk + 1) * 128, tok0:tok0 + TB], [], [gfb], gfb)
                    S.dma("sp", gn, self.gnT[chunk * 128:(chunk + 1) * 128, tok0:tok0 + TB], [], [gnb], gnb)
                    for ts in range(nts):
                        pf = (chunk * nts + ts) % 3 * 2
                        pn = pf + 1
                        cols = slice(ts * 512, (ts + 1) * 512)
                        for kc in range(8):
                            S.op("pe", [wfb, ofb], [self.bps[pf]], lambda e: e.matmul(self.ps[pf][:, :], lhsT=wf[:, kc, mc * 128:(mc + 1) * 128], rhs=ofv[:, kc, cols],
                                                                                      start=(kc == 0), stop=(kc == 7)), inc=False)
                        for kc in range(8):
                            S.op("pe", [wnb, onb], [self.bps[pn]], lambda e: e.matmul(self.ps[pn][:, :], lhsT=wn[:, kc, mc * 128:(mc + 1) * 128], rhs=onv[:, kc, cols],
                                                                                      start=(kc == 0), stop=(kc == 7)), inc=(kc == 7))
                        t1, t1b = t1s[0]
                        t2, t2b = t1s[1]
                        S.op("dve", [self.bps[pf], gfb], [t1b], lambda e: e.tensor_tensor(out=t1, in0=self.ps[pf][:, :], in1=gf[:, cols], op=ALU.mult))
                        S.op("dve", [self.bps[pn], gnb], [t2b], lambda e: e.tensor_tensor(out=t2, in0=self.ps[pn][:, :], in1=gn[:, cols], op=ALU.mult))
                        S.op("dve", [t1b, t2b], [mxb], lambda e: e.tensor_tensor(out=mixv[:, chunk, cols], in0=t1, in1=t2, op=ALU.add))
            if "mixT" in self.taps:
                if "mixT" not in self.scr:
                    self.scratch("mixT", [D, T], BF16)
                S.dma("sp", self.scr["mixT"][:, tok0:tok0 + TB].rearrange("(c p) t -> p c t", p=128), mixv, [mxb], [], mxb)
            yi = [0]

            def epi(tag, chunk, M, ts, ps, bp):
                if ts == 0:
                    yi[0] += 1
                sv, sb_ = yst[yi[0] % 3]
                cols = slice(ts * 512, (ts + 1) * 512)
                if (chunk + ts) % 2 == 0:
                    S.op("act", [bp], [sb_], lambda e: e.activation(out=sv[:, cols], in_=ps[:, :], func=AF.Copy))
                else:
                    S.op("dve", [bp], [sb_], lambda e: e.tensor_copy(out=sv[:, cols], in_=ps[:, :]))
                if ts == nts - 1:
                    S.dma("sp", self.yT[chunk * 128:(chunk + 1) * 128, tok0:tok0 + TB], sv, [sb_], [], sb_)
            self.gemm_F([("w_out", l, 0, D, None)], 16, mixv, mxb, TB, wslots[0:2], epi, banks=(0, 1, 2, 3, 4, 5))
        S.barrier()
        A.off = mark = self.A.base
        self.A.reset()
        self.norm_post(l, 1, self.yT, xT, xTn, 0, T)

    def phase_D(self, l, xT, xTn):
        S, A = self.S, self.A
        TB = 2048
        res_, resb = A.alloc(16 * TB, BF16, name="resD")
        res = res_.rearrange("p (c t) -> p c t", c=16)
        wslots = [A.alloc(8192, BF16, name="Dws%d" % i) for i in range(2)]
        sil = [A.alloc(512, F32, name="Dsil%d" % i) for i in range(2)]
        hst = [A.alloc(TB, BF16, name="Dh%d" % i) for i in range(3)]
        mark = A.off
        nts = TB // 512
        for half in range(T // TB):
            A.off = mark
            tok0 = half * TB
            self.norm_pre(l, 2, xT, tok0, TB, res, resb)
            hi = 0
            for cb in range(FF // 256):
                wt_, wtb = wslots[cb % 2]
                wg = wt_[:, 0:4096].rearrange("p (k n) -> p k n", k=16)
                wu = wt_[:, 4096:8192].rearrange("p (k n) -> p k n", k=16)
                self.load_w("w_ffn_gate", l, cb * 256, 256, 16, wg, wtb)
                self.load_w("w_ffn_up", l, cb * 256, 256, 16, wu, wtb)
                for mc in range(2):
                    chunk = cb * 2 + mc
                    hv, hb = hst[hi % 3]
                    hi += 1
                    for tp in range(nts // 2):
                        bk = [(0, 1, 2, 3), (4, 5, 6, 7)][(chunk * 2 + tp) % 2]
                        for kc in range(16):
                            for j in range(2):
                                ts = tp * 2 + j
                                cols = slice(ts * 512, (ts + 1) * 512)
                                pg_, pu_ = bk[2 * j], bk[2 * j + 1]
                                S.op("pe", [wtb, resb], [self.bps[pg_]], lambda e: e.matmul(self.ps[pg_][:, :], lhsT=wg[:, kc, mc * 128:(mc + 1) * 128], rhs=res[:, kc, cols],
                                                                                           start=(kc == 0), stop=(kc == 15)), inc=False)
                                S.op("pe", [wtb, resb], [self.bps[pu_]], lambda e: e.matmul(self.ps[pu_][:, :], lhsT=wu[:, kc, mc * 128:(mc + 1) * 128], rhs=res[:, kc, cols],
                                                                                           start=(kc == 0), stop=(kc == 15)), inc=(kc == 15 and j == 1))
                        for j in range(2):
                            ts = tp * 2 + j
                            cols = slice(ts * 512, (ts + 1) * 512)
                            pg_, pu_ = bk[2 * j], bk[2 * j + 1]
                            sv, sb_ = sil[j]
                            S.op("act", [self.bps[pg_]], [sb_], lambda e: e.activation(out=sv, in_=self.ps[pg_][:, :], func=AF.Silu))
                            S.op("dve", [self.bps[pu_], sb_], [hb], lambda e: e.tensor_tensor(out=hv[:, cols], in0=sv, in1=self.ps[pu_][:, :], op=ALU.mult))
                    S.dma("sp", self.hT[chunk * 128:(chunk + 1) * 128, tok0:tok0 + TB], hv, [hb], [], hb)
        S.barrier()
        A.reset()
        KC = FF // 128
        TB3 = 1024
        hb_, hbb = A.alloc(KC * TB3, BF16, name="Dhb")
        hbv = hb_.rearrange("p (c t) -> p c t", c=KC)
        wsl = [A.alloc(KC * 256, BF16, name="Dwd%d" % i) for i in range(2)]
        yst = [A.alloc(TB3, F32, name="Dy%d" % i) for i in range(3)]
        yi = 0
        gi = 0
        for blk in range(T // TB3):
            tok0 = blk * TB3
            for q4 in range(4):
                S.dma("sp", hbv[:, q4 * 11:(q4 + 1) * 11, :], self.hT[q4 * 11 * 128:(q4 + 1) * 11 * 128, tok0:tok0 + TB3].rearrange("(c p) t -> p c t", p=128), [], [hbb], hbb)
            for cb in range(8):
                wt_, wtb = wsl[cb % 2]
                wt = wt_.rearrange("p (k n) -> p k n", k=KC)
                self.load_w("w_ffn_down", l, cb * 256, 256, KC, wt, wtb)
                for mc in range(2):
                    chunk = cb * 2 + mc
                    bks = [(gi % 3) * 2, (gi % 3) * 2 + 1]
                    gi += 1
                    for kc in range(KC):
                        for ts in range(2):
                            pb = bks[ts]
                            S.op("pe", [wtb, hbb], [self.bps[pb]], lambda e: e.matmul(self.ps[pb][:, :], lhsT=wt[:, kc, mc * 128:(mc + 1) * 128], rhs=hbv[:, kc, ts * 512:(ts + 1) * 512],
                                                                                      start=(kc == 0), stop=(kc == KC - 1)),
                                 inc=(kc == KC - 1 and ts == 1))
                    sv, sb_ = yst[yi % 3]
                    yi += 1
                    for ts in range(2):
                        pb = bks[ts]
                        if ts == 0:
                            S.op("act", [self.bps[pb]], [sb_], lambda e: e.activation(out=sv[:, 0:512], in_=self.ps[pb][:, :], func=AF.Copy))
                        else:
                            S.op("dve", [self.bps[pb]], [sb_], lambda e: e.tensor_copy(out=sv[:, 512:1024], in_=self.ps[pb][:, :]))
                    S.dma("sp", self.yT[chunk * 128:(chunk + 1) * 128, tok0:tok0 + TB3], sv, [sb_], [], sb_)
        S.barrier()
        A.reset()
        self.norm_post(l, 3, self.yT, xT, xTn, 0, T)


def make_consts_tiles(kb):
    S, A = kb.S, kb.A
    e, eb = A.alloc(1, F32, name="epsc")
    o, ob = A.alloc(1, F32, name="onec")
    return e, o


def build_program(nlayers=L, stop=None, taps=(), LW=L):
    kb = KB(nlayers, stop, taps, LW)
    orig_consts = kb.consts

    def consts2():
        S, A = kb.S, kb.A
        e, eb = A.alloc(1, F32, name="epsc")
        o, ob = A.alloc(1, F32, name="onec")
        S.op("dve", [], [eb], lambda en: en.memset(e, EPS))
        S.op("dve", [], [ob], lambda en: en.memset(o, 1.0))
        kb.epsc = e
        kb.onec = o
        orig_consts()
    kb.consts = consts2
    nc = kb.build()
    return kb, nc


_CACHE = {}


def kernel(**inputs):
    x = np.ascontiguousarray(inputs["x"], dtype=np.float32)
    if "prog" not in _CACHE:
        _CACHE["prog"] = build_program()
    kb, nc = _CACHE["prog"]
    cst = host_consts()
    base = {}
    for n, _, _ in WNAMES:
        base[n] = np.ascontiguousarray(inputs[n], dtype=np.float32)
    name_map = {"norm_mix_pre": "norm_mix_pre", "norm_mix_post": "norm_mix_post", "norm_ffn_pre": "norm_ffn_pre",
                "norm_ffn_post": "norm_ffn_post", "fox_forget_bias": "fox_forget_bias", "cmp_k_pe": "cmp_k_pe", "cmp_v_pe": "cmp_v_pe"}
    for n in name_map:
        base[n] = np.ascontiguousarray(inputs[n], dtype=np.float32)
    base.update(cst)
    NCORES = 4
    in_maps = []
    for c in range(NCORES):
        m = dict(base)
        m["x"] = x[c % 4]
        in_maps.append(m)
    res = run_bass_kernel_spmd(nc, in_maps, core_ids=list(range(NCORES)))
    out = np.stack([np.asarray(res.results[c]["out"], dtype=np.float32) for c in range(4)], axis=0)
    return out
```

```python
import numpy as np
from contextlib import ExitStack
import concourse.bass as bass
import concourse.mybir as mybir
from concourse.bass_utils import run_bass_kernel_spmd

F32 = mybir.dt.float32
BF16 = mybir.dt.bfloat16
AF = mybir.ActivationFunctionType
ALU = mybir.AluOpType

T = 4096
D = 2048
NT = T // 128
L = 2
FF = 5632
INW = 9760
SCALE = 128 ** -0.5
NEG = -30000.0
EPS = 1e-6
O_FQ, O_FK, O_FV, O_FF, O_NQ, O_NKC, O_NVC, O_NKS, O_NVS, O_NKW, O_NVW, O_NG, O_GF, O_GN = (
    0, 1024, 2048, 3072, 3080, 4104, 4360, 4616, 4872, 5128, 5384, 5640, 5664, 7712)


class Slot:
    __slots__ = ("sem", "cnt")

    def __init__(self, sem):
        self.sem = sem
        self.cnt = 0


class Buf:
    __slots__ = ("w", "r", "slot", "name", "persist", "excl")

    def __init__(self, name="", persist=False, excl=False):
        self.excl = excl
        self.w = None
        self.r = []
        self.slot = None
        self.name = name
        self.persist = persist


class Sched:
    def __init__(self, nc, es, nslots=72):
        self.nc = nc
        self.es = es
        self.eng = {"pe": nc.tensor, "act": nc.scalar, "dve": nc.vector, "pool": nc.gpsimd, "sp": nc.sync}
        self.sem = {k: es.enter_context(nc.semaphore("s_" + k)) for k in self.eng}
        self.cnt = {k: 0 for k in self.eng}
        self.waited = {k: {} for k in self.eng}
        self.free = [Slot(es.enter_context(nc.semaphore("d%d" % i))) for i in range(nslots)]
        self.inuse = []
        self.fence = []
        self.fence_pending = {k: False for k in self.eng}
        self.ninst = 0
        self.nwait = 0

    def _wait(self, E, tok):
        if tok is None:
            return
        sem, v = tok
        if E == "pe" and sem is self.sem["pe"]:
            return
        key = id(sem)
        if self.waited[E].get(key, 0) >= v:
            return
        self.waited[E][key] = v
        self.eng[E].wait_ge(sem, v)
        self.nwait += 1

    def _deps(self, E, reads, writes):
        if self.fence_pending[E]:
            self.fence_pending[E] = False
            for t in self.fence:
                self._wait(E, t)
        for b in reads:
            self._wait(E, b.w)
        for b in writes:
            self._wait(E, b.w)
            for t in b.r:
                self._wait(E, t)

    def _commit(self, tok, reads, writes):
        for b in reads:
            b.r.append(tok)
            if len(b.r) > 32:
                d = {}
                for s, v in b.r:
                    if id(s) not in d or d[id(s)][1] < v:
                        d[id(s)] = (s, v)
                b.r = list(d.values())
        for b in writes:
            b.w = tok
            b.r = []

    def op(self, E, reads, writes, fn, inc=True):
        ex = [b for b in reads if b.excl]
        if ex:
            reads = [b for b in reads if not b.excl]
            writes = list(writes) + ex
        self._deps(E, reads, writes)
        ins = fn(self.eng[E])
        if inc:
            self.cnt[E] += 1
            ins.then_inc(self.sem[E], 1)
            tok = (self.sem[E], self.cnt[E])
        else:
            tok = (self.sem[E], self.cnt[E] + 1)
        self._commit(tok, reads, writes)
        self.ninst += 1
        return tok

    def dma(self, Q, out, in_, reads, writes, slotbuf, **kw):
        if slotbuf.slot is None:
            slotbuf.slot = self.free.pop()
            if not slotbuf.persist:
                self.inuse.append(slotbuf.slot)
        self._deps(Q, reads, writes)
        ins = self.eng[Q].dma_start(out=out, in_=in_, **kw)
        sl = slotbuf.slot
        sl.cnt += 16
        ins.then_inc(sl.sem, 16)
        tok = (sl.sem, sl.cnt)
        self._commit(tok, reads, writes)
        self.ninst += 1
        return tok

    def barrier(self):
        f = [(self.sem[k], self.cnt[k]) for k in self.eng if self.cnt[k] > 0]
        f += [(s.sem, s.cnt) for s in self.inuse if s.cnt > 0]
        self.fence = f
        self.free.extend(self.inuse)
        self.inuse = []
        for k in self.eng:
            self.fence_pending[k] = True

    def finish(self, extra=()):
        self.barrier()
        self._deps("sp", [], [])
        for b in extra:
            self._wait("sp", b.w)


class Arena:
    def __init__(self, ap, nelem):
        self.ap = ap
        self.n = nelem
        self.base = 0
        self.off = 0

    def reset(self):
        self.off = self.base

    def freeze(self):
        self.base = self.off

    def alloc(self, n, dt=BF16, parts=128, name=""):
        k = 2 if dt == F32 else 1
        o = (self.off + 1) // 2 * 2
        assert o + n * k <= self.n, "arena overflow %s %d" % (name, o + n * k)
        self.off = o + n * k
        v = self.ap[0:parts, o:o + n * k]
        if dt == F32:
            v = v.bitcast(F32)
        return v, Buf(name)


def host_consts():
    import ml_dtypes
    bf = ml_dtypes.bfloat16
    c = {}
    c["c_identb"] = np.eye(128, dtype=np.float32).astype(bf)
    c["c_identf"] = np.eye(128, dtype=np.float32)
    c["c_onesb"] = np.ones((128, 128), np.float32).astype(bf)
    k = np.arange(128)[:, None]
    q = np.arange(128)[None, :]
    c["c_caus"] = np.where(k <= q, 0.0, NEG).astype(np.float32).astype(bf)
    c["c_anti"] = np.where(k > q, 0.0, NEG).astype(np.float32).astype(bf)
    R = np.zeros((32, 32), np.float32)
    for m in range(16):
        R[m + 16, m] = -1.0
        R[m, m + 16] = 1.0
    c["c_rrot"] = R.astype(bf)
    j = np.arange(64)[:, None]
    kk = np.arange(T)[None, :]
    c["c_E"] = (kk // 64 == j).astype(np.float32).astype(bf)
    half = 16
    inv_freq = np.power(np.float32(500000.0), -np.arange(half, dtype=np.float32) * np.float32(2.0 / 32)).astype(np.float32)

    def cs(pos):
        ang = (pos.astype(np.float32)[:, None] * inv_freq[None, :]).astype(np.float32)
        co = np.cos(ang).astype(np.float32).T
        si = np.sin(ang).astype(np.float32).T
        return np.concatenate([co, co], 0), np.concatenate([si, si], 0)
    c["c_cos"], c["c_sin"] = cs(np.arange(T))
    pc = np.arange(256) * 16 + 31
    c["c_cosc"], c["c_sinc"] = cs(pc)
    r = np.arange(128)[:, None]
    m = np.floor((r - 31) / 16.0)
    cc = np.arange(512)[None, :]
    c["c_maskc"] = ((cc - 256) <= m).astype(np.float32)
    J = np.arange(128)[None, :]
    jc = 64 + (r >= 64).astype(np.int64)
    fut = J > jc
    forced = (J == jc) | (J == jc - 1)
    c["c_keep"] = (~(fut | forced)).astype(np.float32)
    c["c_add"] = np.where(forced, 1e6, np.where(fut, -1.0, 0.0)).astype(np.float32)
    s8 = np.zeros((8, 8, 128), np.float32)
    for i in range(8):
        s8[i, i, :] = 1.0
    c["c_sel8"] = s8.reshape(8, 8 * 128)
    s24 = np.zeros((24, 24, 128), np.float32)
    for i in range(24):
        s24[i, i, :] = 1.0
    c["c_sel24"] = s24.reshape(24, 24 * 128)
    return c


WNAMES = [("w_in", D, INW), ("w_up_fox", 1024, D), ("w_up_nsa", 1024, D), ("w_out", D, D),
          ("w_ffn_gate", D, FF), ("w_ffn_up", D, FF), ("w_ffn_down", FF, D),
          ("cmp_k_w1", 4096, 256), ("cmp_k_w2", 256, 128), ("cmp_v_w1", 4096, 256), ("cmp_v_w2", 256, 128)]
SMALL = [("norm_mix_pre", [L, D]), ("norm_mix_post", [L, D]), ("norm_ffn_pre", [L, D]), ("norm_ffn_post", [L, D]),
         ("fox_forget_bias", [L, 8]), ("cmp_k_pe", [L, 32, 128]), ("cmp_v_pe", [L, 32, 128])]


class KB:
    def __init__(self, nlayers=L, stop=None, taps=(), LW=L):
        self.LW = LW
        self.nlayers = nlayers
        self.stop = stop
        self.taps = taps
        nc = self.nc = bass.Bass("TRN2", target_bir_lowering=False)
        self.es = ExitStack()
        dt = nc.dram_tensor
        self.x = dt("x", [T, D], F32, kind="ExternalInput").ap()
        self.win = {}
        for n, k, m in WNAMES:
            self.win[n] = dt(n, [LW, k, m], F32, kind="ExternalInput").ap()
        self.small = {}
        for n, shp in SMALL:
            self.small[n] = dt(n, [LW] + list(shp[1:]), F32, kind="ExternalInput").ap()
        self.cin = {}
        for n, a in host_consts().items():
            self.cin[n] = dt(n, list(a.shape), F32 if a.dtype == np.float32 else BF16, kind="ExternalInput").ap()
        self.out = dt("out", [T, D], F32, kind="ExternalOutput").ap()
        self.scr = {}

    def scratch(self, name, shape, dtype):
        kind = "ExternalOutput" if name in self.taps else "Internal"
        t = self.nc.dram_tensor(name, shape, dtype, kind=kind).ap()
        self.scr[name] = t
        return t

    def build(self):
        nc = self.nc
        with self.es as es:
            S = self.S = Sched(nc, es)
            arena_t = es.enter_context(nc.sbuf_tensor("arena", [128, 90112], BF16))
            A = self.A = Arena(arena_t, 90112)
            self.ps = [es.enter_context(nc.psum_tensor("ps%d" % i, [128, 512], F32)) for i in range(8)]
            self.bps = [Buf("ps%d" % i, excl=True) for i in range(8)]
            sc = self.scratch
            self.wb = {}
            for n, k, m in WNAMES:
                self.wb[n] = sc("wb_" + n, [self.LW, k, m], BF16)
            self.xT = [sc("xT%d" % i, [D, T], F32) for i in range(3)]
            self.yT = sc("yT", [D, T], F32)
            self.fqT = sc("fqT", [1024, T], BF16)
            self.fkT = sc("fkT", [1024, T], BF16)
            self.fv = sc("fv", [T, 1024], BF16)
            self.ffT = sc("ffT", [8, T], F32)
            self.nqT = sc("nqT", [1024, T], BF16)
            self.nkcT = sc("nkcT", [512, T], BF16)
            self.nksT = sc("nksT", [256, T], BF16)
            self.nvs = sc("nvs", [T, 256], BF16)
            self.nkwT = sc("nkwT", [256, T], BF16)
            self.nvw = sc("nvw", [T, 256], BF16)
            self.ngT = sc("ngT", [24, T], F32)
            self.gfT = sc("gfT", [D, T], BF16)
            self.gnT = sc("gnT", [D, T], BF16)
            self.kcT = sc("kcT", [2, 128, 256], BF16)
            self.vc = sc("vc", [2, 256, 128], BF16)
            self.ofoxT = sc("ofoxT", [1024, T], BF16)
            self.onsaT = sc("onsaT", [1024, T], BF16)
            self.hT = sc("hT", [FF, T], BF16)
            self.consts()
            if self.stop != (0, "T0a"):
                self.precast()
            if self.stop not in ((0, "T0a"), (0, "T0b")):
                self.transpose_in()
            S.barrier()
            cur = 0
            done = False
            for l in range(self.nlayers if self.stop not in ((0, "T0a"), (0, "T0b"), (0, "T0c")) else 0):
                for phase in ("A", "FOXPREP", "CMP", "FOX", "NSA", "C", "D"):
                    A.reset()
                    if self.stop is not None and self.stop[1] in ("A1", "A2", "A3", "A4"):
                        self.phase_A(l, self.xT[cur])
                        S.barrier()
                        done = True
                        break
                    if phase == "A":
                        self.phase_A(l, self.xT[cur])
                    elif phase == "FOXPREP":
                        self.phase_foxprep(l)
                    elif phase == "CMP":
                        self.phase_cmp(l)
                    elif phase == "FOX":
                        self.phase_fox(l)
                    elif phase == "NSA":
                        self.phase_nsa(l)
                    elif phase == "C":
                        nx = (cur + 1) % 3
                        self.phase_C(l, self.xT[cur], self.xT[nx])
                        cur = nx
                    elif phase == "D":
                        nx = (cur + 1) % 3
                        self.phase_D(l, self.xT[cur], self.xT[nx])
                        cur = nx
                    S.barrier()
                    if self.stop == (l, phase):
                        done = True
                        break
                if done:
                    break
            A.reset()
            if self.stop not in ((0, "T0a"), (0, "T0b")):
                self.transpose_out(self.xT[cur])
            S.finish(list(self.wtok.values()) if hasattr(self, 'wtok') else ())
        return nc

    def consts(self):
        S, A = self.S, self.A
        self.C = {}
        for n, ap in self.cin.items():
            shp = list(ap.shape)
            dtp = ap.dtype
            if n in ("c_cos", "c_sin", "c_maskc", "c_keep", "c_add", "c_cosc", "c_sinc", "c_E", "c_sel8", "c_sel24"):
                continue
            v, b = A.alloc(shp[1], dtp, parts=shp[0], name=n)
            S.dma("sp", v, ap[:, :], [], [b], b)
            self.C[n] = (v, b)
        g, gb = A.alloc(L * 4 * 16, F32, name="gains")
        for i, n in enumerate(["norm_mix_pre", "norm_mix_post", "norm_ffn_pre", "norm_ffn_post"]):
            for l in range(self.LW):
                o = (l * 4 + i) * 16
                S.dma("sp", g[:, o:o + 16], self.small[n][l, :].rearrange("(c p) -> p c", p=128), [], [gb], gb,
                      allow_slow_non_contiguous=True)
        self.gains = (g, gb)
        nb, nbb = A.alloc(self.LW, F32, parts=8, name="negb")
        S.dma("sp", nb, self.small["fox_forget_bias"].rearrange("l h -> h l"), [], [nbb], nbb, allow_slow_non_contiguous=True)
        S.op("dve", [nbb], [nbb], lambda e: e.tensor_scalar(out=nb, in0=nb, scalar1=-1.0, scalar2=None, op0=ALU.mult))
        self.negb = (nb, nbb)
        A.freeze()

    def precast(self):
        S = self.S
        self.wtok = {}
        for n, k, m in WNAMES:
            for l in range(self.nlayers):
                b = Buf("wc_" + n, persist=True)
                nparts = 4 if k * m > 4000000 else 1
                rs = k // nparts
                for i in range(nparts):
                    S.dma("pool", self.wb[n][l, i * rs:(i + 1) * rs, :], self.win[n][l, i * rs:(i + 1) * rs, :], [], [b], b)
                self.wtok[(n, l)] = b

    def gain(self, l, i, c):
        g = self.gains[0]
        o = (l * 4 + i) * 16 + c
        return g[:, o:o + 1]

    def transpose_in(self):
        S, A = self.S, self.A
        A.reset()
        identf, bid = self.C["c_identf"]
        xin = [A.alloc(4 * D, F32, name="xin%d" % i) for i in range(2)]
        stg = [A.alloc(16 * 512, F32, name="tstg%d" % i) for i in range(2)]
        xT0 = self.xT[0]
        for blk in range(T // 512):
            xi, xib = xin[blk % 2]
            st, stb = stg[blk % 2]
            xiv = xi.rearrange("p (a d) -> p a d", a=4)
            S.dma("sp", xiv, self.x[blk * 512:(blk + 1) * 512, :].rearrange("(a p) d -> p a d", p=128), [], [xib], xib)
            stv = st.rearrange("p (c t) -> p c t", c=16)
            for c in range(16):
                pb = c % 4
                for a in range(4):
                    S.op("pe", [xib, bid], [self.bps[pb]],
                         lambda e: e.transpose(out=self.ps[pb][:, a * 128:(a + 1) * 128], in_=xiv[:, a, c * 128:(c + 1) * 128], identity=identf),
                         inc=(a == 3))
                eng = "act" if c % 2 == 0 else "dve"
                if eng == "act":
                    S.op("act", [self.bps[pb]], [stb], lambda e: e.activation(out=stv[:, c, :], in_=self.ps[pb][:, :], func=AF.Copy))
                else:
                    S.op("dve", [self.bps[pb]], [stb], lambda e: e.tensor_copy(out=stv[:, c, :], in_=self.ps[pb][:, :]))
            S.dma("sp", xT0[:, blk * 512:(blk + 1) * 512].rearrange("(c p) t -> p c t", p=128), stv, [stb], [], stb)

    def transpose_out(self, xT):
        S, A = self.S, self.A
        identf, bid = self.C["c_identf"]
        xin = [A.alloc(16 * 512, F32, name="oin%d" % i) for i in range(2)]
        stg = [A.alloc(4 * D, F32, name="ostg%d" % i) for i in range(2)]
        for blk in range(T // 512):
            xi, xib = xin[blk % 2]
            st, stb = stg[blk % 2]
            xiv = xi.rearrange("p (c t) -> p c t", c=16)
            S.dma("sp", xiv, xT[:, blk * 512:(blk + 1) * 512].rearrange("(c p) t -> p c t", p=128), [], [xib], xib)
            stv = st.rearrange("p (a d) -> p a d", a=4)
            i = 0
            for a in range(4):
                for cg in range(4):
                    pb = i % 4
                    i += 1
                    for cc in range(4):
                        c = cg * 4 + cc
                        S.op("pe", [xib, bid], [self.bps[pb]],
                             lambda e: e.transpose(out=self.ps[pb][:, cc * 128:(cc + 1) * 128], in_=xiv[:, c, a * 128:(a + 1) * 128], identity=identf),
                             inc=(cc == 3))
                    if i % 2 == 0:
                        S.op("act", [self.bps[pb]], [stb], lambda e: e.activation(out=stv[:, a, cg * 512:(cg + 1) * 512], in_=self.ps[pb][:, :], func=AF.Copy))
                    else:
                        S.op("dve", [self.bps[pb]], [stb], lambda e: e.tensor_copy(out=stv[:, a, cg * 512:(cg + 1) * 512], in_=self.ps[pb][:, :]))
            S.dma("sp", self.out[blk * 512:(blk + 1) * 512, :].rearrange("(a p) d -> p a d", p=128), stv, [stb], [], stb)

    def rstd_bc(self, sq_chunks, sqb, nb, rs, rsb, pbank):
        S = self.S
        ones, ob = self.C["c_onesb"]
        ps, bp = self.ps[pbank], self.bps[pbank]
        for c in range(16):
            S.op("pe", [sqb, ob], [bp], lambda e: e.matmul(ps[:, 0:nb], lhsT=ones, rhs=sq_chunks(c), start=(c == 0), stop=(c == 15)), inc=(c == 15))
        S.op("act", [bp], [rsb], lambda e: e.activation(out=rs, in_=ps[:, 0:nb], func=AF.Ln, scale=1.0 / D, bias=self.epsc))
        S.op("act", [rsb], [rsb], lambda e: e.activation(out=rs, in_=rs, func=AF.Exp, scale=-0.5))

    def norm_pre(self, l, gi, xT, tok0, TB, res, resb):
        S, A = self.S, self.A
        NB = 128
        xb = [A.alloc(16 * NB, F32, name="npx%d" % i) for i in range(2)]
        sq, sqb = A.alloc(16 * NB, BF16, name="npsq")
        rs, rsb = A.alloc(NB, F32, name="nprs")
        sqv = sq.rearrange("p (c t) -> p c t", c=16)
        for bi in range(TB // NB):
            x_, xbb = xb[bi % 2]
            xv = x_.rearrange("p (c t) -> p c t", c=16)
            t0 = tok0 + bi * NB
            S.dma("sp", xv, xT[:, t0:t0 + NB].rearrange("(c p) t -> p c t", p=128), [], [xbb], xbb)
            S.op("act", [xbb], [sqb], lambda e: e.activation(out=sq, in_=x_, func=AF.Square))
            self.rstd_bc(lambda c: sqv[:, c, :], sqb, NB, rs, rsb, 7)
            for c in range(16):
                eng = "dve"
                S.op(eng, [xbb, rsb, self.gains[1]], [resb],
                     lambda e: e.scalar_tensor_tensor(out=res[:, c, bi * NB:(bi + 1) * NB], in0=xv[:, c, :], scalar=self.gain(l, gi, c), in1=rs,
                                                      op0=ALU.mult, op1=ALU.mult))

    def norm_post(self, l, gi, yT, xT, xTn, tok0, TB):
        S, A = self.S, self.A
        NB = 256
        xb = [A.alloc(16 * NB, F32, name="nqx%d" % i) for i in range(2)]
        yb = [A.alloc(16 * NB, F32, name="nqy%d" % i) for i in range(2)]
        sq, sqb = A.alloc(16 * NB, BF16, name="nqsq")
        rs, rsb = A.alloc(NB, F32, name="nqrs")
        sqv = sq.rearrange("p (c t) -> p c t", c=16)
        for bi in range(TB // NB):
            x_, xbb = xb[bi % 2]
            y_, ybb = yb[bi % 2]
            xv = x_.rearrange("p (c t) -> p c t", c=16)
            yv = y_.rearrange("p (c t) -> p c t", c=16)
            t0 = tok0 + bi * NB
            S.dma("sp", xv, xT[:, t0:t0 + NB].rearrange("(c p) t -> p c t", p=128), [], [xbb], xbb)
            S.dma("sp", yv, yT[:, t0:t0 + NB].rearrange("(c p) t -> p c t", p=128), [], [ybb], ybb)
            S.op("act", [ybb], [sqb], lambda e: e.activation(out=sq, in_=y_, func=AF.Square))
            self.rstd_bc(lambda c: sqv[:, c, :], sqb, NB, rs, rsb, 7)
            for c in range(16):
                S.op("dve", [ybb, rsb, self.gains[1]], [ybb],
                     lambda e: e.scalar_tensor_tensor(out=yv[:, c, :], in0=yv[:, c, :], scalar=self.gain(l, gi, c), in1=rs,
                                                      op0=ALU.mult, op1=ALU.mult))
            S.op("dve", [ybb, xbb], [ybb], lambda e: e.tensor_tensor(out=y_, in0=y_, in1=x_, op=ALU.add))
            S.dma("sp", xTn[:, t0:t0 + NB].rearrange("(c p) t -> p c t", p=128), yv, [ybb], [], ybb)

    def load_w(self, name, l, c0, ncols, KC, wt, wtb):
        S = self.S
        src = self.wb[name][l, :, c0:c0 + ncols].rearrange("(kc p) n -> p kc n", p=128)
        S.dma("sp", wt, src, [self.wtok[(name, l)]], [wtb], wtb)

    def gemm_F(self, segs, KC, act, actb, TB, wslots, epi, banks=(0, 1, 2, 3, 4, 5, 6), after=None):
        S = self.S
        nts = TB // 512
        bi = 0
        wi = 0
        for (wn, l, c0, ncols, tag) in segs:
            CBmax = max(128, (8192 // KC) // 128 * 128)
            CBmax = min(CBmax, 512)
            cb0 = 0
            while cb0 < ncols:
                CB = min(CBmax, ncols - cb0)
                wt_full, wtb = wslots[wi % len(wslots)]
                wi += 1
                wt = wt_full[:, 0:KC * CB].rearrange("p (k n) -> p k n", k=KC)
                self.load_w(wn, l, c0 + cb0, CB, KC, wt, wtb)
                for m0 in range(0, CB, 128):
                    M = min(128, CB - m0)
                    chunk = (cb0 + m0) // 128
                    bks = [banks[(bi + i) % len(banks)] for i in range(nts)]
                    bi += nts
                    for kc in range(KC):
                        for ts in range(nts):
                            pb = bks[ts]
                            S.op("pe", [wtb, actb], [self.bps[pb]],
                                 lambda e: e.matmul(self.ps[pb][0:M, :], lhsT=wt[:, kc, m0:m0 + M], rhs=act[:, kc, ts * 512:(ts + 1) * 512],
                                                    start=(kc == 0), stop=(kc == KC - 1)),
                                 inc=(kc == KC - 1 and ts == nts - 1))
                    for ts in range(nts):
                        epi(tag, chunk, M, ts, self.ps[bks[ts]], self.bps[bks[ts]])
                cb0 += CB

    def gemm_T(self, wn, l, c0, ncols, KC, act, actb, TB, wslots, epi, banks=(0, 1, 2, 3)):
        S = self.S
        bi = 0
        wi = 0
        cb0 = 0
        while cb0 < ncols:
            CB = min(512, ncols - cb0)
            wt_full, wtb = wslots[wi % len(wslots)]
            wi += 1
            wt = wt_full[:, 0:KC * CB].rearrange("p (k n) -> p k n", k=KC)
            self.load_w(wn, l, c0 + cb0, CB, KC, wt, wtb)
            for tt in range(TB // 128):
                pb = banks[bi % len(banks)]
                bi += 1
                for kc in range(KC):
                    S.op("pe", [wtb, actb], [self.bps[pb]],
                         lambda e: e.matmul(self.ps[pb][:, 0:CB], lhsT=act[:, kc, tt * 128:(tt + 1) * 128], rhs=wt[:, kc, :],
                                            start=(kc == 0), stop=(kc == KC - 1)),
                         inc=(kc == KC - 1))
                epi(tt, cb0, CB, self.ps[pb], self.bps[pb])
            cb0 += CB

    def phase_A(self, l, xT):
        S, A = self.S, self.A
        TB = 2048
        res_, resb = A.alloc(16 * TB, BF16, name="resA")
        res = res_.rearrange("p (c t) -> p c t", c=16)
        wslots = [A.alloc(8192, BF16, name="wsl%d" % i) for i in range(2)]
        stg = [A.alloc(TB, BF16, name="stgA%d" % i) for i in range(3)]
        stf, stfb = A.alloc(TB, F32, parts=32, name="stgAf")
        cosv, cosb = A.alloc(TB, F32, parts=32, name="cosA")
        sinv, sinb = A.alloc(TB, F32, parts=32, name="sinA")
        tA = [A.alloc(512, F32, parts=32, name="tA%d" % i) for i in range(2)]
        tB = [A.alloc(512, F32, parts=32, name="tB%d" % i) for i in range(2)]
        vst = [A.alloc(512, BF16, name="vst%d" % i) for i in range(3)]
        rrot, rrb = self.C["c_rrot"]
        mark = A.off
        for half in range(T // TB):
            tok0 = half * TB
            A.off = mark
            S.dma("sp", cosv, self.cin["c_cos"][:, tok0:tok0 + TB], [], [cosb], cosb)
            S.dma("sp", sinv, self.cin["c_sin"][:, tok0:tok0 + TB], [], [sinb], sinb)
            self.norm_pre(l, 0, xT, tok0, TB, res, resb)
            sub = self.stop[1] if self.stop is not None else ""
            if sub == "A1":
                if "resA" not in self.scr:
                    self.scratch("resA", [16 * 128, T], BF16)
                S.dma("sp", self.scr["resA"][:, tok0:tok0 + TB].rearrange("(c p) t -> p c t", p=128), res, [resb], [], resb)
                continue
            st_i = [0]
            deferred = []

            def flush():
                while deferred:
                    deferred.pop(0)()

            def epi(tag, chunk, M, ts, ps, bp):
                dest, rope, func, fp32 = tag
                if ts == 0:
                    st_i[0] += 1
                if fp32:
                    sv, sb_ = stf, stfb
                else:
                    sv, sb_ = stg[st_i[0] % 3]
                cols = slice(ts * 512, (ts + 1) * 512)
                if func is not None:
                    S.op("act", [bp], [sb_], lambda e: e.activation(out=sv[0:M, cols], in_=ps[0:M, :], func=func))
                elif rope:
                    S.op("act", [bp], [sb_], lambda e: e.activation(out=sv[0:M, cols], in_=ps[0:M, :], func=AF.Copy))
                    ta, tab = tA[ts % 2]
                    S.op("dve", [bp, cosb], [tab], lambda e: e.tensor_tensor(out=ta, in0=ps[0:32, :], in1=cosv[:, cols], op=ALU.mult))

                    def part2(sv=sv, sb_=sb_, cols=cols, ta=ta, tab=tab, ts=ts):
                        tb_, tbb = tB[ts % 2]
                        S.op("pe", [sb_, rrb], [self.bps[7]], lambda e: e.matmul(self.ps[7][0:32, :], lhsT=rrot, rhs=sv[0:32, cols], start=True, stop=True))
                        S.op("dve", [self.bps[7], sinb], [tbb], lambda e: e.tensor_tensor(out=tb_, in0=self.ps[7][0:32, :], in1=sinv[:, cols], op=ALU.mult))
                        S.op("dve", [tab, tbb], [sb_], lambda e: e.tensor_tensor(out=sv[0:32, cols], in0=ta, in1=tb_, op=ALU.add))
                    part2()
                else:
                    if (chunk + ts) % 2 == 0:
                        S.op("act", [bp], [sb_], lambda e: e.activation(out=sv[0:M, cols], in_=ps[0:M, :], func=AF.Copy))
                    else:
                        S.op("dve", [bp], [sb_], lambda e: e.tensor_copy(out=sv[0:M, cols], in_=ps[0:M, :]))
                if ts == TB // 512 - 1:
                    r0 = chunk * 128
                    S.dma("sp", dest[r0:r0 + M, tok0:tok0 + TB], sv[0:M, :], [sb_], [], sb_)

            segsF = [
                ("w_in", l, O_FQ, 1024, (self.fqT, False, None, False)),
                ("w_in", l, O_FK, 1024, (self.fkT, False, None, False)),
                ("w_in", l, O_FF, 8, (self.ffT, False, None, True)),
                ("w_in", l, O_NQ, 1024, (self.nqT, True, None, False)),
                ("w_in", l, O_NKC, 512, (self.nkcT, False, None, False)),
                ("w_in", l, O_NKS, 256, (self.nksT, True, None, False)),
                ("w_in", l, O_NKW, 256, (self.nkwT, True, None, False)),
                ("w_in", l, O_NG, 24, (self.ngT, False, AF.Sigmoid, True)),
                ("w_in", l, O_GF, 2048, (self.gfT, False, AF.Sigmoid, False)),
                ("w_in", l, O_GN, 2048, (self.gnT, False, AF.Sigmoid, False)),
            ]
            if sub == "A2":
                segsF = segsF[0:1]
            if sub == "A3":
                segsF = segsF[0:3]
            self.gemm_F(segsF, 16, res, resb, TB, wslots, epi)
            flush()
            if sub in ("A2", "A3", "A4"):
                continue
            vi = [0]
            for (c0, ncols, dest) in [(O_FV, 1024, self.fv), (O_NVS, 256, self.nvs), (O_NVW, 256, self.nvw)]:
                def epiT(tt, cb0, CB, ps, bp, dest=dest):
                    sv, sb_ = vst[vi[0] % 3]
                    vi[0] += 1
                    if vi[0] % 2 == 0:
                        S.op("act", [bp], [sb_], lambda e: e.activation(out=sv[:, 0:CB], in_=ps[:, 0:CB], func=AF.Copy))
                    else:
                        S.op("dve", [bp], [sb_], lambda e: e.tensor_copy(out=sv[:, 0:CB], in_=ps[:, 0:CB]))
                    r0 = tok0 + tt * 128
                    S.dma("sp", dest[r0:r0 + 128, cb0:cb0 + CB], sv[:, 0:CB], [sb_], [], sb_)
                self.gemm_T("w_in", l, c0, ncols, 16, res, resb, TB, wslots, epiT)

    def phase_foxprep(self, l):
        S, A = self.S, self.A
        f, fb = A.alloc(T, F32, parts=8, name="ffrow")
        g, gb = A.alloc(T, F32, parts=8, name="ffrow2")
        nb, nbb = self.negb
        dffT = Buf("dffT")
        S.dma("sp", f, self.ffT[:, :], [], [fb], fb)
        S.op("act", [fb, nbb], [fb], lambda e: e.activation(out=f, in_=f, func=AF.Exp, scale=-1.0, bias=nb[:, l:l + 1]))
        S.op("act", [fb], [fb], lambda e: e.activation(out=f, in_=f, func=AF.Ln, scale=1.0, bias=self.onec[0:8, :]))
        S.op("dve", [fb], [gb], lambda e: e.tensor_tensor_scan(out=g, data0=f, data1=f, initial=0.0, op0=ALU.add, op1=ALU.bypass))
        S.dma("sp", self.ffT[:, :], g, [gb], [], gb)

    def phase_cmp(self, l):
        S, A = self.S, self.A
        src, srcb = A.alloc(T, BF16, name="cmpsrc")
        w1_, w1b = A.alloc(32 * 256, BF16, name="cw1")
        w1 = w1_.rearrange("p (k n) -> p k n", k=32)
        w2_, w2b = A.alloc(2 * 128, BF16, name="cw2")
        w2 = w2_.rearrange("p (k n) -> p k n", k=2)
        pe32, pe32b = A.alloc(128, F32, parts=32, name="pe32")
        peT, peTb = A.alloc(32, BF16, name="peT")
        cst, cstb = A.alloc(2, F32, name="ccst")
        hT_, hTb = A.alloc(512, BF16, name="chT")
        hT = hT_.rearrange("p (k n) -> p k n", k=2)
        okb_, okbb = A.alloc(256, BF16, name="cokb")
        ta, tab = A.alloc(256, F32, parts=32, name="cta")
        tb_, tbb = A.alloc(256, F32, parts=32, name="ctb")
        cosc, coscb = A.alloc(256, F32, parts=32, name="cosc")
        sinc, sincb = A.alloc(256, F32, parts=32, name="sinc")
        ovb_, ovbb = A.alloc(256, BF16, name="covb")
        ovb = ovb_.rearrange("p (k n) -> p k n", k=2)
        identf, idfb = self.C["c_identf"]
        rrot, rrb = self.C["c_rrot"]
        S.dma("sp", cosc, self.cin["c_cosc"][:, :], [], [coscb], coscb)
        S.dma("sp", sinc, self.cin["c_sinc"][:, :], [], [sincb], sincb)
        for kind in range(2):
            pre = "cmp_k" if kind == 0 else "cmp_v"
            self.load_w(pre + "_w1", l, 0, 256, 32, w1, w1b)
            self.load_w(pre + "_w2", l, 0, 128, 2, w2, w2b)
            S.dma("sp", pe32, self.small[pre + "_pe"][l, :, :], [], [pe32b], pe32b)
            S.op("pe", [pe32b, idfb], [self.bps[0]], lambda e: e.transpose(out=self.ps[0][:, 0:32], in_=pe32, identity=identf[0:32, 0:32]))
            S.op("dve", [self.bps[0]], [peTb], lambda e: e.tensor_copy(out=peT, in_=self.ps[0][:, 0:32]))
            for hc in range(2):
                for ll in range(32):
                    S.op("pe", [w1b, peTb], [self.bps[1]],
                         lambda e: e.matmul(self.ps[1][:, hc:hc + 1], lhsT=w1[:, ll, hc * 128:(hc + 1) * 128], rhs=peT[:, ll:ll + 1],
                                            start=(ll == 0), stop=(ll == 31)), inc=(ll == 31))
            S.op("dve", [self.bps[1]], [cstb], lambda e: e.tensor_copy(out=cst, in_=self.ps[1][:, 0:2]))
            for g in range(2):
                r0 = kind * 256 + g * 128
                S.dma("sp", src, self.nkcT[r0:r0 + 128, :], [], [srcb], srcb)
                for hc in range(2):
                    pb = 2 + hc
                    for ll in range(32):
                        S.op("pe", [w1b, srcb], [self.bps[pb]],
                             lambda e: e.matmul(self.ps[pb][:, 0:255], lhsT=w1[:, ll, hc * 128:(hc + 1) * 128], rhs=src[:, ll:ll + 16 * 254 + 1:16],
                                                start=(ll == 0), stop=(ll == 31)), inc=(ll == 31))
                    S.op("act", [self.bps[pb], cstb], [hTb],
                         lambda e: e.activation(out=hT[:, hc, 0:255], in_=self.ps[pb][:, 0:255], func=AF.Gelu_apprx_tanh, bias=cst[:, hc:hc + 1]))
                if kind == 0:
                    for hc in range(2):
                        S.op("pe", [w2b, hTb], [self.bps[4]],
                             lambda e: e.matmul(self.ps[4][:, 0:255], lhsT=w2[:, hc, :], rhs=hT[:, hc, 0:255], start=(hc == 0), stop=(hc == 1)), inc=(hc == 1))
                    S.op("dve", [], [okbb], lambda e: e.memset(okb_, 0.0))
                    S.op("act", [self.bps[4]], [okbb], lambda e: e.activation(out=okb_[:, 0:255], in_=self.ps[4][:, 0:255], func=AF.Copy))
                    S.op("dve", [self.bps[4], coscb], [tab], lambda e: e.tensor_tensor(out=ta[:, 0:255], in0=self.ps[4][0:32, 0:255], in1=cosc[:, 0:255], op=ALU.mult))
                    S.op("pe", [okbb, rrb], [self.bps[5]], lambda e: e.matmul(self.ps[5][0:32, 0:255], lhsT=rrot, rhs=okb_[0:32, 0:255], start=True, stop=True))
                    S.op("dve", [self.bps[5], sincb], [tbb], lambda e: e.tensor_tensor(out=tb_[:, 0:255], in0=self.ps[5][0:32, 0:255], in1=sinc[:, 0:255], op=ALU.mult))
                    S.op("dve", [tab, tbb], [okbb], lambda e: e.tensor_tensor(out=okb_[0:32, 0:255], in0=ta[:, 0:255], in1=tb_[:, 0:255], op=ALU.add))
                    S.dma("sp", self.kcT[g, :, :], okb_, [okbb], [], okbb)
                else:
                    S.op("dve", [], [ovbb], lambda e: e.memset(ovb_, 0.0))
                    for nc_ in range(2):
                        n = 128 if nc_ == 0 else 127
                        for hc in range(2):
                            S.op("pe", [w2b, hTb], [self.bps[6]],
                                 lambda e: e.matmul(self.ps[6][0:n, nc_ * 128:(nc_ + 1) * 128], lhsT=hT[:, hc, nc_ * 128:nc_ * 128 + n], rhs=w2[:, hc, :],
                                                    start=(hc == 0), stop=(hc == 1)), inc=(hc == 1))
                        S.op("act", [self.bps[6]], [ovbb], lambda e: e.activation(out=ovb[0:n, nc_, :], in_=self.ps[6][0:n, nc_ * 128:(nc_ + 1) * 128], func=AF.Copy))
                    S.dma("sp", self.vc[g, :, :].rearrange("(k p) d -> p k d", p=128), ovb, [ovbb], [], ovbb)

    def attn_core(self, Q, tiles, qT, qTb, kT, kTb, vv, vb, pO, pR, sbanks, ptiles, extra=None, fox=None, nstart=None, LA=1):
        S = self.S
        ones, ob = self.C["c_onesb"]
        identb, idb = self.C["c_identb"]
        n = len(tiles)
        q0 = Q * 512
        st = nstart if nstart is not None else [0]

        def qk(i):
            kt, qlo, qhi, masks = tiles[i]
            sb_i = sbanks[(st[0] + i) % len(sbanks)]
            ps, bp = self.ps[sb_i], self.bps[sb_i]
            last_plain = (extra is None and not masks)
            need_inc = (i < LA)
            S.op("pe", [kTb, qTb], [bp], lambda e: e.matmul(ps[:, qlo:qhi], lhsT=kT[:, kt * 128:(kt + 1) * 128], rhs=qT[:, q0 + qlo:q0 + qhi],
                                                            start=True, stop=last_plain), inc=(last_plain and need_inc))
            for mi, (slo, map_, mb) in enumerate(masks):
                lastm = (extra is None and mi == len(masks) - 1)
                S.op("pe", [mb, idb], [bp], lambda e: e.matmul(ps[:, slo:slo + 128], lhsT=identb, rhs=map_, start=False, stop=lastm), inc=(lastm and need_inc))
            if extra is not None:
                elhs, eb, erhs, erb = extra
                S.op("pe", [eb, erb], [bp], lambda e: e.matmul(ps[:, qlo:qhi], lhsT=elhs(kt), rhs=erhs[:, q0 + qlo:q0 + qhi], start=False, stop=True), inc=need_inc)

        def pv(i):
            kt, qlo, qhi, masks = tiles[i]
            sb_i = sbanks[(st[0] + i) % len(sbanks)]
            ps, bp = self.ps[sb_i], self.bps[sb_i]
            pt, ptb = ptiles[(st[0] + i) % len(ptiles)]
            if fox is not None:
                cqb, cqbb, negck, nckb, tmps = fox
                tm, tmb = tmps[(st[0] + i) % len(tmps)]
                S.op("dve", [bp, cqbb], [tmb], lambda e: e.scalar_tensor_tensor(out=tm[:, qlo:qhi], in0=ps[:, qlo:qhi], scalar=SCALE, in1=cqb[:, qlo:qhi],
                                                                               op0=ALU.mult, op1=ALU.add))
                S.op("act", [tmb, nckb], [ptb], lambda e: e.activation(out=pt[:, qlo:qhi], in_=tm[:, qlo:qhi], func=AF.Exp, bias=negck(kt), scale=1.0))
            else:
                S.op("act", [bp], [ptb], lambda e: e.activation(out=pt[:, qlo:qhi], in_=ps[:, qlo:qhi], func=AF.Exp, scale=SCALE))
            S.op("pe", [vb, ptb], [self.bps[pO]], lambda e: e.matmul(self.ps[pO][:, qlo:qhi], lhsT=vv[:, kt, :], rhs=pt[:, qlo:qhi], start=(i == 0), stop=(i == n - 1)),
                 inc=False)
            S.op("pe", [ob, ptb], [self.bps[pR]], lambda e: e.matmul(self.ps[pR][:, qlo:qhi], lhsT=ones, rhs=pt[:, qlo:qhi], start=(i == 0), stop=(i == n - 1)),
                 inc=True)

        for i in range(n + LA):
            if i < n:
                qk(i)
            if i >= LA:
                pv(i - LA)
        st[0] += n

    def phase_fox(self, l):
        S, A = self.S, self.A
        Lrow, Lrb = A.alloc(T, F32, parts=8, name="Lrow")
        crow, crb = A.alloc(T, F32, parts=8, name="crow")
        Lcol, Lcb = A.alloc(NT * 8, F32, name="Lcol")
        qs = [A.alloc(T, BF16, name="fq%d" % i) for i in range(2)]
        ks = [A.alloc(T, BF16, name="fk%d" % i) for i in range(2)]
        vs = [A.alloc(T, BF16, name="fv%d" % i) for i in range(2)]
        cqs = [A.alloc(512, F32, name="cqb%d" % i) for i in range(2)]
        tmps = [A.alloc(512, F32, name="ftm%d" % i) for i in range(3)]
        pts = [A.alloc(512, BF16, name="fpt%d" % i) for i in range(6)]
        rinv, rib = A.alloc(512, F32, name="frinv")
        ost = [A.alloc(512, BF16, name="fost%d" % i) for i in range(2)]
        sel8, s8b = A.alloc(1024, F32, parts=8, name="sel8")
        S.dma("sp", sel8, self.cin["c_sel8"][:, :], [], [s8b], s8b)
        identf, idfb = self.C["c_identf"]
        caus, cab = self.C["c_caus"]
        S.dma("sp", Lrow, self.ffT[:, :], [], [Lrb], Lrb)
        S.op("dve", [Lrb], [crb], lambda e: e.tensor_scalar(out=crow, in0=Lrow, scalar1=-1.0, scalar2=None, op0=ALU.mult))
        for kt in range(NT):
            S.op("pe", [Lrb, idfb], [self.bps[7]], lambda e: e.transpose(out=self.ps[7][:, kt * 8:(kt + 1) * 8], in_=Lrow[:, kt * 128:(kt + 1) * 128], identity=identf[0:8, 0:8]),
                 inc=(kt == NT - 1))
        S.op("dve", [self.bps[7]], [Lcb], lambda e: e.tensor_copy(out=Lcol, in_=self.ps[7][:, 0:NT * 8]))
        nst = [0]
        it = 0
        for h in range(8):
            qT, qTb = qs[h % 2]
            kT, kTb = ks[h % 2]
            v_, vb = vs[h % 2]
            vv = v_.rearrange("p (k d) -> p k d", k=NT)
            S.dma("sp", qT, self.fqT[h * 128:(h + 1) * 128, :], [], [qTb], qTb)
            S.dma("sp", kT, self.fkT[h * 128:(h + 1) * 128, :], [], [kTb], kTb)
            S.dma("sp", vv, self.fv[:, h * 128:(h + 1) * 128].rearrange("(k p) d -> p k d", p=128), [], [vb], vb)
            for Q in range(T // 512):
                cqb, cqbb = cqs[it % 2]
                pO, pR = (2, 3)
                it += 1
                S.op("pe", [s8b, crb], [self.bps[6]], lambda e: e.matmul(self.ps[6][:, :], lhsT=sel8[:, h * 128:(h + 1) * 128], rhs=crow[:, Q * 512:(Q + 1) * 512], start=True, stop=True))
                S.op("act", [self.bps[6]], [cqbb], lambda e: e.activation(out=cqb, in_=self.ps[6][:, :], func=AF.Copy))
                tiles = []
                for kt in range(4 * Q + 4):
                    i = kt - 4 * Q
                    if i < 0:
                        tiles.append((kt, 0, 512, []))
                    else:
                        tiles.append((kt, 128 * i, 512, [(128 * i, caus, cab)]))
                self.attn_core(Q, tiles, qT, qTb, kT, kTb, vv, vb, pO, pR, (0, 1, 7, 4, 5), pts,
                               fox=(cqb, cqbb, lambda kt: Lcol[:, kt * 8 + h:kt * 8 + h + 1], Lcb, tmps), nstart=nst, LA=3)
                o_, ob_ = ost[it % 2]
                S.op("dve", [self.bps[pR]], [rib], lambda e: e.reciprocal(out=rinv, in_=self.ps[pR][:, :]))
                S.op("dve", [self.bps[pO], rib], [ob_], lambda e: e.tensor_tensor(out=o_, in0=self.ps[pO][:, :], in1=rinv, op=ALU.mult))
                S.dma("sp", self.ofoxT[h * 128:(h + 1) * 128, Q * 512:(Q + 1) * 512], o_, [ob_], [], ob_)

    def phase_nsa(self, l):
        S, A = self.S, self.A
        ksT, ksb = A.alloc(T, BF16, name="nks")
        kwT, kwb = A.alloc(T, BF16, name="nkw")
        vs_, vsb = A.alloc(T, BF16, name="nvs")
        vw_, vwb = A.alloc(T, BF16, name="nvw")
        vsv = vs_.rearrange("p (k d) -> p k d", k=NT)
        vwv = vw_.rearrange("p (k d) -> p k d", k=NT)
        kc, kcb = A.alloc(256, BF16, name="nkc")
        vc_, vcb = A.alloc(256, BF16, name="nvc")
        vcv = vc_.rearrange("p (k d) -> p k d", k=2)
        qh = [A.alloc(T, BF16, name="nq%d" % i) for i in range(4)]
        oc_, ocb = A.alloc(4 * T, BF16, name="noc")
        ocv = oc_.rearrange("p (h t) -> p h t", h=4)
        nm, nmb = A.alloc(T, BF16, parts=64, name="nnegm")
        gT, gTb = A.alloc(T, F32, parts=24, name="ngT")
        mkc, mkcb = A.alloc(512, F32, name="nmaskc")
        keep, keepb = A.alloc(128, F32, name="nkeep")
        addt, addb = A.alloc(128, F32, name="nadd")
        e_t = [A.alloc(256, F32, name="ne%d" % i) for i in range(2)]
        p_t = [A.alloc(256, F32, name="np%d" % i) for i in range(2)]
        pb_t = [A.alloc(256, BF16, name="npb%d" % i) for i in range(2)]
        pT_t = [A.alloc(256, BF16, name="npT%d" % i) for i in range(2)]
        rs_t = [A.alloc(2, F32, name="nrs%d" % i) for i in range(2)]
        pg, pgb = A.alloc(256, F32, name="npg")
        imp, impb = A.alloc(64, F32, name="nimp")
        v0, v0b = A.alloc(64, F32, name="nv0")
        v1, v1b = A.alloc(64, F32, name="nv1")
        v2, v2b = A.alloc(64, F32, name="nv2")
        mx, mxb = A.alloc(16, F32, name="nmx")
        sl, slb = A.alloc(64, BF16, name="nsl")
        pts = [A.alloc(512, BF16, name="npt%d" % i) for i in range(5)]
        osn = [A.alloc(512, F32, name="nos%d" % i) for i in range(2)]
        rinv, rib = A.alloc(512, F32, name="nrinv")
        acc, accb = A.alloc(512, F32, name="nacc")
        ost = [A.alloc(512, BF16, name="nost%d" % i) for i in range(2)]
        Ec, Eb = A.alloc(T, BF16, parts=64, name="cE")
        sel24, s24b = A.alloc(3072, F32, parts=24, name="sel24")
        S.dma("sp", Ec, self.cin["c_E"][:, :], [], [Eb], Eb)
        S.dma("sp", sel24, self.cin["c_sel24"][:, :], [], [s24b], s24b)
        identb, idb = self.C["c_identb"]
        caus, cab = self.C["c_caus"]
        anti, anb = self.C["c_anti"]
        S.dma("sp", gT, self.ngT[:, :], [], [gTb], gTb)
        S.dma("sp", mkc, self.cin["c_maskc"][:, :], [], [mkcb], mkcb)
        S.dma("sp", keep, self.cin["c_keep"][:, :], [], [keepb], keepb)
        S.dma("sp", addt, self.cin["c_add"][:, :], [], [addb], addb)
        nst = [0]
        it = 0
        for g in range(2):
            S.dma("sp", ksT, self.nksT[g * 128:(g + 1) * 128, :], [], [ksb], ksb)
            S.dma("sp", kwT, self.nkwT[g * 128:(g + 1) * 128, :], [], [kwb], kwb)
            S.dma("sp", vsv, self.nvs[:, g * 128:(g + 1) * 128].rearrange("(k p) d -> p k d", p=128), [], [vsb], vsb)
            S.dma("sp", vwv, self.nvw[:, g * 128:(g + 1) * 128].rearrange("(k p) d -> p k d", p=128), [], [vwb], vwb)
            S.dma("sp", kc, self.kcT[g, :, :], [], [kcb], kcb)
            S.dma("sp", vcv, self.vc[g, :, :].rearrange("(k p) d -> p k d", p=128), [], [vcb], vcb)
            for hh in range(4):
                h = g * 4 + hh
                S.dma("sp", qh[hh][0], self.nqT[h * 128:(h + 1) * 128, :], [], [qh[hh][1]], qh[hh][1])
            k1 = 0
            for qt in range(NT):
                msk = mkc[:, 256 - 8 * qt:512 - 8 * qt]
                for hh in range(4):
                    qT, qTb = qh[hh]
                    e_, eb_ = e_t[k1 % 2]
                    p_, pb_ = p_t[k1 % 2]
                    pbf, pbfb = pb_t[k1 % 2]
                    pT, pTb = pT_t[k1 % 2]
                    rs, rsb = rs_t[k1 % 2]
                    sbk = k1 % 2
                    k1 += 1
                    S.op("pe", [qTb, kcb], [self.bps[sbk]], lambda e: e.matmul(self.ps[sbk][:, 0:256], lhsT=qT[:, qt * 128:(qt + 1) * 128], rhs=kc, start=True, stop=True))
                    S.op("act", [self.bps[sbk]], [eb_], lambda e: e.activation(out=e_, in_=self.ps[sbk][:, 0:256], func=AF.Exp, scale=SCALE))
                    S.op("dve", [], [rsb], lambda e: e.memset(rs, 0.0))
                    S.op("dve", [eb_, mkcb], [eb_, rsb], lambda e: e.scalar_tensor_tensor(out=e_, in0=e_, scalar=1.0, in1=msk, op0=ALU.mult, op1=ALU.mult, accum_out=rs[:, 0:1]))
                    S.op("dve", [rsb], [rsb], lambda e: e.tensor_scalar(out=rs[:, 1:2], in0=rs[:, 0:1], scalar1=1e-30, scalar2=None, op0=ALU.max))
                    S.op("dve", [rsb], [rsb], lambda e: e.reciprocal(out=rs[:, 1:2], in_=rs[:, 1:2]))
                    S.op("dve", [eb_, rsb], [pb_], lambda e: e.tensor_scalar(out=p_, in0=e_, scalar1=rs[:, 1:2], scalar2=None, op0=ALU.mult))
                    S.op("dve", [pb_], [pbfb], lambda e: e.tensor_copy(out=pbf, in_=p_))
                    if hh == 0:
                        S.op("dve", [pb_], [pgb], lambda e: e.tensor_copy(out=pg, in_=p_))
                    else:
                        S.op("dve", [pb_, pgb], [pgb], lambda e: e.tensor_tensor(out=pg, in0=pg, in1=p_, op=ALU.add))
                    psb = self.ps[6][:].bitcast(BF16)
                    for nc_ in range(2):
                        S.op("pe", [pbfb, idb], [self.bps[6]], lambda e: e.transpose(out=psb[:, nc_ * 128:(nc_ + 1) * 128], in_=pbf[:, nc_ * 128:(nc_ + 1) * 128], identity=identb),
                             inc=(nc_ == 1))
                    S.op("act", [self.bps[6]], [pTb], lambda e: e.activation(out=pT, in_=psb[:, 0:256], func=AF.Copy))
                    for nc_ in range(2):
                        S.op("pe", [vcb, pTb], [self.bps[7]], lambda e: e.matmul(self.ps[7][:, 0:128], lhsT=vcv[:, nc_, :], rhs=pT[:, nc_ * 128:(nc_ + 1) * 128],
                                                                                 start=(nc_ == 0), stop=(nc_ == 1)), inc=(nc_ == 1))
                    S.op("dve", [self.bps[7]], [ocb], lambda e: e.tensor_copy(out=ocv[:, hh, qt * 128:(qt + 1) * 128], in_=self.ps[7][:, 0:128]))
                pg4 = pg.rearrange("p (j m) -> p j m", m=4)
                S.op("dve", [pgb], [impb], lambda e: e.tensor_tensor(out=imp, in0=pg4[:, :, 0], in1=pg4[:, :, 1], op=ALU.add))
                S.op("dve", [pgb, impb], [impb], lambda e: e.tensor_tensor(out=imp, in0=imp, in1=pg4[:, :, 2], op=ALU.add))
                S.op("dve", [pgb, impb], [impb], lambda e: e.scalar_tensor_tensor(out=imp, in0=pg4[:, :, 3], scalar=0.5, in1=imp, op0=ALU.mult, op1=ALU.add))
                S.op("dve", [pgb, impb], [impb], lambda e: e.scalar_tensor_tensor(out=imp[:, 1:64], in0=pg4[:, 0:63, 3], scalar=0.5, in1=imp[:, 1:64], op0=ALU.mult, op1=ALU.add))
                kp = keep[:, 64 - 2 * qt:128 - 2 * qt]
                ad = addt[:, 64 - 2 * qt:128 - 2 * qt]
                S.op("dve", [impb, keepb], [v0b], lambda e: e.tensor_tensor(out=v0, in0=imp, in1=kp, op=ALU.mult))
                S.op("dve", [v0b, addb], [v0b], lambda e: e.tensor_tensor(out=v0, in0=v0, in1=ad, op=ALU.add))
                S.op("dve", [v0b], [v0b], lambda e: e.memset(v0[:, 0:1], 1e6))
                S.op("dve", [v0b], [mxb], lambda e: e.max(out=mx[:, 0:8], in_=v0))
                S.op("dve", [v0b, mxb], [v1b], lambda e: e.match_replace(out=v1, in_to_replace=mx[:, 0:8], in_values=v0, imm_value=-2.0))
                S.op("dve", [v1b], [mxb], lambda e: e.max(out=mx[:, 8:16], in_=v1))
                S.op("dve", [v1b, mxb], [v2b], lambda e: e.match_replace(out=v2, in_to_replace=mx[:, 8:16], in_values=v1, imm_value=-2.0))
                S.op("dve", [v0b, v2b], [v2b], lambda e: e.tensor_tensor(out=v2, in0=v2, in1=v0, op=ALU.not_equal))
                S.op("dve", [v0b], [v1b], lambda e: e.tensor_scalar(out=v1, in0=v0, scalar1=0.0, scalar2=None, op0=ALU.is_ge))
                S.op("dve", [v1b, v2b], [v2b], lambda e: e.tensor_tensor(out=v2, in0=v2, in1=v1, op=ALU.mult))
                S.op("dve", [v2b], [slb], lambda e: e.tensor_scalar(out=sl, in0=v2, scalar1=-1.0, scalar2=-NEG, op0=ALU.add, op1=ALU.mult))
                psb5 = self.ps[5][:].bitcast(BF16)
                S.op("pe", [slb, idb], [self.bps[5]], lambda e: e.transpose(out=psb5[0:64, 0:128], in_=sl, identity=identb))
                S.op("act", [self.bps[5]], [nmb], lambda e: e.activation(out=nm[:, qt * 128:(qt + 1) * 128], in_=psb5[0:64, 0:128], func=AF.Copy))
            for hh in range(4):
                h = g * 4 + hh
                qT, qTb = qh[hh]
                for Q in range(T // 512):
                    tiles = []
                    for kt in range(4 * Q + 4):
                        i = kt - 4 * Q
                        if i < 0:
                            tiles.append((kt, 0, 512, []))
                        else:
                            tiles.append((kt, 128 * i, 512, [(128 * i, caus, cab)]))
                    self.attn_core(Q, tiles, qT, qTb, ksT, ksb, vsv, vsb, 2, 3, (0, 1, 7), pts,
                                   extra=(lambda kt: Ec[:, kt * 128:(kt + 1) * 128], Eb, nm, nmb), nstart=nst, LA=2)
                    os_, osb = osn[0]
                    S.op("dve", [self.bps[3]], [rib], lambda e: e.reciprocal(out=rinv, in_=self.ps[3][:, :]))
                    S.op("dve", [self.bps[2], rib], [osb], lambda e: e.tensor_tensor(out=os_, in0=self.ps[2][:, :], in1=rinv, op=ALU.mult))
                    tiles = []
                    for kt in range(max(0, 4 * Q - 4), 4 * Q + 4):
                        d = kt - 4 * Q
                        lo = max(0, d)
                        hi = min(3, d + 4)
                        masks = []
                        if 0 <= d <= 3:
                            masks.append((128 * d, caus, cab))
                        if 0 <= d + 4 <= 3:
                            masks.append((128 * (d + 4), anti, anb))
                        tiles.append((kt, 128 * lo, 128 * (hi + 1), masks))
                    tiles.sort(key=lambda t_: 0 if t_[0] == 4 * Q else 1)
                    self.attn_core(Q, tiles, qT, qTb, kwT, kwb, vwv, vwb, 4, 5, (0, 1, 7), pts, nstart=nst, LA=2)
                    ow_, owb = osn[1]
                    S.op("dve", [self.bps[5]], [rib], lambda e: e.reciprocal(out=rinv, in_=self.ps[5][:, :]))
                    S.op("dve", [self.bps[4], rib], [owb], lambda e: e.tensor_tensor(out=ow_, in0=self.ps[4][:, :], in1=rinv, op=ALU.mult))
                    o_, ob_ = ost[it % 2]
                    it += 1
                    qs_ = slice(Q * 512, (Q + 1) * 512)
                    for br, (src_, srcb_) in enumerate([(ocv[:, hh, qs_], ocb), (os_, osb), (ow_, owb)]):
                        r = h * 3 + br
                        S.op("pe", [s24b, gTb], [self.bps[6]], lambda e: e.matmul(self.ps[6][:, :], lhsT=sel24[:, r * 128:(r + 1) * 128], rhs=gT[:, qs_], start=True, stop=True))
                        if br == 0:
                            S.op("dve", [self.bps[6], srcb_], [accb], lambda e: e.tensor_tensor(out=acc, in0=src_, in1=self.ps[6][:, :], op=ALU.mult))
                        else:
                            S.op("dve", [self.bps[6], srcb_], [srcb_], lambda e: e.tensor_tensor(out=src_, in0=src_, in1=self.ps[6][:, :], op=ALU.mult))
                            if br == 1:
                                S.op("dve", [srcb_, accb], [accb], lambda e: e.tensor_tensor(out=acc, in0=acc, in1=src_, op=ALU.add))
                            else:
                                S.op("dve", [srcb_, accb], [ob_], lambda e: e.tensor_tensor(out=o_, in0=acc, in1=src_, op=ALU.add))
                    S.dma("sp", self.onsaT[h * 128:(h + 1) * 128, qs_], o_, [ob_], [], ob_)

    def phase_C(self, l, xT, xTn):
        S, A = self.S, self.A
        TB = 1024
        of_, ofb = A.alloc(8 * TB, BF16, name="Cof")
        on_, onb = A.alloc(8 * TB, BF16, name="Con")
        mx_, mxb = A.alloc(16 * TB, BF16, name="Cmix")
        ofv = of_.rearrange("p (c t) -> p c t", c=8)
        onv = on_.rearrange("p (c t) -> p c t", c=8)
        mixv = mx_.rearrange("p (c t) -> p c t", c=16)
        wslots = [A.alloc(8192, BF16, name="Cws%d" % i) for i in range(3)]
        gts = [A.alloc(TB, BF16, name="Cg%d" % i) for i in range(4)]
        t1s = [A.alloc(512, F32, name="Ct%d" % i) for i in range(2)]
        yst = [A.alloc(TB, F32, name="Cy%d" % i) for i in range(3)]
        mark = A.off
        for blk in range(T // TB):
            A.off = mark
            tok0 = blk * TB
            S.dma("sp", ofv, self.ofoxT[:, tok0:tok0 + TB].rearrange("(c p) t -> p c t", p=128), [], [ofb], ofb)
            S.dma("sp", onv, self.onsaT[:, tok0:tok0 + TB].rearrange("(c p) t -> p c t", p=128), [], [onb], onb)
            nts = TB // 512
            gi = 0
            for cb in range(4):
                wf_, wfb = wslots[(2 * cb) % 3]
                wn_, wnb = wslots[(2 * cb + 1) % 3]
                wf = wf_[:, 0:8 * 512].rearrange("p (k n) -> p k n", k=8)
                wn = wn_[:, 0:8 * 512].rearrange("p (k n) -> p k n", k=8)
                self.load_w("w_up_fox", l, cb * 512, 512, 8, wf, wfb)
                self.load_w("w_up_nsa", l, cb * 512, 512, 8, wn, wnb)
                for mc in range(4):
                    chunk = cb * 4 + mc
                    gf, gfb = gts[gi % 4]
                    gn, gnb = gts[(gi + 1) % 4]
                    gi += 2
                    S.dma("sp", gf, self.gfT[chunk * 128:(chunk + 1) * 128, tok0:tok0 + TB], [], [gfb], gfb)
                    S.dma("sp", gn, self.gnT[chunk * 128:(chunk + 1) * 128, tok0:tok0 + TB], [], [gnb], gnb)
                    for ts in range(nts):
                        pf = (chunk * nts + ts) % 3 * 2
                        pn = pf + 1
                        cols = slice(ts * 512, (ts + 1) * 512)
                        for kc in range(8):
                            S.op("pe", [wfb, ofb], [self.bps[pf]], lambda e: e.matmul(self.ps[pf][:, :], lhsT=wf[:, kc, mc * 128:(mc + 1) * 128], rhs=ofv[:, kc, cols],
                                                                                      start=(kc == 0), stop=(kc == 7)), inc=False)
                        for kc in range(8):
                            S.op("pe", [wnb, onb], [self.bps[pn]], lambda e: e.matmul(self.ps[pn][:, :], lhsT=wn[:, kc, mc * 128:(mc + 1) * 128], rhs=onv[:, kc, cols],
                                                                                      start=(kc == 0), stop=(kc == 7)), inc=(kc == 7))
                        t1, t1b = t1s[0]
                        t2, t2b = t1s[1]
                        S.op("dve", [self.bps[pf], gfb], [t1b], lambda e: e.tensor_tensor(out=t1, in0=self.ps[pf][:, :], in1=gf[:, cols], op=ALU.mult))
                        S.op("dve", [self.bps[pn], gnb], [t2b], lambda e: e.tensor_tensor(out=t2, in0=self.ps[pn][:, :], in1=gn[:, cols], op=ALU.mult))
                        S.op("dve", [t1b, t2b], [mxb], lambda e: e.tensor_tensor(out=mixv[:, chunk, cols], in0=t1, in1=t2, op=ALU.add))
            if "mixT" in self.taps:
                if "mixT" not in self.scr:
                    self.scratch("mixT", [D, T], BF16)
                S.dma("sp", self.scr["mixT"][:, tok0:tok0 + TB].rearrange("(c p) t -> p c t", p=128), mixv, [mxb], [], mxb)
            yi = [0]

            def epi(tag, chunk, M, ts, ps, bp):
                if ts == 0:
                    yi[0] += 1
                sv, sb_ = yst[yi[0] % 3]
                cols = slice(ts * 512, (ts + 1) * 512)
                if (chunk + ts) % 2 == 0:
                    S.op("act", [bp], [sb_], lambda e: e.activation(out=sv[:, cols], in_=ps[:, :], func=AF.Copy))
                else:
                    S.op("dve", [bp], [sb_], lambda e: e.tensor_copy(out=sv[:, cols], in_=ps[:, :]))
                if ts == nts - 1:
                    S.dma("sp", self.yT[chunk * 128:(chunk + 1) * 128, tok0:tok0 + TB], sv, [sb_], [], sb_)
            self.gemm_F([("w_out", l, 0, D, None)], 16, mixv, mxb, TB, wslots[0:2], epi, banks=(0, 1, 2, 3, 4, 5))
        S.barrier()
        A.off = mark = self.A.base
        self.A.reset()
        self.norm_post(l, 1, self.yT, xT, xTn, 0, T)

    def phase_D(self, l, xT, xTn):
        S, A = self.S, self.A
        TB = 2048
        res_, resb = A.alloc(16 * TB, BF16, name="resD")
        res = res_.rearrange("p (c t) -> p c t", c=16)
        wslots = [A.alloc(8192, BF16, name="Dws%d" % i) for i in range(2)]
        sil = [A.alloc(512, F32, name="Dsil%d" % i) for i in range(2)]
        hst = [A.alloc(TB, BF16, name="Dh%d" % i) for i in range(3)]
        mark = A.off
        nts = TB // 512
        for half in range(T // TB):
            A.off = mark
            tok0 = half * TB
            self.norm_pre(l, 2, xT, tok0, TB, res, resb)
            hi = 0
            for cb in range(FF // 256):
                wt_, wtb = wslots[cb % 2]
                wg = wt_[:, 0:4096].rearrange("p (k n) -> p k n", k=16)
                wu = wt_[:, 4096:8192].rearrange("p (k n) -> p k n", k=16)
                self.load_w("w_ffn_gate", l, cb * 256, 256, 16, wg, wtb)
                self.load_w("w_ffn_up", l, cb * 256, 256, 16, wu, wtb)
                for mc in range(2):
                    chunk = cb * 2 + mc
                    hv, hb = hst[hi % 3]
                    hi += 1
                    for tp in range(nts // 2):
                        bk = [(0, 1, 2, 3), (4, 5, 6, 7)][(chunk * 2 + tp) % 2]
                        for kc in range(16):
                            for j in range(2):
                                ts = tp * 2 + j
                                cols = slice(ts * 512, (ts + 1) * 512)
                                pg_, pu_ = bk[2 * j], bk[2 * j + 1]
                                S.op("pe", [wtb, resb], [self.bps[pg_]], lambda e: e.matmul(self.ps[pg_][:, :], lhsT=wg[:, kc, mc * 128:(mc + 1) * 128], rhs=res[:, kc, cols],
                                                                                           start=(kc == 0), stop=(kc == 15)), inc=False)
                                S.op("pe", [wtb, resb], [self.bps[pu_]], lambda e: e.matmul(self.ps[pu_][:, :], lhsT=wu[:, kc, mc * 128:(mc + 1) * 128], rhs=res[:, kc, cols],
                                                                                           start=(kc == 0), stop=(kc == 15)), inc=(kc == 15 and j == 1))
                        for j in range(2):
                            ts = tp * 2 + j
                            cols = slice(ts * 512, (ts + 1) * 512)
                            pg_, pu_ = bk[2 * j], bk[2 * j + 1]
                            sv, sb_ = sil[j]
                            S.op("act", [self.bps[pg_]], [sb_], lambda e: e.activation(out=sv, in_=self.ps[pg_][:, :], func=AF.Silu))
                            S.op("dve", [self.bps[pu_], sb_], [hb], lambda e: e.tensor_tensor(out=hv[:, cols], in0=sv, in1=self.ps[pu_][:, :], op=ALU.mult))
                    S.dma("sp", self.hT[chunk * 128:(chunk + 1) * 128, tok0:tok0 + TB], hv, [hb], [], hb)
        S.barrier()
        A.reset()
        KC = FF // 128
        TB3 = 1024
        hb_, hbb = A.alloc(KC * TB3, BF16, name="Dhb")
        hbv = hb_.rearrange("p (c t) -> p c t", c=KC)
        wsl = [A.alloc(KC * 256, BF16, name="Dwd%d" % i) for i in range(2)]
        yst = [A.alloc(TB3, F32, name="Dy%d" % i) for i in range(3)]
        yi = 0
        gi = 0
        for blk in range(T // TB3):
            tok0 = blk * TB3
            for q4 in range(4):
                S.dma("sp", hbv[:, q4 * 11:(q4 + 1) * 11, :], self.hT[q4 * 11 * 128:(q4 + 1) * 11 * 128, tok0:tok0 + TB3].rearrange("(c p) t -> p c t", p=128), [], [hbb], hbb)
            for cb in range(8):
                wt_, wtb = wsl[cb % 2]
                wt = wt_.rearrange("p (k n) -> p k n", k=KC)
                self.load_w("w_ffn_down", l, cb * 256, 256, KC, wt, wtb)
                for mc in range(2):
                    chunk = cb * 2 + mc
                    bks = [(gi % 3) * 2, (gi % 3) * 2 + 1]
                    gi += 1
                    for kc in range(KC):
                        for ts in range(2):
                            pb = bks[ts]
                            S.op("pe", [wtb, hbb], [self.bps[pb]], lambda e: e.matmul(self.ps[pb][:, :], lhsT=wt[:, kc, mc * 128:(mc + 1) * 128], rhs=hbv[:, kc, ts * 512:(ts + 1) * 512],
                                                                                      start=(kc == 0), stop=(kc == KC - 1)),
                                 inc=(kc == KC - 1 and ts == 1))
                    sv, sb_ = yst[yi % 3]
                    yi += 1
                    for ts in range(2):
                        pb = bks[ts]
                        if ts == 0:
                            S.op("act", [self.bps[pb]], [sb_], lambda e: e.activation(out=sv[:, 0:512], in_=self.ps[pb][:, :], func=AF.Copy))
                        else:
                            S.op("dve", [self.bps[pb]], [sb_], lambda e: e.tensor_copy(out=sv[:, 512:1024], in_=self.ps[pb][:, :]))
                    S.dma("sp", self.yT[chunk * 128:(chunk + 1) * 128, tok0:tok0 + TB3], sv, [sb_], [], sb_)
        S.barrier()
        A.reset()
        self.norm_post(l, 3, self.yT, xT, xTn, 0, T)


def make_consts_tiles(kb):
    S, A = kb.S, kb.A
    e, eb = A.alloc(1, F32, name="epsc")
    o, ob = A.alloc(1, F32, name="onec")
    return e, o


def build_program(nlayers=L, stop=None, taps=(), LW=L):
    kb = KB(nlayers, stop, taps, LW)
    orig_consts = kb.consts

    def consts2():
        S, A = kb.S, kb.A
        e, eb = A.alloc(1, F32, name="epsc")
        o, ob = A.alloc(1, F32, name="onec")
        S.op("dve", [], [eb], lambda en: en.memset(e, EPS))
        S.op("dve", [], [ob], lambda en: en.memset(o, 1.0))
        kb.epsc = e
        kb.onec = o
        orig_consts()
    kb.consts = consts2
    nc = kb.build()
    return kb, nc


_CACHE = {}


def kernel(**inputs):
    x = np.ascontiguousarray(inputs["x"], dtype=np.float32)
    if "prog" not in _CACHE:
        _CACHE["prog"] = build_program()
    kb, nc = _CACHE["prog"]
    cst = host_consts()
    base = {}
    for n, _, _ in WNAMES:
        base[n] = np.ascontiguousarray(inputs[n], dtype=np.float32)
    name_map = {"norm_mix_pre": "norm_mix_pre", "norm_mix_post": "norm_mix_post", "norm_ffn_pre": "norm_ffn_pre",
                "norm_ffn_post": "norm_ffn_post", "fox_forget_bias": "fox_forget_bias", "cmp_k_pe": "cmp_k_pe", "cmp_v_pe": "cmp_v_pe"}
    for n in name_map:
        base[n] = np.ascontiguousarray(inputs[n], dtype=np.float32)
    base.update(cst)
    NCORES = 4
    in_maps = []
    for c in range(NCORES):
        m = dict(base)
        m["x"] = x[c % 4]
        in_maps.append(m)
    res = run_bass_kernel_spmd(nc, in_maps, core_ids=list(range(NCORES)))
    out = np.stack([np.asarray(res.results[c]["out"], dtype=np.float32) for c in range(4)], axis=0)
    return out
```
